# Optimizing a Trainium2 kernel written in Bass

```python
import math
import jax, jax.numpy as jnp
from jax import lax
import numpy as np

D_MODEL = 1024
BATCH = 8
SEQ = 2048
DEPTH = 1
DEC_BATCH = 128
DEC_SEQ = 1
PAST_LEN = 16384
PAGE_SIZE = 128

SSD_EXPAND = 2
SSD_INNER = SSD_EXPAND * D_MODEL
SSD_HEADDIM = 64
SSD_HEADS = SSD_INNER // SSD_HEADDIM
SSD_GROUPS = 4
SSD_STATE = 128
SSD_CONV = 4
SSD_CHUNK = 128
SSD_XBC = SSD_INNER + 2 * SSD_GROUPS * SSD_STATE
CONF_DIM = D_MODEL
CONF_WIDTH = 31
MEM_TOKENS = 256
MEM_HEADS = 4
MEM_HEADDIM = 256
MEM_DIM = MEM_HEADS * MEM_HEADDIM
N_BRANCHES = 3
PEER_HEADS = 8
PEER_NKEYS = 128
PEER_EXPERTS = PEER_NKEYS * PEER_NKEYS
PEER_DKEY = 256
PEER_DHALF = PEER_DKEY // 2
PEER_TOPK = 16
PEER_BLOCK = 128
ALPHA = (2.0 * DEPTH) ** 0.25
BETA = (8.0 * DEPTH) ** -0.25
LN_EPS = 1e-5
S_Z = SSD_INNER
S_XBC = S_Z + SSD_XBC
S_DT = S_XBC + SSD_HEADS
S_CONF = S_DT + 2 * CONF_DIM
S_MEMQ = S_CONF + MEM_DIM
D_IN = S_MEMQ + N_BRANCHES * D_MODEL

kernel_name = 'hybrid_ssd_conformer_peer_decoder_step'


def _layer_norm(x, g, b):
    xf = x.astype(jnp.float32)
    mu = jnp.mean(xf, -1, keepdims=True)
    var = jnp.mean(jnp.square(xf - mu), -1, keepdims=True)
    return ((xf - mu) * lax.rsqrt(var + LN_EPS)).astype(x.dtype) * g + b


def _gated_rms_norm(y, z, w):
    h = (y * jax.nn.silu(z)).astype(jnp.float32)
    h = h * lax.rsqrt(jnp.mean(jnp.square(h), -1, keepdims=True) + LN_EPS)
    return h.astype(y.dtype) * w


def _causal_dwconv(u, buf, w, b):
    full = jnp.concatenate([buf.astype(u.dtype), u], axis=1)
    out = lax.conv_general_dilated(full, w.astype(u.dtype)[:, None, :], window_strides=(1,), padding='VALID',
                                   dimension_numbers=('NWC', 'WIO', 'NWC'), feature_group_count=u.shape[-1])
    return out + b, full[:, full.shape[1] - (w.shape[0] - 1):]


def _ssd_scan(x, dt, a, b_mat, c_mat, d_skip, h0):
    bsz, l, nh, p = x.shape
    g, n = b_mat.shape[2], b_mat.shape[3]
    r = nh // g
    q = min(SSD_CHUNK, l)
    nc = -(-l // q)
    pad = nc * q - l
    x_in = x
    if pad:
        padf = lambda t: jnp.pad(t, [(0, 0), (0, pad)] + [(0, 0)] * (t.ndim - 2))
        x, dt, b_mat, c_mat = padf(x), padf(dt), padf(b_mat), padf(c_mat)
    xc = x.reshape(bsz, nc, q, g, r, p)
    dtc = dt.reshape(bsz, nc, q, g, r)
    bc = b_mat.reshape(bsz, nc, q, g, n)
    cc = c_mat.reshape(bsz, nc, q, g, n)
    da = jnp.moveaxis(dtc.astype(jnp.float32) * a.astype(jnp.float32).reshape(g, r), 2, -1)
    a_cs = jnp.cumsum(da, axis=-1)
    xdt = xc * dtc[..., None]
    seg = a_cs[..., :, None] - a_cs[..., None, :]
    causal = jnp.tril(jnp.ones((q, q), dtype=bool))
    decay_ls = jnp.where(causal, jnp.exp(jnp.where(causal, seg, 0.0)), 0.0).astype(x.dtype)
    cb = jnp.einsum('bclgn,bcsgn->bcgls', cc, bc)
    y_diag = jnp.einsum('bcgls,bcgrls,bcsgrp->bclgrp', cb, decay_ls, xdt)
    decay_to_end = jnp.exp(a_cs[..., -1:] - a_cs).astype(x.dtype)
    states = jnp.einsum('bclgn,bcgrl,bclgrp->bcgrpn', bc, decay_to_end, xdt)
    chunk_decay = jnp.exp(a_cs[..., -1]).astype(states.dtype)

    def step(h, inp):
        st, dec = inp
        return h * dec[..., None, None] + st, h

    h_init = h0.reshape(bsz, g, r, p, n).astype(states.dtype)
    h_fin, h_prev = lax.scan(step, h_init, (jnp.moveaxis(states, 1, 0), jnp.moveaxis(chunk_decay, 1, 0)))
    h_prev = jnp.moveaxis(h_prev, 0, 1)
    y_off = jnp.einsum('bclgn,bcgrpn,bcgrl->bclgrp', cc, h_prev, jnp.exp(a_cs).astype(x.dtype))
    y = (y_diag + y_off).reshape(bsz, nc * q, nh, p)[:, :l] + x_in * d_skip[:, None]
    return y, h_fin.reshape(bsz, nh, p, n)


def _peer_block(t, w_q, sub_keys, u_tab, v_tab):
    nt = t.shape[0]
    qv = (t @ w_q).reshape(nt, PEER_HEADS, 2, PEER_DHALF)
    s = jnp.einsum('thkd,hknd->thkn', qv, sub_keys)
    s_top, i_top = lax.top_k(s, PEER_TOPK)
    cand = s_top[:, :, 0, :, None] + s_top[:, :, 1, None, :]
    cand_id = i_top[:, :, 0, :, None] * PEER_NKEYS + i_top[:, :, 1, None, :]
    best, pos = lax.top_k(cand.reshape(nt, PEER_HEADS, PEER_TOPK * PEER_TOPK), PEER_TOPK)
    idx = jnp.take_along_axis(cand_id.reshape(nt, PEER_HEADS, PEER_TOPK * PEER_TOPK), pos, axis=-1)
    gate = jax.nn.softmax(best.astype(jnp.float32), axis=-1).astype(t.dtype)
    u = jnp.take(u_tab, idx, axis=0)
    act = jax.nn.gelu(jnp.einsum('td,thkd->thk', t, u), approximate=False)
    v = jnp.take(v_tab, idx, axis=0)
    return jnp.einsum('thk,thkd->td', gate * act, v)


def _peer_ffn(x, w_q, sub_keys, u_tab, v_tab):
    shp = x.shape
    t = x.reshape(-1, shp[-1])
    n = t.shape[0]
    nb = -(-n // PEER_BLOCK)
    t = jnp.pad(t, ((0, nb * PEER_BLOCK - n), (0, 0))).reshape(nb, PEER_BLOCK, shp[-1])
    out = lax.map(lambda blk: _peer_block(blk, w_q, sub_keys, u_tab, v_tab), t)
    return out.reshape(nb * PEER_BLOCK, shp[-1])[:n].reshape(shp)


def _hybrid_layer(x, ssm_state, ssd_buf, conf_buf, mem_k, mem_v,
                  w_in, ssd_conv_w, ssd_conv_b, ssd_dt_bias, ssd_a_log, ssd_d, ssd_norm_w, ssd_w_out,
                  conf_conv_w, conf_conv_b, conf_ln_g, conf_ln_b, conf_w_out, mem_w_o, w_out, ln1_g, ln1_b,
                  peer_w_q, peer_sub_keys, peer_u, peer_v, ln2_g, ln2_b):
    b, l = x.shape[0], x.shape[1]
    proj = x @ w_in
    z, xbc, dt_raw, conf_in, q_mem, gates = jnp.split(proj, [S_Z, S_XBC, S_DT, S_CONF, S_MEMQ], axis=-1)
    xbc_c, new_ssd_buf = _causal_dwconv(xbc, ssd_buf, ssd_conv_w, ssd_conv_b)
    xbc_c = jax.nn.silu(xbc_c)
    xs, bm, cm = jnp.split(xbc_c, [SSD_INNER, SSD_INNER + SSD_GROUPS * SSD_STATE], axis=-1)
    dt = jax.nn.softplus(dt_raw + ssd_dt_bias)
    a = -jnp.exp(ssd_a_log)
    y_ssd, new_ssm = _ssd_scan(xs.reshape(b, l, SSD_HEADS, SSD_HEADDIM), dt, a,
                               bm.reshape(b, l, SSD_GROUPS, SSD_STATE), cm.reshape(b, l, SSD_GROUPS, SSD_STATE),
                               ssd_d, ssm_state)
    branch_a = _gated_rms_norm(y_ssd.reshape(b, l, SSD_INNER), z, ssd_norm_w) @ ssd_w_out
    glu = conf_in[..., :CONF_DIM] * jax.nn.sigmoid(conf_in[..., CONF_DIM:])
    cv, new_conf_buf = _causal_dwconv(glu, conf_buf, conf_conv_w, conf_conv_b)
    branch_b = jax.nn.silu(_layer_norm(cv, conf_ln_g, conf_ln_b)) @ conf_w_out
    qh = q_mem.reshape(b, l, MEM_HEADS, MEM_HEADDIM)
    sc = jnp.einsum('blhd,bmhd->bhlm', qh, mem_k).astype(jnp.float32) * (MEM_HEADDIM ** -0.5)
    pr = jax.nn.softmax(sc, axis=-1).astype(x.dtype)
    o = jnp.einsum('bhlm,bmhd->blhd', pr, mem_v).reshape(b, l, MEM_DIM)
    branch_c = o @ mem_w_o
    g = jax.nn.sigmoid(gates).reshape(b, l, N_BRANCHES, D_MODEL)
    merged = g[:, :, 0] * branch_a + g[:, :, 1] * branch_b + g[:, :, 2] * branch_c
    x = _layer_norm(ALPHA * x + merged @ w_out, ln1_g, ln1_b)
    x = _layer_norm(ALPHA * x + _peer_ffn(x, peer_w_q, peer_sub_keys, peer_u, peer_v), ln2_g, ln2_b)
    return x, new_ssm, new_ssd_buf, new_conf_buf


def setup_inputs(seed: int = 0) -> dict:
    key = jax.random.key(seed)
    ks = jax.random.split(key, 40)
    f32 = jnp.float32
    nrm = lambda k, shape, s: jax.random.normal(k, shape, f32) * s
    dt0 = jnp.exp(jax.random.uniform(ks[11], (DEPTH, SSD_HEADS), f32, math.log(1e-3), math.log(1e-1)))
    return {
        'x_prompt': nrm(ks[0], (BATCH, SEQ, D_MODEL), 1.0),
        'x_sample': nrm(ks[1], (DEC_BATCH, DEC_SEQ, D_MODEL), 1.0),
        'state_ssd': nrm(ks[2], (DEPTH, DEC_BATCH, SSD_HEADS, SSD_HEADDIM, SSD_STATE), 0.1),
        'state_ssd_conv': nrm(ks[3], (DEPTH, DEC_BATCH, SSD_CONV - 1, SSD_XBC), 1.0),
        'state_conf_conv': nrm(ks[4], (DEPTH, DEC_BATCH, CONF_WIDTH - 1, CONF_DIM), 0.5),
        'cache_mem_k': nrm(ks[5], (DEPTH, DEC_BATCH, MEM_TOKENS, MEM_HEADS, MEM_HEADDIM), 1.0),
        'cache_mem_v': nrm(ks[6], (DEPTH, DEC_BATCH, MEM_TOKENS, MEM_HEADS, MEM_HEADDIM), BETA),
        'mem_prompt': nrm(ks[7], (BATCH, MEM_TOKENS, D_MODEL), 1.0),
        'w_in': nrm(ks[8], (DEPTH, D_MODEL, D_IN), D_MODEL ** -0.5),
        'ssd_conv_w': nrm(ks[9], (DEPTH, SSD_CONV, SSD_XBC), SSD_CONV ** -0.5),
        'ssd_conv_b': nrm(ks[10], (DEPTH, SSD_XBC), 0.01),
        'ssd_dt_bias': dt0 + jnp.log(-jnp.expm1(-dt0)),
        'ssd_a_log': jnp.log(jax.random.uniform(ks[12], (DEPTH, SSD_HEADS), f32, 1.0, 16.0)),
        'ssd_d': 1.0 + nrm(ks[13], (DEPTH, SSD_HEADS), 0.1),
        'ssd_norm_w': 1.0 + nrm(ks[14], (DEPTH, SSD_INNER), 0.02),
        'ssd_w_out': nrm(ks[15], (DEPTH, SSD_INNER, D_MODEL), BETA * SSD_INNER ** -0.5),
        'conf_conv_w': nrm(ks[16], (DEPTH, CONF_WIDTH, CONF_DIM), CONF_WIDTH ** -0.5),
        'conf_conv_b': nrm(ks[17], (DEPTH, CONF_DIM), 0.01),
        'conf_ln_g': 1.0 + nrm(ks[18], (DEPTH, CONF_DIM), 0.02),
        'conf_ln_b': nrm(ks[19], (DEPTH, CONF_DIM), 0.01),
        'conf_w_out': nrm(ks[20], (DEPTH, CONF_DIM, D_MODEL), BETA * CONF_DIM ** -0.5),
        'mem_w_k': nrm(ks[21], (DEPTH, D_MODEL, MEM_DIM), D_MODEL ** -0.5),
        'mem_w_v': nrm(ks[22], (DEPTH, D_MODEL, MEM_DIM), BETA * D_MODEL ** -0.5),
        'mem_w_o': nrm(ks[23], (DEPTH, MEM_DIM, D_MODEL), BETA * MEM_DIM ** -0.5),
        'w_out': nrm(ks[24], (DEPTH, D_MODEL, D_MODEL), BETA * D_MODEL ** -0.5),
        'ln1_g': 1.0 + nrm(ks[25], (DEPTH, D_MODEL), 0.02),
        'ln1_b': nrm(ks[26], (DEPTH, D_MODEL), 0.01),
        'peer_w_q': nrm(ks[27], (DEPTH, D_MODEL, PEER_HEADS * PEER_DKEY), D_MODEL ** -0.5),
        'peer_sub_keys': nrm(ks[28], (DEPTH, PEER_HEADS, 2, PEER_NKEYS, PEER_DHALF), PEER_DHALF ** -0.5),
        'peer_u': nrm(ks[29], (DEPTH, PEER_EXPERTS, D_MODEL), D_MODEL ** -0.5),
        'peer_v': nrm(ks[30], (DEPTH, PEER_EXPERTS, D_MODEL), BETA * PEER_HEADS ** -0.5),
        'ln2_g': 1.0 + nrm(ks[31], (DEPTH, D_MODEL), 0.02),
        'ln2_b': nrm(ks[32], (DEPTH, D_MODEL), 0.01),
    }


def reference(x_prompt, x_sample, state_ssd, state_ssd_conv, state_conf_conv, cache_mem_k, cache_mem_v, mem_prompt,
              w_in, ssd_conv_w, ssd_conv_b, ssd_dt_bias, ssd_a_log, ssd_d, ssd_norm_w, ssd_w_out,
              conf_conv_w, conf_conv_b, conf_ln_g, conf_ln_b, conf_w_out, mem_w_k, mem_w_v, mem_w_o, w_out,
              ln1_g, ln1_b, peer_w_q, peer_sub_keys, peer_u, peer_v, ln2_g, ln2_b):
    bp = x_prompt.shape[0]
    yp, ys = x_prompt, x_sample
    p_ssm, p_sbuf, p_cbuf, p_mk, p_mv = [], [], [], [], []
    s_ssm, s_sbuf, s_cbuf = [], [], []
    for l in range(DEPTH):
        lw = (w_in[l], ssd_conv_w[l], ssd_conv_b[l], ssd_dt_bias[l], ssd_a_log[l], ssd_d[l], ssd_norm_w[l],
              ssd_w_out[l], conf_conv_w[l], conf_conv_b[l], conf_ln_g[l], conf_ln_b[l], conf_w_out[l],
              mem_w_o[l], w_out[l], ln1_g[l], ln1_b[l], peer_w_q[l], peer_sub_keys[l], peer_u[l], peer_v[l],
              ln2_g[l], ln2_b[l])
        mk = (mem_prompt @ mem_w_k[l]).reshape(bp, MEM_TOKENS, MEM_HEADS, MEM_HEADDIM)
        mv = (mem_prompt @ mem_w_v[l]).reshape(bp, MEM_TOKENS, MEM_HEADS, MEM_HEADDIM)
        h0 = jnp.zeros((bp, SSD_HEADS, SSD_HEADDIM, SSD_STATE), yp.dtype)
        sb0 = jnp.zeros((bp, SSD_CONV - 1, SSD_XBC), yp.dtype)
        cb0 = jnp.zeros((bp, CONF_WIDTH - 1, CONF_DIM), yp.dtype)
        yp, h_p, sb_p, cb_p = _hybrid_layer(yp, h0, sb0, cb0, mk, mv, *lw)
        p_ssm.append(h_p); p_sbuf.append(sb_p); p_cbuf.append(cb_p); p_mk.append(mk); p_mv.append(mv)
        ys, h_s, sb_s, cb_s = _hybrid_layer(ys, state_ssd[l], state_ssd_conv[l], state_conf_conv[l],
                                            cache_mem_k[l], cache_mem_v[l], *lw)
        s_ssm.append(h_s); s_sbuf.append(sb_s); s_cbuf.append(cb_s)
    new_ssd_p = jnp.stack(p_ssm)
    new_ssd_conv_p = jnp.stack(p_sbuf)
    new_conf_conv_p = jnp.stack(p_cbuf)
    new_mem_k_p = jnp.stack(p_mk)
    new_mem_v_p = jnp.stack(p_mv)
    new_ssd_s = jnp.stack(s_ssm)
    new_ssd_conv_s = jnp.stack(s_sbuf)
    new_conf_conv_s = jnp.stack(s_cbuf)
    return (yp, ys, new_ssd_p, new_ssd_conv_p, new_conf_conv_p, new_mem_k_p, new_mem_v_p,
            new_ssd_s, new_ssd_conv_s, new_conf_conv_s)
```

```python
import contextlib
import numpy as np
import concourse.bass as bass
import concourse.mybir as mybir
from concourse.bass_utils import run_bass_kernel_spmd

F32 = mybir.dt.float32
BF16 = mybir.dt.bfloat16
I32 = mybir.dt.int32
U32 = mybir.dt.uint32
AF = mybir.ActivationFunctionType
ALU = mybir.AluOpType
AX = mybir.AxisListType

D = 1024
T = 2048
NS = 16
ALPHA = 2.0 ** 0.25
EPS = 1e-5
S_Z, S_XBC, S_DT, S_CONF, S_MEMQ, D_IN = 2048, 5120, 5152, 7200, 8224, 11296


class Sched:
    ENGS = ("pe", "act", "dve", "pool", "sp")

    def __init__(self, nc, n_dma_slots=48, same_engine_sync=True):
        import os
        same_engine_sync = os.environ.get('SAMESYNC', '1') == '1'
        self.nc = nc
        self.q = {e: [] for e in self.ENGS}
        self.cnt = {e: 0 for e in self.ENGS}
        self.waited = {}
        self.same = same_engine_sync
        self.last_write = {}
        self.readers = {}
        self.n_slots = n_dma_slots
        self.slot_uses = [0] * n_dma_slots
        self.slot_rr = 0
        self.sw_rr = 0
        self.n_hw = n_dma_slots - 16

    def _deps(self, reads, writes):
        deps = []
        for b in reads:
            t = self.last_write.get(b)
            if t is not None:
                deps.append(t)
        for b in writes:
            t = self.last_write.get(b)
            if t is not None:
                deps.append(t)
            deps.extend(self.readers.get(b, ()))
        return deps

    def _commit(self, tok, reads, writes):
        for b in reads:
            self.readers.setdefault(b, []).append(tok)
        for b in writes:
            self.last_write[b] = tok
            self.readers[b] = []

    def _emit_waits(self, e, deps):
        need = {}
        for (kind, key, n) in deps:
            if kind == "eng" and key == e and (not self.same or e in ("pe", "sp")):
                continue
            k = (kind, key)
            if n > need.get(k, 0):
                need[k] = n
        for k, n in need.items():
            if self.waited.get((e, k), 0) >= n:
                continue
            self.waited[(e, k)] = n
            self.q[e].append(("wait", k, n))

    alias = {}

    def _x(self, keys):
        out = []
        for k in keys:
            out.extend(self.alias.get(k, [k]))
        return out

    frozen = False

    def op(self, e, fn, reads=(), writes=()):
        if self.frozen:
            return None
        reads, writes = self._x(reads), self._x(writes)
        deps = self._deps(reads, writes)
        self._emit_waits(e, deps)
        self.cnt[e] += 1
        tok = ("eng", e, self.cnt[e])
        self.q[e].append(("op", fn, None))
        self._commit(tok, reads, writes)
        return tok

    def dma(self, fn, reads=(), writes=(), e="sp"):
        if self.frozen:
            return None
        reads, writes = self._x(reads), self._x(writes)
        deps = self._deps(reads, writes)
        if e == "pool":
            s = self.n_hw + self.sw_rr
            self.sw_rr = (self.sw_rr + 1) % (self.n_slots - self.n_hw)
        else:
            s = self.slot_rr
            self.slot_rr = (self.slot_rr + 1) % self.n_hw
        if self.slot_uses[s] > 0:
            deps.append(("dma", s, self.slot_uses[s]))
        self._emit_waits(e, deps)
        self.slot_uses[s] += 1
        tok = ("dma", s, self.slot_uses[s])
        self.q[e].append(("dma", fn, s))
        self._commit(tok, reads, writes)
        return tok

    def emit(self):
        nc = self.nc
        deps = [("dma", s, u) for s, u in enumerate(self.slot_uses) if u > 0]
        self._emit_waits("sp", deps)
        with contextlib.ExitStack() as st:
            esem = {e: st.enter_context(nc.semaphore("s_" + e)) for e in self.ENGS}
            dsem = [st.enter_context(nc.semaphore("d_%d" % i)) for i in range(self.n_slots)]
            block = st.enter_context(nc.Block())

            def run(e, eng):
                for (kind, a, b) in self.q[e]:
                    if kind == "wait":
                        if a[0] == "eng":
                            eng.wait_ge(esem[a[1]], b)
                        else:
                            eng.wait_ge(dsem[a[1]], 16 * b)
                    elif kind == "op":
                        a(eng).then_inc(esem[e], 1)
                    else:
                        a(eng).then_inc(dsem[b], 16)

            @block.tensor
            def _(eng):
                run("pe", eng)

            @block.scalar
            def _(eng):
                run("act", eng)

            @block.vector
            def _(eng):
                run("dve", eng)

            @block.gpsimd
            def _(eng):
                run("pool", eng)

            @block.sync
            def _(eng):
                run("sp", eng)


def build(stop="all", dbg=()):
    nc = bass.Bass("TRN2", target_bir_lowering=False)
    S = Sched(nc)
    st = contextlib.ExitStack()
    dr = {}

    def din(name, shape, dt=F32):
        dr[name] = nc.dram_tensor(name, list(shape), dt, kind="ExternalInput").ap()
        return dr[name]

    def dout(name, shape, dt=F32):
        dr[name] = nc.dram_tensor(name, list(shape), dt, kind="ExternalOutput").ap()
        return dr[name]

    def dscr(name, shape, dt=F32):
        kind = "ExternalOutput" if name.startswith("dbg_") else "Internal"
        dr[name] = nc.dram_tensor(name, list(shape), dt, kind=kind).ap()
        return dr[name]

    def sb(name, shape, dt=F32):
        return st.enter_context(nc.sbuf_tensor(name, list(shape), dt))

    xp = din("xp", [T, D]); xpT = din("xpT", [D, T])
    xs = din("xs", [NS, D]); xsT = din("xsT", [D, NS])
    w_in = din("w_in", [D, D_IN])
    scw = din("scw", [128, 24, 4]); scb = din("scb", [128, 24])
    dtb = din("dtb", [1, 32]); alog = din("alog", [1, 32]); dsk = din("dsk", [1, 32])
    nw = din("nw", [1, 2048]); wso = din("wso", [2048, D])
    ccw = din("ccw", [128, 8, 31]); ccb = din("ccb", [128, 8]); clg = din("clg", [128, 8]); clb = din("clb", [128, 8])
    wco = din("wco", [D, D]); wk = din("wk", [D, D]); wv = din("wv", [D, D]); wmo = din("wmo", [D, D]); wout = din("wout", [D, D])
    ln1g = din("ln1g", [1, D]); ln1b = din("ln1b", [1, D]); ln2g = din("ln2g", [1, D]); ln2b = din("ln2b", [1, D])
    wq = din("wq", [D, 2048]); skT = din("skT", [128, 16, 128])
    pu = din("pu", [16384, D]); pv = din("pv", [16384, D])
    mempT = din("mempT", [D, 256])
    st_ssd = din("st_ssd", [NS, 128, 2048]); st_sconvT = din("st_sconvT", [128, 24, NS, 3]); st_sconv = din("st_sconv", [NS, 3, 3072])
    st_cconvT = din("st_cconvT", [128, 8, NS, 30]); st_cconv = din("st_cconv", [NS, 30, D])
    ck = din("ck", [NS, 256, D]); cv = din("cv", [NS, 256, D])

    yp = dout("yp", [T, D]); ys = dout("ys", [NS, D])
    o_ssd_p = dout("o_ssd_p", [2048, 128]); o_sconv_p = dout("o_sconv_p", [3, 3072]); o_cconv_p = dout("o_cconv_p", [30, D])
    o_k_p = dout("o_k_p", [256, D]); o_v_p = dout("o_v_p", [256, D])
    o_ssd_s = dout("o_ssd_s", [NS, 128, 2048]); o_sconv_s = dout("o_sconv_s", [NS, 3, 3072]); o_cconv_s = dout("o_cconv_s", [NS, 30, D])

    ident = sb("ident", [128, 128]); identb = sb("identb", [128, 128], BF16)
    ones = sb("ones", [128, 128]); tri = sb("tri", [128, 128]); sel_last = sb("sel_last", [128, 128])
    S.op("pool", lambda e: e.memset(ident[:], 0.0), writes=["ident"])
    S.op("pool", lambda e: e.affine_select(out=ident[:], in_=ident[:], pattern=[[-1, 128]], compare_op=ALU.not_equal,
                                           fill=1.0, base=0, channel_multiplier=1), reads=["ident"], writes=["ident"])
    S.op("pool", lambda e: e.tensor_copy(out=identb[:], in_=ident[:]), reads=["ident"], writes=["identb"])
    S.op("pool", lambda e: e.memset(ones[:], 1.0), writes=["ones"])
    S.op("pool", lambda e: e.affine_select(out=tri[:], in_=ones[:], pattern=[[1, 128]], compare_op=ALU.is_ge,
                                           fill=0.0, base=0, channel_multiplier=-1), reads=["ones"], writes=["tri"])
    S.op("pool", lambda e: e.affine_select(out=sel_last[:], in_=ones[:], pattern=[[0, 128]], compare_op=ALU.is_ge,
                                           fill=0.0, base=-127, channel_multiplier=1), reads=["ones"], writes=["sel_last"])

    scw_t = sb("scw_t", [128, 24, 4]); scb_t = sb("scb_t", [128, 24])
    ccw_t = sb("ccw_t", [128, 8, 31]); ccb_t = sb("ccb_t", [128, 8]); clg_t = sb("clg_t", [128, 8]); clb_t = sb("clb_t", [128, 8])
    for t_, d_, k_ in ((scw_t, scw, "scw_t"), (scb_t, scb, "scb_t"), (ccw_t, ccw, "ccw_t"), (ccb_t, ccb, "ccb_t"),
                       (clg_t, clg, "clg_t"), (clb_t, clb, "clb_t")):
        S.dma(lambda e, t_=t_, d_=d_: e.dma_start(out=t_[:], in_=d_), writes=[k_])
    dtb_t = sb("dtb_t", [128, 32]); a_t = sb("a_t", [128, 32]); dsk_t = sb("dsk_t", [128, 32])
    S.dma(lambda e: e.dma_start(out=dtb_t[:], in_=dtb.partition_broadcast(128)), writes=["dtb_t"])
    S.dma(lambda e: e.dma_start(out=a_t[:], in_=alog.partition_broadcast(128)), writes=["a_t"])
    S.dma(lambda e: e.dma_start(out=dsk_t[:], in_=dsk.partition_broadcast(128)), writes=["dsk_t"])
    S.op("act", lambda e: e.activation(out=a_t[:], in_=a_t[:], func=AF.Exp), reads=["a_t"], writes=["a_t"])
    S.op("dve", lambda e: e.tensor_scalar(out=a_t[:], in0=a_t[:], scalar1=-1.0, scalar2=None, op0=ALU.mult), reads=["a_t"], writes=["a_t"])

    PB = [st.enter_context(nc.psum_tensor("pb%d" % i, [128, 512], F32)) for i in range(8)]

    R = sb("R", [128, 4096]); MT = sb("MT", [128, 4096], BF16)
    S.alias = {"sB": ["aT", "cvT"], "sC": ["G4", "G5", "G6", "G7"], "h0_0": ["G8", "G9", "G10", "G11"], "h0_1": ["G8", "G9", "G10", "G11"], "Ks": ["G4", "G5", "G6", "G7"],
               "Vs": ["G8", "G9", "G10", "G11"],
               "MT": ["MT0", "MT1", "MT2", "MT3"], "xT32": ["R"],
               "qb": ["y32"], "selw": ["y32"], "skT32": ["R"], "mT32": ["R"], "kv32": ["xres"], "mTb": ["sz"], "otok": ["R"], "o4": ["R"],
               "x1": ["xtok"], "h1": ["xtok"], "qpT": ["xdt"], "mrgb": ["xdte"], "x1T": ["xdte"], "ynT": ["MT0", "MT1"], "cactT": ["MT2"],
               "qT": ["MT3"], "gbr": ["aT"], "mrg": ["cvT"], "wld0": ["G4", "G5", "G6", "G7", "G8", "G9", "G10", "G11"], "Rall": ["R", "R2"]}

    import os
    STAGE = os.environ.get('STAGE', '')

    def cut(tag):
        if STAGE.startswith(tag):
            S.frozen = True

    def OP(e, fn, r=(), w=()):
        return S.op(e, fn, reads=r, writes=w)

    def bc(ap, shape):
        return ap.to_broadcast(list(shape))

    wld = [sb("wld0", [128, 4096])] * 2
    wbf = [sb("wbf0", [128, 4096], BF16), sb("wbf1", [128, 4096], BF16)]
    wctr = [0]

    def load_w32(dram, kc_n, c0, ncols, r0=0):
        i = wctr[0] % 2
        wctr[0] += 1
        src = dram[r0:r0 + kc_n * 128, c0:c0 + ncols].rearrange("(kc p) n -> p kc n", p=128)
        v32 = wld[i][:, 0:kc_n * ncols].rearrange("p (kc n) -> p kc n", kc=kc_n)
        vbf = wbf[i][:, 0:kc_n * ncols].rearrange("p (kc n) -> p kc n", kc=kc_n)
        S.dma(lambda e: e.dma_start(out=v32, in_=src), writes=["wld0"])
        if wctr[0] % 2:
            OP("act", lambda e: e.copy(out=vbf, in_=v32), ["wld0"], ["wbf%d" % i])
        else:
            OP("dve", lambda e: e.tensor_copy(out=vbf, in_=v32), ["wld0"], ["wbf%d" % i])
        return vbf, "wbf%d" % i


    PANELS = ([("w_in", w_in, 8, S_Z + pn * 512, 512) for pn in range(6)] + [("w_in", w_in, 8, S_XBC, 32)]
              + [("w_in", w_in, 8, pn * 512, 512) for pn in range(4)] + [("w_in", w_in, 8, S_DT + pn * 512, 512) for pn in range(4)]
              + [("w_in", w_in, 8, S_CONF + pn * 512, 512) for pn in range(2)]
              + [("w_in", w_in, 8, S_MEMQ + br * 1024 + pn * 512, 512) for br in range(3) for pn in range(2)]
              + [("wso", wso, 16, pn * 256, 256) for pn in range(4)] + [("wco", wco, 8, pn * 512, 512) for pn in range(2)]
              + [("wmo", wmo, 8, pn * 512, 512) for pn in range(2)] + [("wout", wout, 8, pn * 512, 512) for pn in range(2)]
              + [("wq", wq, 8, pn * 512, 512) for pn in range(4)])
    wscr = dscr("wscr", [len(PANELS), 128, 4096], BF16)
    panel_id = {}
    for pi, (nm_, dram_, kc_n, c0, ncols) in enumerate(PANELS):
        panel_id[(nm_, c0)] = pi
        S.dma(lambda e, pi=pi, dram_=dram_, kc_n=kc_n, c0=c0, ncols=ncols: e.dma_start(
            out=wscr[pi][:, 0:kc_n * ncols].rearrange("p (kc n) -> p kc n", kc=kc_n),
            in_=dram_[0:kc_n * 128, c0:c0 + ncols].rearrange("(kc p) n -> p kc n", p=128)), writes=["wscr%d" % pi], e="pool")
    pu16 = dscr("pu16", [16384, D], BF16); pv16 = dscr("pv16", [16384, D], BF16)
    TABKEYS = []
    for ti, (src_, dst_) in enumerate(((pu, pu16), (pv, pv16))):
        for c in range(32):
            S.dma(lambda e, src_=src_, dst_=dst_, c=c: e.dma_start(out=dst_[c * 512:(c + 1) * 512, :], in_=src_[c * 512:(c + 1) * 512, :]),
                  writes=["tab%d_%d" % (ti, c)], e="pool")
            TABKEYS.append("tab%d_%d" % (ti, c))

    def load_w(dram, kc_n, c0, ncols, r0=0):
        pi = panel_id[(dram.name, c0)]
        i = wctr[0] % 2
        wctr[0] += 1
        n_ = kc_n * ncols
        S.dma(lambda e: e.dma_start(out=wbf[i][:, 0:n_], in_=wscr[pi][:, 0:n_]), reads=["wscr%d" % pi], writes=["wbf%d" % i])
        return wbf[i][:, 0:n_].rearrange("p (kc n) -> p kc n", kc=kc_n), "wbf%d" % i

    hist_s = sb("hist_s", [128, 24, 3], BF16); hist_c = sb("hist_c", [128, 8, 30], BF16)
    OP("pool", lambda e: e.memset(hist_s[:], 0.0), w=["hist_s"])
    OP("pool", lambda e: e.memset(hist_c[:], 0.0), w=["hist_c"])
    hT = sb("hT", [128, 2048]); hTb = sb("hTb", [128, 2048], BF16)
    OP("pool", lambda e: e.memset(hT[:], 0.0), w=["hT"])
    OP("pool", lambda e: e.memset(hTb[:], 0.0), w=["hTb"])
    tail_s = sb("tail_s", [128, 24, 16]); tail_c = sb("tail_c", [128, 8, 32])
    nw_t = sb("nw_t", [128, 2048]); ln_t = sb("ln_t", [128, 4, 1024])
    S.dma(lambda e: e.dma_start(out=nw_t[:], in_=nw.partition_broadcast(128)), writes=["nw_t"])
    for i_, d_ in enumerate((ln1g, ln1b, ln2g, ln2b)):
        S.dma(lambda e, i_=i_, d_=d_: e.dma_start(out=ln_t[:, i_, :], in_=d_.partition_broadcast(128)), writes=["ln_t"])
    cut("c1")
    skT32 = R[:, 0:2048].rearrange("p (a b) -> p a b", b=128); skTb = sb("skTb", [128, 16, 128], BF16)
    S.dma(lambda e: e.dma_start(out=skT32[:], in_=skT), writes=["skT32"])
    OP("pool", lambda e: e.tensor_copy(out=skTb[:], in_=skT32[:]), ["skT32"], ["skTb"])
    cut("c2")
    iota16 = sb("iota16", [128, 16])
    OP("pool", lambda e: e.iota(iota16[:], pattern=[[1, 16]], base=0, channel_multiplier=0, allow_small_or_imprecise_dtypes=True), w=["iota16"])

    cut("c3")
    dtt = sb("dtt", [128, 32]); sz = sb("sz", [128, 2048], BF16)
    xres = sb("xres", [128, 1024]); kv32 = xres
    mTb = sz[:, :].rearrange("p (a b) -> p a b", b=256)
    KT = sb("KT", [128, 8, 256], BF16); Vb = sb("Vb", [128, 2, 1024], BF16)
    mT32 = R[:, 0:2048].rearrange("p (a b) -> p a b", b=256)
    S.dma(lambda e: e.dma_start(out=mT32[:], in_=mempT.rearrange("(kc p) m -> p kc m", p=128)), writes=["mT32"])
    OP("dve", lambda e: e.tensor_copy(out=mTb[:], in_=mT32[:]), ["mT32"], ["mTb"])
    for pn in range(2):
        vbf, wkey = load_w32(wk, 8, pn * 512, 512)
        for cc in range(4):
            for kc in range(8):
                OP("pe", lambda e, kc=kc, cc=cc, vbf=vbf: e.matmul(PB[0][:, 0:256], lhsT=vbf[:, kc, cc * 128:(cc + 1) * 128], rhs=mTb[:, kc, :],
                                                                  start=(kc == 0), stop=(kc == 7)), [wkey, "mTb"], ["pb0"])
            OP("act", lambda e, c=pn * 4 + cc: e.copy(out=KT[:, c, :], in_=PB[0][:, 0:256]), ["pb0"], ["KT"])
    cut("c4")
    for wi, (wd, od) in enumerate(((wk, o_k_p), (wv, o_v_p))):
        for mt in range(2):
            for pn in range(2):
                vbf, wkey = load_w32(wd, 8, pn * 512, 512)
                for kc in range(8):
                    OP("pe", lambda e, kc=kc, vbf=vbf, mt=mt: e.matmul(PB[1][:, :], lhsT=mTb[:, kc, mt * 128:(mt + 1) * 128], rhs=vbf[:, kc, :],
                                                                      start=(kc == 0), stop=(kc == 7)), [wkey, "mTb"], ["pb1"])
                OP("act", lambda e, pn=pn: e.copy(out=kv32[:, pn * 512:(pn + 1) * 512], in_=PB[1][:, :]), ["pb1"], ["kv32"])
                if wi == 1 and "novb" not in STAGE:
                    OP("dve", lambda e, pn=pn, mt=mt: e.tensor_copy(out=Vb[:, mt, pn * 512:(pn + 1) * 512], in_=kv32[:, pn * 512:(pn + 1) * 512]), ["kv32"], ["Vb"])
            if "noout" not in STAGE:
                S.dma(lambda e, od=od, mt=mt: e.dma_start(out=od[mt * 128:(mt + 1) * 128, :], in_=kv32[:]), reads=["kv32"], writes=["okv"])

    cut("c5")
    R_early = R
    xT32 = R_early[:, 0:1024].rearrange("p (a b) -> p a b", b=128); xTb = sb("xTb", [128, 8, 128], BF16)
    xtok = sb("xtok", [128, 2048])
    BT = sb("BT", [128, 4, 128], BF16); CT = sb("CT", [128, 4, 128], BF16); Btok = sb("Btok", [128, 4, 128], BF16)
    Ctok_s = sb("Ctok_s", [NS, 4, 128]); Btok_s = sb("Btok_s", [NS, 4, 128])
    xh = [sb("xh%d" % i, [128, 160], BF16) for i in range(2)]
    fm32 = [sb("fm32_%d" % i, [128, 512]) for i in range(2)]
    dg = [sb("dg0", [128, 4, 128], BF16)] * 2
    xdt = sb("xdt", [128, 2048], BF16); xdte = sb("xdte", [128, 2048], BF16); cbm = sb("cbm", [128, 512])
    sm = sb("sm", [128, 8, 32])
    y32 = sb("y32", [128, 2048]); ynT = MT[:, 0:2048].rearrange("p (a b) -> p a b", b=128)
    acv = sb("acv", [128, 16, 128]); aT = acv[:, 0:8, :]; cvT = acv[:, 8:16, :]; cactT = MT[:, 2048:3072].rearrange("p (a b) -> p a b", b=128)
    stat = sb("stat", [128, 4, 128])
    qT = MT[:, 3072:4096].rearrange("p (a b) -> p a b", b=128); oT = sb("oT", [128, 8, 128], BF16); PnT = sb("PnT", [128, 2, 128], BF16)
    att = sb("att", [128, 3, 256]); col = sb("col", [128, 16])
    gbr = acv[:, 0:8, :]; mrg = acv[:, 8:16, :]; mrgb = xdte[:, 0:1024].rearrange("p (a b) -> p a b", b=128)
    x1 = xtok[:, 0:1024]; x1T = xdte[:, 1024:2048].rearrange("p (a b) -> p a b", b=128); h1 = xtok[:, 1024:2048]
    qpT = xdt[:, :].rearrange("p (a b) -> p a b", b=128)
    top = sb("top", [128, 16, 16]); idxu = sb("idxu", [128, 16, 16], U32); idxf = sb("idxf", [128, 16, 16])
    best = sb("best", [128, 8, 16]); pos = sb("pos", [128, 8, 16], U32); pab = sb("pab", [128, 2, 128], U32); pabf = sb("pabf", [128, 2, 128])
    selw = y32; ids = sb("ids", [128, 2, 128]); idi = sb("idi", [128, 128], I32)
    gw = sb("gw", [128, 128]); dots = sb("dots", [128, 128]); coef = sb("coef", [128, 128])
    Gall = wld[0][:, :].rearrange("p (a b) -> p a b", b=1024); Gp = sb("Gp", [128, 4096], BF16); G = [Gp[:, i * 1024:(i + 1) * 1024] for i in range(4)] + [wld[0][:, :].bitcast(BF16)[:, i * 1024:(i + 1) * 1024] for i in range(8)]; NG = len(G); x1p = sb("x1p", [128, 1024])
    dgv = [sb("dgv%d" % i, [128, 128], BF16) for i in range(2)]
    cut("c6")
    OP("pool", lambda e: e.memset(idi[:], 0), w=["idi"])
    sQ = sb("sQ", [128, NS, 16]); sB = acv; sC = Gall[:, 0:2, :].rearrange("p a (b n) -> p (a b) n", n=128)
    h0 = [Gall[:, 2:4, :].rearrange("p a d -> p (a d)")] * 2
    scr_x = dscr("scr_x", [NS, 2048]); scr_B = dscr("scr_B", [NS, 512]); scr_C = dscr("scr_C", [NS, 512]); scr_y = dscr("scr_y", [NS, 2048])
    scr_q = dscr("scr_q", [NS, 1024]); scr_o = dscr("scr_o", [NS, 4, 1024])
    cut("c7")
    sel32 = sb("sel32", [32, 128])
    OP("pool", lambda e: e.memset(sel32[:], 1.0), w=["sel32"])
    OP("pool", lambda e: e.affine_select(out=sel32[:], in_=sel32[:], pattern=[[1, 128]], compare_op=ALU.is_ge, fill=0.0, base=0,
                                         channel_multiplier=-4), ["sel32"], ["sel32"])
    OP("pool", lambda e: e.affine_select(out=sel32[:], in_=sel32[:], pattern=[[-1, 128]], compare_op=ALU.is_ge, fill=0.0, base=3,
                                         channel_multiplier=4), ["sel32"], ["sel32"])
    cut("c8")
    hp = sb("hp", [32, 4]); hp2 = sb("hp2", [32, 3, NS]); qsc = sb("qsc", [128, 4, NS])
    S.dma(lambda e: e.dma_start(out=hp[:, 0:1], in_=dtb.rearrange("o h -> h o")), writes=["hp"])
    S.dma(lambda e: e.dma_start(out=hp[:, 1:2], in_=alog.rearrange("o h -> h o")), writes=["hp"])
    S.dma(lambda e: e.dma_start(out=hp[:, 2:3], in_=dsk.rearrange("o h -> h o")), writes=["hp"])
    OP("act", lambda e: e.activation(out=hp[:, 1:2], in_=hp[:, 1:2], func=AF.Exp), ["hp"], ["hp"])
    OP("dve", lambda e: e.tensor_scalar(out=hp[:, 1:2], in0=hp[:, 1:2], scalar1=-1.0, scalar2=None, op0=ALU.mult), ["hp"], ["hp"])
    Ks = Gall[:, 0:2, :]; Vs = Gall[:, 2:4, :]; qb = selw[:, 0:1024]
    Sall = sb("Sall", [128, NS, 8]); Eall = sb("Eall", [128, NS, 8]); o4 = R[0:4, 1024:2048]; otok = R[0:NS, 0:1024]

    def ln_tok(src, dst, gi, N, scratch=None, skey="selw", key=None):
        scr = scratch if scratch is not None else selw[:, 0:1024]
        ks = [key] if key else ["h1", "x1"]
        OP("dve", lambda e: e.tensor_reduce(out=col[0:N, 0:1], in_=src[0:N, :], axis=AX.X, op=ALU.add), ks, ["col"])
        OP("dve", lambda e: e.tensor_scalar(out=col[0:N, 0:1], in0=col[0:N, 0:1], scalar1=1.0 / 1024, scalar2=None, op0=ALU.mult), ["col"], ["col"])
        OP("dve", lambda e: e.tensor_scalar(out=src[0:N, :], in0=src[0:N, :], scalar1=col[0:N, 0:1], scalar2=None, op0=ALU.subtract),
           ["col"] + ks, ks)
        OP("act", lambda e: e.activation(out=scr[0:N, :], in_=src[0:N, :], func=AF.Square, accum_out=col[0:N, 1:2]), ks, [skey, "col"])
        OP("dve", lambda e: e.tensor_scalar(out=col[0:N, 1:2], in0=col[0:N, 1:2], scalar1=1.0 / 1024, scalar2=EPS, op0=ALU.mult, op1=ALU.add),
           ["col"], ["col"])
        OP("act", lambda e: e.activation(out=col[0:N, 1:2], in_=col[0:N, 1:2], func=AF.Sqrt), ["col"], ["col"])
        OP("dve", lambda e: e.reciprocal(out=col[0:N, 1:2], in_=col[0:N, 1:2]), ["col"], ["col"])
        OP("dve", lambda e: e.scalar_tensor_tensor(out=dst[0:N, :], in0=src[0:N, :], scalar=col[0:N, 1:2], in1=ln_t[0:N, gi, :],
                                                   op0=ALU.mult, op1=ALU.mult), ks + ["col", "ln_t"], ks)
        OP("dve", lambda e: e.tensor_tensor(out=dst[0:N, :], in0=dst[0:N, :], in1=ln_t[0:N, gi + 1, :], op=ALU.add), ks + ["ln_t"], ks)

    import os
    STAGE = os.environ.get('STAGE', '')

    PUMP = int(os.environ.get('PUMP', '2'))
    PUMP2 = int(os.environ.get('PUMP2', '3'))
    PP = [PUMP2]

    def emit_block(blk):
        samp = blk == 16
        N = NS if samp else 128
        t0 = 0 if samp else blk * 128
        last = blk == 15
        srcT = xsT if samp else xpT[:, t0:t0 + 128]
        S.dma(lambda e: e.dma_start(out=xT32[:, :, 0:N], in_=srcT.rearrange("(kc p) t -> p kc t", p=128)), writes=["xT32"])
        OP("dve", lambda e: e.tensor_copy(out=xTb[:, :, 0:N], in_=xT32[:, :, 0:N]), ["xT32"], ["xTb"])
        S.dma(lambda e: e.dma_start(out=xres[0:N, :], in_=(xs if samp else xp[t0:t0 + 128, :])), writes=["xres"])

        def proj_fm(vbf, wkey, cc, ps, pskey):
            pump(PP[0])
            for kc in range(8):
                OP("pe", lambda e, kc=kc: e.matmul(ps[:, 0:N], lhsT=vbf[:, kc, cc * 128:(cc + 1) * 128], rhs=xTb[:, kc, 0:N],
                                                   start=(kc == 0), stop=(kc == 7)), [wkey, "xTb"], [pskey])

        def proj_tm(vbf, wkey, ncols, ps, pskey, c0=0):
            for kc in range(8):
                OP("pe", lambda e, kc=kc: e.matmul(ps[0:N, 0:ncols], lhsT=xTb[:, kc, 0:N], rhs=vbf[:, kc, c0:c0 + ncols],
                                                   start=(kc == 0), stop=(kc == 7)), [wkey, "xTb"], [pskey])

        PP[0] = 3
        for pn in range(6):
            pump(PUMP)
            vbf, wkey = load_w(w_in, 8, S_Z + pn * 512, 512)
            for cc in range(4):
                j = pn * 4 + cc
                ps, pskey = PB[j % 2], "pb%d" % (j % 2)
                proj_fm(vbf, wkey, cc, ps, pskey)
                xhj, xkey = xh[j % 2], "xh%d" % (j % 2)
                dgj, dkey = dg[0], "dg0"
                for k in range(4):
                    OP("act", lambda e, k=k, j=j, dgj=dgj: e.activation(out=dgj[:, k, :], in_=ident[:], func=AF.Copy, scale=scw_t[:, j, k:k + 1]),
                       ["ident", "scw_t"], [dkey])
                if not samp:
                    OP("dve", lambda e, j=j, xhj=xhj: e.tensor_copy(out=xhj[:, 0:3], in_=hist_s[:, j, :]), ["hist_s"], [xkey])
                    OP("act", lambda e, xhj=xhj, ps=ps: e.copy(out=xhj[:, 3:3 + N], in_=ps[:, 0:N]), [pskey], [xkey])
                    OP("dve", lambda e, j=j, xhj=xhj: e.tensor_copy(out=hist_s[:, j, :], in_=xhj[:, N:N + 3]), [xkey], ["hist_s"])
                    if last:
                        OP("dve", lambda e, j=j, ps=ps: e.tensor_copy(out=tail_s[:, j, 0:3], in_=ps[:, N - 3:N]), [pskey], ["tail_s"])
                    rhs_k = lambda k, xhj=xhj: xhj[:, k:k + N]
                else:
                    xv = xhj[:, 0:64].rearrange("p (b k) -> p b k", k=4)
                    stg = fm32[0][:, 0:48].rearrange("p (b k) -> p b k", k=3)
                    S.dma(lambda e, j=j, stg=stg: e.dma_start(out=stg, in_=st_sconvT[:, j, :, :]), writes=["fm32_0"])
                    OP("dve", lambda e, xv=xv, stg=stg: e.tensor_copy(out=xv[:, :, 0:3], in_=stg), ["fm32_0"], [xkey])
                    OP("act", lambda e, xv=xv, ps=ps: e.copy(out=xv[:, :, 3], in_=ps[:, 0:N]), [pskey], [xkey])
                    OP("dve", lambda e, j=j, ps=ps: e.tensor_copy(out=tail_s[:, j, 0:16], in_=ps[:, 0:N]), [pskey], ["tail_s"])
                    rhs_k = lambda k, xv=xv: xv[:, :, k]
                pc, pckey = PB[2 + j % 2], "pb%d" % (2 + j % 2)
                for k in range(4):
                    OP("pe", lambda e, k=k, dgj=dgj, rhs_k=rhs_k, pc=pc: e.matmul(pc[:, 0:N], lhsT=dgj[:, k, :], rhs=rhs_k(k),
                                                                                 start=(k == 0), stop=(k == 3)), [dkey, xkey], [pckey])
                fm, fkey = fm32[1], "fm32_1"
                OP("act", lambda e, j=j, pc=pc: e.activation(out=fm[:, 0:N], in_=pc[:, 0:N], func=AF.Silu, bias=scb_t[:, j:j + 1]),
                   [pckey, "scb_t"], [fkey])
                if j >= 20:
                    OP("act", lambda e, j=j: e.copy(out=CT[:, j - 20, 0:N], in_=fm[:, 0:N]), [fkey], ["CT"])
                    if not samp:
                        continue
                if 16 <= j < 20:
                    OP("act", lambda e, j=j: e.copy(out=BT[:, j - 16, 0:N], in_=fm[:, 0:N]), [fkey], ["BT"])
                pt, ptkey = PB[4 + j % 2], "pb%d" % (4 + j % 2)
                OP("pe", lambda e, pt=pt: e.transpose(out=pt[0:N, 0:128], in_=fm[:, 0:N], identity=ident[:]), [fkey, "ident"], [ptkey])
                if j < 16:
                    OP("act", lambda e, j=j, pt=pt: e.copy(out=xtok[0:N, j * 128:(j + 1) * 128], in_=pt[0:N, 0:128]), [ptkey], ["xtok"])
                elif j < 20:
                    OP("act", lambda e, j=j, pt=pt: e.copy(out=Btok[0:N, j - 16, :], in_=pt[0:N, 0:128]), [ptkey], ["Btok"])
                    if samp:
                        OP("act", lambda e, j=j, pt=pt: e.copy(out=Btok_s[0:N, j - 16, :], in_=pt[0:N, 0:128]), [ptkey], ["Btok_s"])
                else:
                    OP("act", lambda e, j=j, pt=pt: e.copy(out=Ctok_s[0:N, j - 20, :], in_=pt[0:N, 0:128]), [ptkey], ["Ctok_s"])

        if STAGE == "xbc":
            return
        pump(PUMP)
        vbf, wkey = load_w(w_in, 8, S_XBC, 32)
        proj_tm(vbf, wkey, 32, PB[4], "pb4")
        OP("dve", lambda e: e.tensor_tensor(out=dtt[0:N, :], in0=PB[4][0:N, 0:32], in1=dtb_t[0:N, :], op=ALU.add), ["pb4", "dtb_t"], ["dtt"])
        OP("act", lambda e: e.activation(out=dtt[0:N, :], in_=dtt[0:N, :], func=AF.Exp), ["dtt"], ["dtt"])
        OP("act", lambda e: e.activation(out=dtt[0:N, :], in_=dtt[0:N, :], func=AF.Ln, bias=1.0), ["dtt"], ["dtt"])
        if samp:
            OP("pe", lambda e: e.transpose(out=PB[4][0:32, 64:64 + NS], in_=dtt[0:NS, :], identity=ident[0:NS, 0:NS]), ["dtt", "ident"], ["pb4"])
            OP("act", lambda e: e.copy(out=hp2[:, 0, :], in_=PB[4][0:32, 64:64 + NS]), ["pb4"], ["hp2"])
            OP("act", lambda e: e.activation(out=hp2[:, 1, :], in_=hp2[:, 0, :], func=AF.Exp, scale=hp[:, 1:2]), ["hp2", "hp"], ["hp2"])
            OP("dve", lambda e: e.tensor_copy(out=hp2[:, 2, :], in_=bc(hp[:, 2:3], [32, NS])), ["hp"], ["hp2"])
        if STAGE == "dt":
            return
        for pn in range(4):
            pump(PUMP)
            vbf, wkey = load_w(w_in, 8, pn * 512, 512)
            ps, pskey = PB[pn % 2], "pb%d" % (pn % 2)
            proj_tm(vbf, wkey, 512, ps, pskey)
            OP("act", lambda e, pn=pn, ps=ps: e.activation(out=sz[0:N, pn * 512:(pn + 1) * 512], in_=ps[0:N, :], func=AF.Silu), [pskey], ["sz"])

        if STAGE == "z":
            return
        if not samp:
            da, acs, dte, cd, eacs = (sm[:, i_, :] for i_ in range(5))
            OP("dve", lambda e: e.tensor_tensor(out=da, in0=dtt[:, :], in1=a_t[:, :], op=ALU.mult), ["dtt", "a_t"], ["sm"])
            OP("pe", lambda e: e.matmul(PB[4][:, 0:32], lhsT=tri[:], rhs=da, start=True, stop=True), ["tri", "sm"], ["pb4"])
            OP("act", lambda e: e.copy(out=acs, in_=PB[4][:, 0:32]), ["pb4"], ["sm"])
            OP("pe", lambda e: e.matmul(PB[4][:, 32:64], lhsT=sel_last[:], rhs=acs, start=True, stop=True), ["sel_last", "sm"], ["pb4"])
            OP("dve", lambda e: e.tensor_tensor(out=dte, in0=PB[4][:, 32:64], in1=acs, op=ALU.subtract), ["pb4", "sm"], ["sm"])
            OP("act", lambda e: e.activation(out=dte, in_=dte, func=AF.Exp), ["sm"], ["sm"])
            OP("dve", lambda e: e.tensor_tensor(out=dte, in0=dte, in1=dtt[:, :], op=ALU.mult), ["sm", "dtt"], ["sm"])
            OP("act", lambda e: e.activation(out=cd, in_=PB[4][:, 32:64], func=AF.Exp), ["pb4"], ["sm"])
            OP("act", lambda e: e.activation(out=eacs, in_=acs, func=AF.Exp), ["sm"], ["sm"])
            R3 = R[:, :].rearrange("p (h l) -> p h l", l=128)
            OP("pool", lambda e: e.tensor_tensor(out=R3, in0=bc(tri[:, :].unsqueeze(1), [128, 32, 128]), in1=bc(da.unsqueeze(2), [128, 32, 128]),
                                                 op=ALU.mult), ["tri", "sm"], ["R", "R2"])
            for half in range(2):
                for q in range(4):
                    qq = half * 4 + q
                    OP("pe", lambda e, q=q, qq=qq: e.matmul(PB[q][:, :], lhsT=ones[:], rhs=R[:, qq * 512:(qq + 1) * 512], start=True, stop=True),
                       ["ones", "R" if half == 0 else "R2"], ["pb%d" % q])
                for hh in range(16):
                    h = half * 16 + hh
                    OP("dve", lambda e, h=h, hh=hh: e.tensor_scalar(out=R[:, h * 128:(h + 1) * 128], in0=PB[hh // 4][:, (hh % 4) * 128:(hh % 4 + 1) * 128],
                                                                    scalar1=acs[:, h:h + 1], scalar2=0.0, op0=ALU.subtract, op1=ALU.min),
                       ["pb%d" % (hh // 4), "sm"], ["R" if half == 0 else "R2"])
            pump(6)
            OP("act", lambda e: e.activation(out=R[:, :], in_=R[:, :], func=AF.Exp), ["R", "R2"], ["R", "R2"])
            pump(6)
            for g in range(4):
                OP("pe", lambda e, g=g: e.matmul(PB[4][:, g * 128:(g + 1) * 128], lhsT=BT[:, g, :], rhs=CT[:, g, :], start=True, stop=True),
                   ["BT", "CT"], ["pb4"])
            OP("dve", lambda e: e.tensor_tensor(out=cbm[:, :].rearrange("p (g l) -> p g l", l=128), in0=PB[4][:, :].rearrange("p (g l) -> p g l", l=128),
                                                in1=bc(tri[:, :].unsqueeze(1), [128, 4, 128]), op=ALU.mult), ["pb4", "tri"], ["cbm"])
            OP("dve", lambda e: e.tensor_tensor(out=MT[:, :].rearrange("p (g r l) -> p g r l", g=4, r=8),
                                                in0=R[:, :].rearrange("p (g r l) -> p g r l", g=4, r=8),
                                                in1=bc(cbm[:, :].rearrange("p (g l) -> p g l", l=128).unsqueeze(2), [128, 4, 8, 128]), op=ALU.mult),
               ["R", "R2", "cbm"], ["MT"])
            pump(6)
            x3 = xtok[:, :].rearrange("p (h d) -> p h d", d=64)
            OP("pool", lambda e: e.tensor_tensor(out=xdt[:, :].rearrange("p (h d) -> p h d", d=64), in0=x3,
                                                 in1=bc(dtt[:, :].unsqueeze(2), [128, 32, 64]), op=ALU.mult), ["xtok", "dtt"], ["xdt"])
            OP("pool", lambda e: e.tensor_tensor(out=xdte[:, :].rearrange("p (h d) -> p h d", d=64), in0=x3,
                                                 in1=bc(dte.unsqueeze(2), [128, 32, 64]), op=ALU.mult), ["xtok", "sm"], ["xdte"])
            for h in range(32):
                OP("pe", lambda e, h=h: e.matmul(PB[h // 8][:, (h % 8) * 64:(h % 8 + 1) * 64], lhsT=MT[:, h * 128:(h + 1) * 128],
                                                 rhs=xdt[:, h * 64:(h + 1) * 64], start=True, stop=True), ["MT", "xdt"], ["pb%d" % (h // 8)])
            for g in range(4):
                bk = 4 + g % 2
                OP("pe", lambda e, g=g, bk=bk: e.matmul(PB[bk][:, :], lhsT=CT[:, g, :], rhs=hTb[:, g * 512:(g + 1) * 512], start=True, stop=True),
                   ["CT", "hTb"], ["pb%d" % bk])
                OP("dve", lambda e, g=g, bk=bk: e.tensor_tensor(out=R[:, g * 512:(g + 1) * 512].rearrange("p (r d) -> p r d", d=64),
                                                                in0=PB[bk][:, :].rearrange("p (r d) -> p r d", d=64),
                                                                in1=bc(eacs[:, g * 8:(g + 1) * 8].unsqueeze(2), [128, 8, 64]), op=ALU.mult),
                   ["pb%d" % bk, "sm"], ["R"])
                OP("dve", lambda e, g=g: e.tensor_tensor(out=y32[:, g * 512:(g + 1) * 512], in0=PB[g][:, :], in1=R[:, g * 512:(g + 1) * 512],
                                                         op=ALU.add), ["pb%d" % g, "R"], ["y32"])
            OP("pool", lambda e: e.tensor_tensor(out=R[:, 2048:4096].rearrange("p (h d) -> p h d", d=64), in0=x3,
                                                 in1=bc(dsk_t[:, :].unsqueeze(2), [128, 32, 64]), op=ALU.mult), ["xtok", "dsk_t"], ["R2"])
            OP("pool", lambda e: e.tensor_tensor(out=y32[:, :], in0=y32[:, :], in1=R[:, 2048:4096], op=ALU.add), ["R2", "y32"], ["y32"])
            pump(6)
            for g in range(4):
                OP("pe", lambda e, g=g: e.matmul(PB[g][:, :], lhsT=Btok[:, g, :], rhs=xdte[:, g * 512:(g + 1) * 512], start=True, stop=True),
                   ["Btok", "xdte"], ["pb%d" % g])
            OP("dve", lambda e: e.tensor_tensor(out=hT[:, :].rearrange("p (h d) -> p h d", d=64), in0=hT[:, :].rearrange("p (h d) -> p h d", d=64),
                                                in1=bc(cd.unsqueeze(2), [128, 32, 64]), op=ALU.mult), ["hT", "sm"], ["hT"])
            for g in range(4):
                OP("dve", lambda e, g=g: e.tensor_tensor(out=hT[:, g * 512:(g + 1) * 512], in0=hT[:, g * 512:(g + 1) * 512], in1=PB[g][:, :],
                                                         op=ALU.add), ["hT", "pb%d" % g], ["hT"])
            OP("act", lambda e: e.copy(out=hTb[:, :], in_=hT[:, :]), ["hT"], ["hTb"])
            if last:
                for hp_ in range(16):
                    OP("pe", lambda e, hp_=hp_: e.transpose(out=PB[hp_ % 2][:, 0:128], in_=hT[:, hp_ * 128:(hp_ + 1) * 128], identity=ident[:]),
                       ["hT", "ident"], ["pb%d" % (hp_ % 2)])
                    OP("act", lambda e, hp_=hp_: e.copy(out=fm32[hp_ % 2][:, 0:128], in_=PB[hp_ % 2][:, 0:128]), ["pb%d" % (hp_ % 2)], ["fm32_%d" % (hp_ % 2)])
                    S.dma(lambda e, hp_=hp_: e.dma_start(out=o_ssd_p[hp_ * 128:(hp_ + 1) * 128, :], in_=fm32[hp_ % 2][:, 0:128]),
                          reads=["fm32_%d" % (hp_ % 2)], writes=["o_ssd_p"])
        else:
            S.dma(lambda e: e.dma_start(out=scr_x, in_=xtok[0:NS, :]), reads=["xtok"], writes=["scr_x"])
            S.dma(lambda e: e.dma_start(out=scr_B, in_=Btok_s[:, :, :].rearrange("p g n -> p (g n)")), reads=["Btok_s"], writes=["scr_B"])
            S.dma(lambda e: e.dma_start(out=scr_C, in_=Ctok_s[:, :, :].rearrange("p g n -> p (g n)")), reads=["Ctok_s"], writes=["scr_C"])
            S.dma(lambda e: e.dma_start(out=sQ[:], in_=scr_x.rearrange("b (q r) -> q b r", r=16)), reads=["scr_x"], writes=["sQ"])
            for g in range(4):
                S.dma(lambda e, g=g: e.dma_start(out=sB[32 * g:32 * (g + 1), :, :], in_=scr_B[:, g * 128:(g + 1) * 128].partition_broadcast(32)),
                      reads=["scr_B"], writes=["sB"])
                S.dma(lambda e, g=g: e.dma_start(out=sC[32 * g:32 * (g + 1), :, :], in_=scr_C[:, g * 128:(g + 1) * 128].partition_broadcast(32)),
                      reads=["scr_C"], writes=["sC"])
            OP("pe", lambda e: e.matmul(PB[4][:, 128:128 + 3 * NS], lhsT=sel32[:, :], rhs=hp2[:, :, :].rearrange("p a b -> p (a b)"),
                                        start=True, stop=True), ["sel32", "hp2"], ["pb4"])
            OP("act", lambda e: e.copy(out=qsc[:, 0:3, :].rearrange("p a b -> p (a b)"), in_=PB[4][:, 128:128 + 3 * NS]), ["pb4"], ["qsc"])
            dtx = sm[:, 0:8, :].rearrange("p a b -> p (a b)")[:, 0:256].rearrange("p (b r) -> p b r", r=16)
            OP("dve", lambda e: e.tensor_tensor(out=dtx, in0=sQ[:, :, :], in1=bc(qsc[:, 0, :].unsqueeze(2), [128, NS, 16]), op=ALU.mult),
               ["sQ", "qsc"], ["sm"])
            yo = y32[:, 0:256].rearrange("p (b r) -> p b r", r=16)
            for b in range(NS):
                hb, hkey = h0[b % 2], "h0_%d" % (b % 2)
                S.dma(lambda e, b=b, hb=hb: e.dma_start(out=hb[:, :], in_=st_ssd[b]), writes=[hkey])
                h3 = hb[:, :].rearrange("p (r n) -> p r n", n=128)
                R3 = R[:, 0:2048].rearrange("p (r n) -> p r n", n=128)
                OP("dve", lambda e, b=b, h3=h3, R3=R3: e.tensor_tensor(out=R3, in0=h3, in1=bc(sC[:, b, :].unsqueeze(1), [128, 16, 128]), op=ALU.mult),
                   [hkey, "sC"], ["R"])
                OP("dve", lambda e, b=b, R3=R3: e.tensor_reduce(out=yo[:, b, :], in_=R3, axis=AX.X, op=ALU.add), ["R"], ["y32"])
                R4 = R[:, 2048:4096].rearrange("p (r n) -> p r n", n=128)
                OP("pool", lambda e, b=b, R4=R4: e.tensor_tensor(out=R4, in0=bc(dtx[:, b, :].unsqueeze(2), [128, 16, 128]),
                                                                 in1=bc(sB[:, b, :].unsqueeze(1), [128, 16, 128]), op=ALU.mult), ["sm", "sB"], ["R2"])
                OP("dve", lambda e, b=b, hb=hb: e.scalar_tensor_tensor(out=hb[:, :], in0=hb[:, :], scalar=qsc[:, 1, b:b + 1], in1=R[:, 2048:4096],
                                                                       op0=ALU.mult, op1=ALU.add), [hkey, "qsc", "R2"], [hkey])
                S.dma(lambda e, b=b, hb=hb: e.dma_start(out=o_ssd_s[b], in_=hb[:, :]), reads=[hkey], writes=["o_ssd_s"])
            OP("dve", lambda e: e.tensor_tensor(out=R[:, 0:2048].rearrange("p (b n) -> p b n", n=128), in0=sC[:, :, :], in1=sB[:, :, :], op=ALU.mult),
               ["sC", "sB"], ["R"])
            OP("dve", lambda e: e.tensor_reduce(out=qsc[:, 3, :], in_=R[:, 0:2048].rearrange("p (b n) -> p b n", n=128), axis=AX.X, op=ALU.add),
               ["R"], ["qsc"])
            OP("dve", lambda e: e.tensor_tensor(out=yo, in0=yo, in1=bc(qsc[:, 1, :].unsqueeze(2), [128, NS, 16]), op=ALU.mult), ["y32", "qsc"], ["y32"])
            OP("dve", lambda e: e.tensor_tensor(out=dtx, in0=dtx, in1=bc(qsc[:, 3, :].unsqueeze(2), [128, NS, 16]), op=ALU.mult), ["sm", "qsc"], ["sm"])
            OP("dve", lambda e: e.tensor_tensor(out=yo, in0=yo, in1=dtx, op=ALU.add), ["y32", "sm"], ["y32"])
            OP("dve", lambda e: e.tensor_tensor(out=dtx, in0=sQ[:, :, :], in1=bc(qsc[:, 2, :].unsqueeze(2), [128, NS, 16]), op=ALU.mult), ["sQ", "qsc"], ["sm"])
            OP("dve", lambda e: e.tensor_tensor(out=yo, in0=yo, in1=dtx, op=ALU.add), ["y32", "sm"], ["y32"])
            S.dma(lambda e: e.dma_start(out=scr_y.rearrange("b (q r) -> q b r", r=16), in_=yo), reads=["y32"], writes=["scr_y"])
            S.dma(lambda e: e.dma_start(out=y32[0:NS, :], in_=scr_y), reads=["scr_y"], writes=["y32"])

        if STAGE == "ssd":
            return
        pump(PUMP)
        OP("dve", lambda e: e.tensor_tensor(out=y32[0:N, :], in0=y32[0:N, :], in1=sz[0:N, :], op=ALU.mult), ["y32", "sz"], ["y32"])
        OP("act", lambda e: e.activation(out=R[0:N, 0:2048], in_=y32[0:N, :], func=AF.Square, accum_out=col[0:N, 2:3]), ["y32"], ["R", "col"])
        OP("dve", lambda e: e.tensor_scalar(out=col[0:N, 2:3], in0=col[0:N, 2:3], scalar1=1.0 / 2048, scalar2=EPS, op0=ALU.mult, op1=ALU.add), ["col"], ["col"])
        OP("act", lambda e: e.activation(out=col[0:N, 2:3], in_=col[0:N, 2:3], func=AF.Sqrt), ["col"], ["col"])
        OP("dve", lambda e: e.reciprocal(out=col[0:N, 2:3], in_=col[0:N, 2:3]), ["col"], ["col"])
        OP("dve", lambda e: e.scalar_tensor_tensor(out=y32[0:N, :], in0=y32[0:N, :], scalar=col[0:N, 2:3], in1=nw_t[0:N, :], op0=ALU.mult, op1=ALU.mult),
           ["y32", "col", "nw_t"], ["y32"])
        for fc in range(16):
            pt, ptkey = PB[4 + fc % 2], "pb%d" % (4 + fc % 2)
            OP("pe", lambda e, fc=fc, pt=pt: e.transpose(out=pt[:, 0:N], in_=y32[0:N, fc * 128:(fc + 1) * 128], identity=ident[0:N, 0:N]),
               ["y32", "ident"], [ptkey])
            OP("act", lambda e, fc=fc, pt=pt: e.copy(out=ynT[:, fc, 0:N], in_=pt[:, 0:N]), [ptkey], ["ynT"])

        if STAGE == "rms":
            return
        pump(PUMP)
        PP[0] = 3
        for pn in range(4):
            pump(PUMP)
            vbf, wkey = load_w(w_in, 8, S_DT + pn * 512, 512)
            for cc in range(4):
                c8 = (pn % 2) * 4 + cc
                ps, pskey = PB[cc % 2], "pb%d" % (cc % 2)
                proj_fm(vbf, wkey, cc, ps, pskey)
                if pn < 2:
                    OP("act", lambda e, c8=c8, ps=ps: e.copy(out=aT[:, c8, 0:N], in_=ps[:, 0:N]), [pskey], ["aT"])
                    continue
                fm, fkey = fm32[1], "fm32_1"
                OP("act", lambda e, ps=ps: e.activation(out=fm[:, 0:N], in_=ps[:, 0:N], func=AF.Sigmoid), [pskey], [fkey])
                OP("dve", lambda e, c8=c8: e.tensor_tensor(out=fm[:, 0:N], in0=fm[:, 0:N], in1=aT[:, c8, 0:N], op=ALU.mult), [fkey, "aT"], [fkey])
                xhj, xkey = xh[cc % 2], "xh%d" % (cc % 2)
                if not samp:
                    OP("dve", lambda e, c8=c8, xhj=xhj: e.tensor_copy(out=xhj[:, 0:30], in_=hist_c[:, c8, :]), ["hist_c"], [xkey])
                    OP("dve", lambda e, xhj=xhj: e.tensor_copy(out=xhj[:, 30:30 + N], in_=fm[:, 0:N]), [fkey], [xkey])
                    OP("dve", lambda e, c8=c8, xhj=xhj: e.tensor_copy(out=hist_c[:, c8, :], in_=xhj[:, N:N + 30]), [xkey], ["hist_c"])
                    if last:
                        OP("dve", lambda e, c8=c8: e.tensor_copy(out=tail_c[:, c8, 0:30], in_=fm[:, N - 30:N]), [fkey], ["tail_c"])
                    rhs_k = lambda k, xhj=xhj: xhj[:, k:k + N]
                else:
                    gv = R[:, 0:496].rearrange("p (b k) -> p b k", k=31)
                    stg = R[:, 512:992].rearrange("p (b k) -> p b k", k=30)
                    S.dma(lambda e, c8=c8, stg=stg: e.dma_start(out=stg, in_=st_cconvT[:, c8, :, :]), writes=["R"])
                    gvb = xdt[:, 0:496].rearrange("p (b k) -> p b k", k=31)
                    OP("dve", lambda e, gvb=gvb, stg=stg: e.tensor_copy(out=gvb[:, :, 0:30], in_=stg), ["R"], ["xdt"])
                    OP("dve", lambda e, gvb=gvb: e.tensor_copy(out=gvb[:, :, 30], in_=fm[:, 0:N]), [fkey], ["xdt"])
                    OP("dve", lambda e, c8=c8: e.tensor_copy(out=tail_c[:, c8, 0:16], in_=fm[:, 0:N]), [fkey], ["tail_c"])
                    rhs_k = lambda k, gvb=gvb: gvb[:, :, k]
                    xkey = "xdt"
                OP("dve", lambda e, c8=c8, rhs_k=rhs_k: e.tensor_scalar(out=cvT[:, c8, 0:N], in0=rhs_k(0), scalar1=ccw_t[:, c8, 0:1],
                                                                       scalar2=ccb_t[:, c8:c8 + 1], op0=ALU.mult, op1=ALU.add),
                   [xkey, "ccw_t", "ccb_t"], ["cvT"])
                for k in range(1, 31):
                    OP("dve", lambda e, k=k, c8=c8, rhs_k=rhs_k: e.scalar_tensor_tensor(out=cvT[:, c8, 0:N], in0=rhs_k(k), scalar=ccw_t[:, c8, k:k + 1],
                                                                                     in1=cvT[:, c8, 0:N], op0=ALU.mult, op1=ALU.add),
                       [xkey, "ccw_t", "cvT"], ["cvT"])
        OP("act", lambda e: e.activation(out=aT[:, :, 0:N], in_=cvT[:, :, 0:N], func=AF.Square), ["cvT"], ["aT"])
        for c8 in range(8):
            OP("pe", lambda e, c8=c8: e.matmul(PB[4][:, 0:N], lhsT=ones[:], rhs=cvT[:, c8, 0:N], start=(c8 == 0), stop=(c8 == 7)), ["ones", "cvT"], ["pb4"])
        for c8 in range(8):
            OP("pe", lambda e, c8=c8: e.matmul(PB[5][:, 0:N], lhsT=ones[:], rhs=aT[:, c8, 0:N], start=(c8 == 0), stop=(c8 == 7)), ["ones", "aT"], ["pb5"])
        mean, var = stat[:, 0, 0:N], stat[:, 1, 0:N]
        OP("dve", lambda e: e.tensor_scalar(out=mean, in0=PB[4][:, 0:N], scalar1=1.0 / 1024, scalar2=None, op0=ALU.mult), ["pb4"], ["stat"])
        OP("dve", lambda e: e.tensor_scalar(out=var, in0=PB[5][:, 0:N], scalar1=1.0 / 1024, scalar2=None, op0=ALU.mult), ["pb5"], ["stat"])
        OP("dve", lambda e: e.tensor_tensor(out=stat[:, 2, 0:N], in0=mean, in1=mean, op=ALU.mult), ["stat"], ["stat"])
        OP("dve", lambda e: e.tensor_tensor(out=var, in0=var, in1=stat[:, 2, 0:N], op=ALU.subtract), ["stat"], ["stat"])
        OP("dve", lambda e: e.tensor_scalar(out=var, in0=var, scalar1=EPS, scalar2=None, op0=ALU.add), ["stat"], ["stat"])
        OP("act", lambda e: e.activation(out=var, in_=var, func=AF.Sqrt), ["stat"], ["stat"])
        OP("dve", lambda e: e.reciprocal(out=var, in_=var), ["stat"], ["stat"])
        OP("dve", lambda e: e.tensor_tensor(out=cvT[:, :, 0:N], in0=cvT[:, :, 0:N], in1=bc(mean.unsqueeze(1), [128, 8, N]), op=ALU.subtract),
           ["cvT", "stat"], ["cvT"])
        OP("dve", lambda e: e.tensor_tensor(out=cvT[:, :, 0:N], in0=cvT[:, :, 0:N], in1=bc(var.unsqueeze(1), [128, 8, N]), op=ALU.mult),
           ["cvT", "stat"], ["cvT"])
        for c8 in range(8):
            OP("act", lambda e, c8=c8: e.activation(out=cactT[:, c8, 0:N], in_=cvT[:, c8, 0:N], func=AF.Silu, bias=clb_t[:, c8:c8 + 1],
                                                    scale=clg_t[:, c8:c8 + 1]), ["cvT", "clb_t", "clg_t"], ["cactT"])

        if STAGE == "conf":
            return
        PP[0] = 2
        for pn in range(2):
            pump(PUMP)
            vbf, wkey = load_w(w_in, 8, S_CONF + pn * 512, 512)
            for cc in range(4):
                ps, pskey = PB[cc % 2], "pb%d" % (cc % 2)
                proj_fm(vbf, wkey, cc, ps, pskey)
                OP("act", lambda e, c=pn * 4 + cc, ps=ps: e.copy(out=qT[:, c, 0:N], in_=ps[:, 0:N]), [pskey], ["qT"])
                if samp:
                    OP("act", lambda e, c=pn * 4 + cc, ps=ps: e.copy(out=cvT[:, c, 0:N], in_=ps[:, 0:N]), [pskey], ["cvT"])
        if not samp:
            for h in range(4):
                pump(3)
                for c2 in range(2):
                    OP("pe", lambda e, h=h, c2=c2: e.matmul(PB[2][:, 0:256], lhsT=qT[:, 2 * h + c2, :], rhs=KT[:, 2 * h + c2, :],
                                                            start=(c2 == 0), stop=(c2 == 1)), ["qT", "KT"], ["pb2"])
                OP("dve", lambda e: e.tensor_reduce(out=col[:, 4:5], in_=PB[2][:, 0:256], axis=AX.X, op=ALU.max), ["pb2"], ["col"])
                OP("dve", lambda e: e.tensor_scalar(out=col[:, 4:5], in0=col[:, 4:5], scalar1=-1.0 / 16, scalar2=None, op0=ALU.mult), ["col"], ["col"])
                OP("act", lambda e: e.activation(out=att[:, 0, :], in_=PB[2][:, 0:256], func=AF.Exp, bias=col[:, 4:5], scale=1.0 / 16,
                                                 accum_out=col[:, 5:6]), ["pb2", "col"], ["att", "col"])
                OP("dve", lambda e: e.reciprocal(out=col[:, 5:6], in_=col[:, 5:6]), ["col"], ["col"])
                OP("dve", lambda e: e.tensor_scalar(out=att[:, 1, :], in0=att[:, 0, :], scalar1=col[:, 5:6], scalar2=None, op0=ALU.mult), ["att", "col"], ["att"])
                for mc in range(2):
                    OP("pe", lambda e, mc=mc: e.transpose(out=PB[3][:, mc * 128:(mc + 1) * 128], in_=att[:, 1, mc * 128:(mc + 1) * 128], identity=ident[:]),
                       ["att", "ident"], ["pb3"])
                OP("act", lambda e: e.copy(out=PnT[:, :, :].rearrange("p a b -> p (a b)"), in_=PB[3][:, 0:256]), ["pb3"], ["PnT"])
                for c2 in range(2):
                    for mc in range(2):
                        OP("pe", lambda e, h=h, c2=c2, mc=mc: e.matmul(PB[4][:, 0:128], lhsT=Vb[:, mc, (2 * h + c2) * 128:(2 * h + c2 + 1) * 128],
                                                                      rhs=PnT[:, mc, :], start=(mc == 0), stop=(mc == 1)), ["Vb", "PnT"], ["pb4"])
                    OP("act", lambda e, h=h, c2=c2: e.copy(out=oT[:, 2 * h + c2, :], in_=PB[4][:, 0:128]), ["pb4"], ["oT"])
        else:
            for c in range(8):
                OP("pe", lambda e, c=c: e.transpose(out=PB[2][0:NS, c * 128:(c + 1) * 128] if c < 4 else PB[3][0:NS, (c - 4) * 128:(c - 3) * 128],
                                                    in_=cvT[:, c, 0:NS], identity=ident[:]), ["cvT", "ident"], ["pb2", "pb3"])
            OP("act", lambda e: e.copy(out=otok[:, 0:512], in_=PB[2][0:NS, :]), ["pb2"], ["otok"])
            OP("act", lambda e: e.copy(out=otok[:, 512:1024], in_=PB[3][0:NS, :]), ["pb3"], ["otok"])
            S.dma(lambda e: e.dma_start(out=scr_q, in_=otok[:, :]), reads=["otok"], writes=["scr_q"])
            for b in range(NS):
                S.dma(lambda e, b=b: e.dma_start(out=qb[:, :], in_=scr_q[b:b + 1, :].partition_broadcast(128)), reads=["scr_q"], writes=["qb"])
                Kb_, kk_ = (Ks, "Ks") if b % 2 == 0 else (Vs, "Vs")
                S.dma(lambda e, b=b, Kb_=Kb_: e.dma_start(out=Kb_[:, :, :], in_=ck[b].rearrange("(mc p) d -> p mc d", p=128)), writes=[kk_])
                OP("dve", lambda e, Kb_=Kb_: e.tensor_tensor(out=Kb_[:, :, :], in0=Kb_[:, :, :], in1=bc(qb[:, :].unsqueeze(1), [128, 2, 1024]), op=ALU.mult),
                   [kk_, "qb"], [kk_])
                OP("dve", lambda e, b=b, Kb_=Kb_: e.tensor_reduce(out=Sall[:, b, :], in_=Kb_[:, :, :].rearrange("p mc (h d) -> p (mc h) d", d=256), axis=AX.X,
                                                                  op=ALU.add), [kk_], ["Sall"])
            OP("pe", lambda e: e.transpose(out=PB[2][:, 0:128], in_=Sall[:, :, :].rearrange("p b c -> p (b c)"), identity=ident[:]), ["Sall", "ident"], ["pb2"])
            OP("dve", lambda e: e.tensor_reduce(out=col[:, 6:7], in_=PB[2][:, 0:128], axis=AX.X, op=ALU.max), ["pb2"], ["col"])
            OP("pe", lambda e: e.transpose(out=PB[3][0:1, 0:128], in_=col[:, 6:7], identity=ident[:]), ["col", "ident"], ["pb3"])
            OP("dve", lambda e: e.tensor_reduce(out=stat[0:1, 3, 0:NS], in_=PB[3][0:1, 0:128].rearrange("p (b c) -> p b c", c=8), axis=AX.X, op=ALU.max),
               ["pb3"], ["stat"])
            OP("pe", lambda e: e.matmul(PB[2][:, 256:256 + NS], lhsT=ones[0:1, :], rhs=stat[0:1, 3, 0:NS], start=True, stop=True), ["ones", "stat"], ["pb2"])
            OP("dve", lambda e: e.tensor_tensor(out=Eall[:, :, :], in0=Sall[:, :, :], in1=bc(PB[2][:, 256:256 + NS].unsqueeze(2), [128, NS, 8]),
                                                op=ALU.subtract), ["Sall", "pb2"], ["Eall"])
            OP("act", lambda e: e.activation(out=Eall[:, :, :], in_=Eall[:, :, :], func=AF.Exp, scale=1.0 / 16), ["Eall"], ["Eall"])
            OP("pe", lambda e: e.matmul(PB[3][:, 0:128], lhsT=ones[:], rhs=Eall[:, :, :].rearrange("p b c -> p (b c)"), start=True, stop=True),
               ["ones", "Eall"], ["pb3"])
            den = Sall[:, :, 0:4]
            pd = PB[3][:, 0:128].rearrange("p (b mc h) -> p b mc h", mc=2, h=4)
            OP("dve", lambda e: e.tensor_copy(out=den, in_=pd[:, :, 0, :]), ["pb3"], ["Sall"])
            OP("dve", lambda e: e.tensor_tensor(out=den, in0=den, in1=pd[:, :, 1, :], op=ALU.add), ["pb3", "Sall"], ["Sall"])
            OP("dve", lambda e: e.reciprocal(out=den, in_=den), ["Sall"], ["Sall"])
            OP("dve", lambda e: e.tensor_tensor(out=Eall[:, :, :].rearrange("p b (mc h) -> p b mc h", h=4),
                                                in0=Eall[:, :, :].rearrange("p b (mc h) -> p b mc h", h=4),
                                                in1=bc(den.unsqueeze(2), [128, NS, 2, 4]), op=ALU.mult), ["Eall", "Sall"], ["Eall"])
            for b in range(NS):
                Vb_, vk_ = (Ks, "Ks") if b % 2 == 0 else (Vs, "Vs")
                S.dma(lambda e, b=b, Vb_=Vb_: e.dma_start(out=Vb_[:, :, :], in_=cv[b].rearrange("(mc p) d -> p mc d", p=128)), writes=[vk_])
                for hf in range(2):
                    for mc in range(2):
                        OP("pe", lambda e, b=b, hf=hf, mc=mc, Vb_=Vb_: e.matmul(PB[4 + hf][0:4, :], lhsT=Eall[:, b, mc * 4:(mc + 1) * 4],
                                                                      rhs=Vb_[:, mc, hf * 512:(hf + 1) * 512], start=(mc == 0), stop=(mc == 1)),
                           ["Eall", vk_], ["pb%d" % (4 + hf)])
                    OP("act", lambda e, hf=hf: e.copy(out=o4[:, hf * 512:(hf + 1) * 512], in_=PB[4 + hf][0:4, :]), ["pb%d" % (4 + hf)], ["o4"])
                S.dma(lambda e, b=b: e.dma_start(out=scr_o[b], in_=o4[:, :]), reads=["o4"], writes=["scr_o"])
            for h in range(4):
                S.dma(lambda e, h=h: e.dma_start(out=otok[:, h * 256:(h + 1) * 256], in_=scr_o[:, h, h * 256:(h + 1) * 256]), reads=["scr_o"], writes=["otok"])
            for c in range(8):
                OP("pe", lambda e, c=c: e.transpose(out=PB[4][:, 0:NS], in_=otok[:, c * 128:(c + 1) * 128], identity=ident[0:NS, 0:NS]),
                   ["otok", "ident"], ["pb4"])
                OP("act", lambda e, c=c: e.copy(out=oT[:, c, 0:NS], in_=PB[4][:, 0:NS]), ["pb4"], ["oT"])

        if STAGE == "attn":
            return
        pump(PUMP)
        for br in range(3):
            for pn in range(2):
                pump(PUMP)
                vbf, wkey = load_w(w_in, 8, S_MEMQ + br * 1024 + pn * 512, 512)
                for cc in range(4):
                    ps, pskey = PB[cc % 2], "pb%d" % (cc % 2)
                    proj_fm(vbf, wkey, cc, ps, pskey)
                    OP("act", lambda e, c=pn * 4 + cc, ps=ps: e.activation(out=gbr[:, c, 0:N], in_=ps[:, 0:N], func=AF.Sigmoid), [pskey], ["gbr"])
            wd, kcn, src, skey = ((wso, 16, ynT, "ynT"), (wco, 8, cactT, "cactT"), (wmo, 8, oT, "oT"))[br]
            pw = 4096 // kcn
            for pn in range(1024 // pw):
                pump(PUMP)
                vbf, wkey = load_w(wd, kcn, pn * pw, pw)
                for cc in range(pw // 128):
                    dch = pn * (pw // 128) + cc
                    ps, pskey = PB[2 + cc % 2], "pb%d" % (2 + cc % 2)
                    for kc in range(kcn):
                        OP("pe", lambda e, kc=kc, cc=cc, vbf=vbf, ps=ps, src=src, kcn=kcn: e.matmul(ps[:, 0:N], lhsT=vbf[:, kc, cc * 128:(cc + 1) * 128],
                                                                                          rhs=src[:, kc, 0:N], start=(kc == 0), stop=(kc == kcn - 1)),
                           [wkey, skey], [pskey])
                    if br == 0:
                        OP("dve", lambda e, dch=dch, ps=ps: e.tensor_tensor(out=mrg[:, dch, 0:N], in0=ps[:, 0:N], in1=gbr[:, dch, 0:N], op=ALU.mult),
                           [pskey, "gbr"], ["mrg"])
                    else:
                        OP("dve", lambda e, dch=dch, ps=ps: e.tensor_tensor(out=gbr[:, dch, 0:N], in0=ps[:, 0:N], in1=gbr[:, dch, 0:N], op=ALU.mult),
                           [pskey, "gbr"], ["gbr"])
                        OP("pool", lambda e, dch=dch: e.tensor_tensor(out=mrg[:, dch, 0:N], in0=mrg[:, dch, 0:N], in1=gbr[:, dch, 0:N], op=ALU.add),
                           ["mrg", "gbr"], ["mrg"])
        OP("act", lambda e: e.copy(out=mrgb[:, :, 0:N], in_=mrg[:, :, 0:N]), ["mrg"], ["mrgb"])
        for pn in range(2):
            pump(PUMP)
            vbf, wkey = load_w(wout, 8, pn * 512, 512)
            ps, pskey = PB[pn], "pb%d" % pn
            for kc in range(8):
                OP("pe", lambda e, kc=kc, vbf=vbf, ps=ps: e.matmul(ps[0:N, :], lhsT=mrgb[:, kc, 0:N], rhs=vbf[:, kc, :], start=(kc == 0), stop=(kc == 7)),
                   [wkey, "mrgb"], [pskey])
            OP("dve", lambda e, pn=pn, ps=ps: e.scalar_tensor_tensor(out=h1[0:N, pn * 512:(pn + 1) * 512], in0=xres[0:N, pn * 512:(pn + 1) * 512],
                                                                     scalar=ALPHA, in1=ps[0:N, :], op0=ALU.mult, op1=ALU.add), [pskey, "xres"], ["h1"])
        ln_tok(h1, x1, 0, N)
        for c in range(8):
            pt, ptkey = PB[4 + c % 2], "pb%d" % (4 + c % 2)
            OP("pe", lambda e, c=c, pt=pt: e.transpose(out=pt[:, 0:N], in_=x1[0:N, c * 128:(c + 1) * 128], identity=ident[0:N, 0:N]), ["x1", "ident"], [ptkey])
            OP("act", lambda e, c=c, pt=pt: e.copy(out=x1T[:, c, 0:N], in_=pt[:, 0:N]), [ptkey], ["x1T"])

        if STAGE == "ln1":
            return
        for pn in range(4):
            vbf, wkey = load_w(wq, 8, pn * 512, 512)
            for cc in range(4):
                pump(2)
                ps, pskey = PB[cc % 2], "pb%d" % (cc % 2)
                for kc in range(8):
                    OP("pe", lambda e, kc=kc, cc=cc, vbf=vbf, ps=ps: e.matmul(ps[:, 0:N], lhsT=vbf[:, kc, cc * 128:(cc + 1) * 128], rhs=x1T[:, kc, 0:N],
                                                                             start=(kc == 0), stop=(kc == 7)), [wkey, "x1T"], [pskey])
                OP("act", lambda e, c=pn * 4 + cc, ps=ps: e.copy(out=qpT[:, c, 0:N], in_=ps[:, 0:N]), [pskey], ["qpT"])
        for c in range(16):
            OP("pe", lambda e, c=c: e.matmul(PB[2 + c // 4][0:N, (c % 4) * 128:(c % 4 + 1) * 128], lhsT=qpT[:, c, 0:N], rhs=skTb[:, c, :],
                                             start=True, stop=True), ["qpT", "skTb"], ["pb%d" % (2 + c // 4)])
        for q in range(4):
            OP("act", lambda e, q=q: e.copy(out=R[0:N, q * 512:(q + 1) * 512], in_=PB[2 + q][0:N, :]), ["pb%d" % (2 + q)], ["R"])
        sc_, scw_ = R[0:N, 0:2048], R[0:N, 2048:4096]
        for c in range(16):
            cs = slice(c * 128, (c + 1) * 128)
            OP("dve", lambda e, c=c, cs=cs: e.max(out=top[0:N, c, 0:8], in_=sc_[:, cs]), ["R"], ["top"])
            OP("dve", lambda e, c=c, cs=cs: e.match_replace(out=scw_[:, cs], in_to_replace=top[0:N, c, 0:8], in_values=sc_[:, cs], imm_value=-1e30),
               ["R", "top"], ["R2"])
            OP("dve", lambda e, c=c, cs=cs: e.max(out=top[0:N, c, 8:16], in_=scw_[:, cs]), ["R2"], ["top"])
            OP("dve", lambda e, c=c, cs=cs: e.max_index(out=idxu[0:N, c, 0:8], in_max=top[0:N, c, 0:8], in_values=sc_[:, cs]), ["R", "top"], ["idxu"])
            OP("dve", lambda e, c=c, cs=cs: e.max_index(out=idxu[0:N, c, 8:16], in_max=top[0:N, c, 8:16], in_values=sc_[:, cs]), ["R", "top"], ["idxu"])
        OP("dve", lambda e: e.tensor_copy(out=idxf[0:N, :, :], in_=idxu[0:N, :, :]), ["idxu"], ["idxf"])
        topv = top[0:N, :, :].rearrange("p (h two) k -> p h two k", two=2)
        idxv = idxf[0:N, :, :].rearrange("p (h two) k -> p h two k", two=2)
        cand = R[0:N, 0:2048].rearrange("p (h a b) -> p h a b", h=8, a=16)
        candw = R[0:N, 2048:4096]
        OP("dve", lambda e: e.tensor_tensor(out=cand, in0=bc(topv[:, :, 0, :].unsqueeze(3), [N, 8, 16, 16]), in1=bc(topv[:, :, 1, :].unsqueeze(2), [N, 8, 16, 16]),
                                            op=ALU.add), ["top"], ["R"])
        for h in range(8):
            cs = slice(h * 256, (h + 1) * 256)
            OP("dve", lambda e, h=h, cs=cs: e.max(out=best[0:N, h, 0:8], in_=sc_[:, cs]), ["R"], ["best"])
            OP("dve", lambda e, h=h, cs=cs: e.match_replace(out=candw[:, cs], in_to_replace=best[0:N, h, 0:8], in_values=sc_[:, cs], imm_value=-1e30),
               ["R", "best"], ["R2"])
            OP("dve", lambda e, h=h, cs=cs: e.max(out=best[0:N, h, 8:16], in_=candw[:, cs]), ["R2"], ["best"])
            OP("dve", lambda e, h=h, cs=cs: e.max_index(out=pos[0:N, h, 0:8], in_max=best[0:N, h, 0:8], in_values=sc_[:, cs]), ["R", "best"], ["pos"])
            OP("dve", lambda e, h=h, cs=cs: e.max_index(out=pos[0:N, h, 8:16], in_max=best[0:N, h, 8:16], in_values=sc_[:, cs]), ["R", "best"], ["pos"])
        posf = pos[0:N, :, :].rearrange("p h k -> p (h k)")
        OP("dve", lambda e: e.tensor_single_scalar(out=pab[0:N, 0, :], in_=posf, scalar=4, op=ALU.logical_shift_right), ["pos"], ["pab"])
        OP("dve", lambda e: e.tensor_single_scalar(out=pab[0:N, 1, :], in_=posf, scalar=15, op=ALU.bitwise_and), ["pos"], ["pab"])
        OP("dve", lambda e: e.tensor_copy(out=pabf[0:N, :, :], in_=pab[0:N, :, :]), ["pab"], ["pabf"])
        for two in range(2):
            m4 = selw[0:N, :].rearrange("p (h k a) -> p h k a", h=8, k=16)
            OP("dve", lambda e, two=two, m4=m4: e.tensor_tensor(out=m4, in0=bc(pabf[0:N, two, :].rearrange("p (h k) -> p h k", k=16).unsqueeze(3), [N, 8, 16, 16]),
                                                               in1=bc(iota16[0:N, :].unsqueeze(1).unsqueeze(1), [N, 8, 16, 16]), op=ALU.is_equal),
               ["pabf", "iota16"], ["selw"])
            OP("dve", lambda e, two=two, m4=m4: e.tensor_tensor(out=m4, in0=m4, in1=bc(idxv[:, :, two, :].unsqueeze(2), [N, 8, 16, 16]), op=ALU.mult),
               ["selw", "idxf"], ["selw"])
            OP("dve", lambda e, two=two: e.tensor_reduce(out=ids[0:N, two, :], in_=selw[0:N, :].rearrange("p (hk a) -> p hk a", a=16), axis=AX.X, op=ALU.add),
               ["selw"], ["ids"])
        OP("dve", lambda e: e.scalar_tensor_tensor(out=ids[0:N, 0, :], in0=ids[0:N, 0, :], scalar=128.0, in1=ids[0:N, 1, :], op0=ALU.mult, op1=ALU.add),
           ["ids"], ["ids"])
        gwt = pabf[0:N, 0, :]
        g3 = gwt.rearrange("p (h k) -> p h k", k=16)
        OP("dve", lambda e: e.tensor_tensor(out=g3, in0=best[0:N, :, :], in1=bc(best[0:N, :, 0:1], [N, 8, 16]), op=ALU.subtract), ["best", "pabf"], ["pabf"])
        OP("act", lambda e: e.activation(out=gwt, in_=gwt, func=AF.Exp), ["pabf"], ["pabf"])
        OP("dve", lambda e: e.tensor_reduce(out=col[0:N, 8:16], in_=g3, axis=AX.X, op=ALU.add), ["pabf"], ["col"])
        OP("dve", lambda e: e.reciprocal(out=col[0:N, 8:16], in_=col[0:N, 8:16]), ["col"], ["col"])
        OP("dve", lambda e: e.tensor_tensor(out=g3, in0=g3, in1=bc(col[0:N, 8:16].unsqueeze(2), [N, 8, 16]), op=ALU.mult), ["pabf", "col"], ["pabf"])
        pump(10 ** 6)
        OP("act", lambda e: e.copy(out=x1p[0:N, :], in_=x1[0:N, :]), ["x1"], ["x1p"])
        OP("dve", lambda e: e.tensor_copy(out=idi[0:N, :], in_=ids[0:N, 0, :]), ["ids"], ["idi"])
        OP("act", lambda e: e.copy(out=gw[0:N, :], in_=gwt), ["pabf"], ["gw"])
        if STAGE == "topk":
            return
        return

    def gen_peer(blk):
        samp = blk == 16
        N = NS if samp else 128
        t0 = 0 if samp else blk * 128
        GRP = 16
        ring = [0]
        for g0 in range(0, 128, GRP):
            gi_ = g0 // GRP
            dk, ck = "dots%d" % (gi_ % 2), "coef%d" % (gi_ % 2)
            for s_ in range(g0, g0 + GRP):
                ri = ring[0] % NG
                ring[0] += 1
                Gs, gkey = G[ri], "G%d" % ri
                S.dma(lambda e, s_=s_, Gs=Gs: e.indirect_dma_start(out=Gs[0:N, :], out_offset=None, in_=pu16[:, :],
                                                                   in_offset=bass.IndirectOffsetOnAxis(ap=idi[0:N, s_:s_ + 1], axis=0)),
                      reads=["idi"] + TABKEYS, writes=[gkey], e="pool")
                OP("dve", lambda e, s_=s_, Gs=Gs: e.scalar_tensor_tensor(out=Gs[0:N, :], in0=Gs[0:N, :], scalar=1.0, in1=x1p[0:N, :], op0=ALU.mult,
                                                                         op1=ALU.mult, accum_out=dots[0:N, s_:s_ + 1]), [gkey, "x1p"], [gkey, dk])
                yield
            OP("act", lambda e, g0=g0: e.activation(out=coef[0:N, g0:g0 + GRP], in_=dots[0:N, g0:g0 + GRP], func=AF.Gelu), [dk], [ck])
            OP("dve", lambda e, g0=g0: e.tensor_tensor(out=coef[0:N, g0:g0 + GRP], in0=coef[0:N, g0:g0 + GRP], in1=gw[0:N, g0:g0 + GRP], op=ALU.mult),
               [ck, "gw"], [ck])
            for s_ in range(g0, g0 + GRP):
                ri = ring[0] % NG
                ring[0] += 1
                Gs, gkey = G[ri], "G%d" % ri
                dv, dvkey = dgv[s_ % 2], "dgv%d" % (s_ % 2)
                S.dma(lambda e, s_=s_, Gs=Gs: e.indirect_dma_start(out=Gs[0:N, :], out_offset=None, in_=pv16[:, :],
                                                                   in_offset=bass.IndirectOffsetOnAxis(ap=idi[0:N, s_:s_ + 1], axis=0)),
                      reads=["idi"] + TABKEYS, writes=[gkey], e="pool")
                OP("act", lambda e, s_=s_, dv=dv: e.activation(out=dv[0:N, 0:N], in_=ident[0:N, 0:N], func=AF.Copy, scale=coef[0:N, s_:s_ + 1]),
                   ["ident", ck], [dvkey])
                for hf in range(2):
                    OP("pe", lambda e, s_=s_, hf=hf, dv=dv, Gs=Gs: e.matmul(PB[6 + hf][0:N, :], lhsT=dv[0:N, 0:N], rhs=Gs[0:N, hf * 512:(hf + 1) * 512],
                                                                           start=(s_ == 0), stop=(s_ == 127)), [dvkey, gkey], ["pb%d" % (6 + hf)])
                yield
        for hf in range(2):
            OP("dve", lambda e, hf=hf: e.scalar_tensor_tensor(out=x1p[0:N, hf * 512:(hf + 1) * 512], in0=x1p[0:N, hf * 512:(hf + 1) * 512], scalar=ALPHA,
                                                              in1=PB[6 + hf][0:N, :], op0=ALU.mult, op1=ALU.add), ["pb%d" % (6 + hf), "x1p"], ["x1p"])
        ln_tok(x1p, x1p, 2, N, scratch=G[0], skey="G0", key="x1p")
        S.dma(lambda e: e.dma_start(out=(ys if samp else yp[t0:t0 + 128, :]), in_=x1p[0:N, :]), reads=["x1p"], writes=["yout"])
        yield

    pend = [None]

    def pump(k):
        for _ in range(k):
            if pend[0] is None:
                return
            try:
                next(pend[0])
            except StopIteration:
                pend[0] = None
                return

    def emit_tails():
        for (tl, tkey, nch, ncol, o_p, o_s, st_in, W) in ((tail_s, "tail_s", 24, 3, o_sconv_p, o_sconv_s, st_sconv, 3), (tail_c, "tail_c", 8, 30, o_cconv_p, o_cconv_s, st_cconv, 30)):
            pass

    blist = list(range(17)) if stop == "all" else [int(x) for x in str(stop).split("+") if x != "0x"]
    if str(stop).isdigit():
        blist = list(range(int(stop)))
    for blk in blist:
        if blk == 15:
            pass
        emit_block(blk)
        if STAGE == '':
            pend[0] = gen_peer(blk)
        if blk == 15:
            for (tl, tkey, nch, ncol, o_p) in ((tail_s, "tail_s", 24, 3, o_sconv_p), (tail_c, "tail_c", 8, 30, o_cconv_p)):
                for j in range(nch):
                    OP("pe", lambda e, tl=tl, j=j, ncol=ncol: e.transpose(out=PB[j % 2][0:ncol, 0:128], in_=tl[:, j, 0:ncol], identity=ident[:]),
                       [tkey, "ident"], ["pb%d" % (j % 2)])
                    OP("act", lambda e, j=j, ncol=ncol: e.copy(out=fm32[j % 2][0:ncol, 0:128], in_=PB[j % 2][0:ncol, 0:128]), ["pb%d" % (j % 2)], ["fm32_%d" % (j % 2)])
                    S.dma(lambda e, j=j, ncol=ncol, o_p=o_p: e.dma_start(out=o_p[:, j * 128:(j + 1) * 128], in_=fm32[j % 2][0:ncol, 0:128]),
                          reads=["fm32_%d" % (j % 2)], writes=["otail"])
        if blk == 16:
            for (tl, tkey, nch, W, o_s, st_in) in ((tail_s, "tail_s", 24, 3, o_sconv_s, st_sconv), (tail_c, "tail_c", 8, 30, o_cconv_s, st_cconv)):
                S.dma(lambda e, o_s=o_s, st_in=st_in, W=W: e.dma_start(out=o_s[:, 0:W - 1, :], in_=st_in[:, 1:W, :]), writes=["otail_s"])
                for j in range(nch):
                    OP("pe", lambda e, tl=tl, j=j: e.transpose(out=PB[j % 2][0:NS, 0:128], in_=tl[:, j, 0:NS], identity=ident[:]),
                       [tkey, "ident"], ["pb%d" % (j % 2)])
                    OP("act", lambda e, j=j: e.copy(out=fm32[j % 2][0:NS, 0:128], in_=PB[j % 2][0:NS, 0:128]), ["pb%d" % (j % 2)], ["fm32_%d" % (j % 2)])
                    S.dma(lambda e, j=j, o_s=o_s, W=W: e.dma_start(out=o_s[:, W - 1, j * 128:(j + 1) * 128], in_=fm32[j % 2][0:NS, 0:128]),
                          reads=["fm32_%d" % (j % 2)], writes=["otail_s2"])
    pump(10 ** 6)
    for nm in dbg:
        t_, key = {"y32": (y32, "y32"), "x1": (x1, "x1"), "xtok": (xtok, "xtok"), "h1": (h1, "h1"), "mrg": (mrg, "mrg"), "hT": (hT, "hT"),
                   "cvT": (cvT, "cvT"), "oT": (oT, "oT"), "ynT": (ynT, "ynT"), "dots": (dots, "dots"), "ids": (ids, "ids"), "gw": (gw, "gw"),
                   "coef": (coef, "coef"), "dtt": (dtt, "dtt"), "cactT": (cactT, "cactT"), "qsc": (qsc, "qsc"), "hp2": (hp2, "hp2"), "hp": (hp, "hp"), "sQ": (sQ, "sQ"), "sB": (sB, "sB"), "sC": (sC, "sC")}[nm]
        shp = list(t_[:].shape)
        dd = dscr("dbg_" + nm, shp, F32)
        if t_[:].dtype != F32:
            n_ = int(np.prod(shp[1:]))
            t32 = R[:, 0:n_].rearrange("p (a b) -> p a b", b=shp[-1]) if len(shp) == 3 else R[:, 0:n_]
            OP("dve", lambda e, t_=t_, t32=t32: e.tensor_copy(out=t32, in_=t_[:]), [key], ["R"])
            S.dma(lambda e, dd=dd, t32=t32: e.dma_start(out=dd, in_=t32), reads=["R"])
        else:
            S.dma(lambda e, dd=dd, t_=t_: e.dma_start(out=dd, in_=t_[:]), reads=[key])
    S.emit()
    st.close()
    return nc


def _fm(v, nch):
    v = np.asarray(v)
    return np.ascontiguousarray(np.moveaxis(v.reshape((nch, 128) + v.shape[1:]), 0, 1))


def prep_inputs(inp, c):
    f = lambda a: np.ascontiguousarray(np.asarray(a, dtype=np.float32))
    L = 0
    sl = slice(NS * c, NS * (c + 1))
    m = {}
    m["xp"] = f(inp["x_prompt"][c]); m["xpT"] = f(inp["x_prompt"][c].T)
    m["xs"] = f(inp["x_sample"][sl, 0]); m["xsT"] = f(inp["x_sample"][sl, 0].T)
    m["w_in"] = f(inp["w_in"][L])
    m["scw"] = _fm(f(inp["ssd_conv_w"][L].T), 24); m["scb"] = _fm(f(inp["ssd_conv_b"][L]), 24)
    m["dtb"] = f(inp["ssd_dt_bias"][L][None]); m["alog"] = f(inp["ssd_a_log"][L][None]); m["dsk"] = f(inp["ssd_d"][L][None])
    m["nw"] = f(inp["ssd_norm_w"][L][None]); m["wso"] = f(inp["ssd_w_out"][L])
    m["ccw"] = _fm(f(inp["conf_conv_w"][L].T), 8); m["ccb"] = _fm(f(inp["conf_conv_b"][L]), 8)
    m["clg"] = _fm(f(inp["conf_ln_g"][L]), 8); m["clb"] = _fm(f(inp["conf_ln_b"][L]), 8)
    m["wco"] = f(inp["conf_w_out"][L]); m["wk"] = f(inp["mem_w_k"][L]); m["wv"] = f(inp["mem_w_v"][L])
    m["wmo"] = f(inp["mem_w_o"][L]); m["wout"] = f(inp["w_out"][L])
    for k in ("ln1_g", "ln1_b", "ln2_g", "ln2_b"):
        m[k.replace("_", "")] = f(inp[k][L][None])
    m["wq"] = f(inp["peer_w_q"][L])
    sk = np.asarray(inp["peer_sub_keys"][L]).reshape(16, 128, 128)
    m["skT"] = f(np.transpose(sk, (2, 0, 1)))
    m["pu"] = f(inp["peer_u"][L]); m["pv"] = f(inp["peer_v"][L])
    m["mempT"] = f(inp["mem_prompt"][c].T)
    m["st_ssd"] = f(np.asarray(inp["state_ssd"][L][sl]).reshape(NS, 128, 2048))
    sc = np.asarray(inp["state_ssd_conv"][L][sl])
    m["st_sconv"] = f(sc)
    m["st_sconvT"] = f(np.transpose(sc.reshape(NS, 3, 24, 128), (3, 2, 0, 1)))
    cc = np.asarray(inp["state_conf_conv"][L][sl])
    m["st_cconv"] = f(cc)
    m["st_cconvT"] = f(np.transpose(cc.reshape(NS, 30, 8, 128), (3, 2, 0, 1)))
    m["ck"] = f(np.asarray(inp["cache_mem_k"][L][sl]).reshape(NS, 256, D))
    m["cv"] = f(np.asarray(inp["cache_mem_v"][L][sl]).reshape(NS, 256, D))
    return m


_NC_CACHE = {}


def kernel(**inputs):
    if "nc" not in _NC_CACHE:
        _NC_CACHE["nc"] = build(stop="all")
    nc = _NC_CACHE["nc"]
    in_maps = [prep_inputs(inputs, c) for c in range(8)]
    res = run_bass_kernel_spmd(nc, in_maps, core_ids=list(range(8)))
    r = res.results
    st = lambda k: np.stack([np.asarray(r[c][k], dtype=np.float32) for c in range(8)])
    y_p = st("yp")
    y_s = st("ys").reshape(128, 1, D)
    ssd_p = st("o_ssd_p").reshape(1, 8, 32, 64, 128)
    sconv_p = st("o_sconv_p").reshape(1, 8, 3, 3072)
    cconv_p = st("o_cconv_p").reshape(1, 8, 30, D)
    k_p = st("o_k_p").reshape(1, 8, 256, 4, 256)
    v_p = st("o_v_p").reshape(1, 8, 256, 4, 256)
    ssd_s = st("o_ssd_s").reshape(1, 128, 32, 64, 128)
    sconv_s = st("o_sconv_s").reshape(1, 128, 3, 3072)
    cconv_s = st("o_cconv_s").reshape(1, 128, 30, D)
    return (y_p, y_s, ssd_p, sconv_p, cconv_p, k_p, v_p, ssd_s, sconv_s, cconv_s)
```

```python
import contextlib
import numpy as np
import concourse.bass as bass
import concourse.mybir as mybir
from concourse.bass_utils import run_bass_kernel_spmd

F32 = mybir.dt.float32
BF16 = mybir.dt.bfloat16
I32 = mybir.dt.int32
U32 = mybir.dt.uint32
AF = mybir.ActivationFunctionType
ALU = mybir.AluOpType
AX = mybir.AxisListType

D = 1024
T = 2048
NS = 16
ALPHA = 2.0 ** 0.25
EPS = 1e-5
S_Z, S_XBC, S_DT, S_CONF, S_MEMQ, D_IN = 2048, 5120, 5152, 7200, 8224, 11296


class Sched:
    ENGS = ("pe", "act", "dve", "pool", "sp")

    def __init__(self, nc, n_dma_slots=48, same_engine_sync=True):
        import os
        same_engine_sync = os.environ.get('SAMESYNC', '1') == '1'
        self.nc = nc
        self.q = {e: [] for e in self.ENGS}
        self.cnt = {e: 0 for e in self.ENGS}
        self.waited = {}
        self.same = same_engine_sync
        self.last_write = {}
        self.readers = {}
        self.n_slots = n_dma_slots
        self.slot_uses = [0] * n_dma_slots
        self.slot_rr = 0
        self.sw_rr = 0
        self.n_hw = n_dma_slots - 16

    def _deps(self, reads, writes):
        deps = []
        for b in reads:
            t = self.last_write.get(b)
            if t is not None:
                deps.append(t)
        for b in writes:
            t = self.last_write.get(b)
            if t is not None:
                deps.append(t)
            deps.extend(self.readers.get(b, ()))
        return deps

    def _commit(self, tok, reads, writes):
        for b in reads:
            self.readers.setdefault(b, []).append(tok)
        for b in writes:
            self.last_write[b] = tok
            self.readers[b] = []

    def _emit_waits(self, e, deps):
        need = {}
        for (kind, key, n) in deps:
            if kind == "eng" and key == e and (not self.same or e in ("pe", "sp")):
                continue
            k = (kind, key)
            if n > need.get(k, 0):
                need[k] = n
        for k, n in need.items():
            if self.waited.get((e, k), 0) >= n:
                continue
            self.waited[(e, k)] = n
            self.q[e].append(("wait", k, n))

    alias = {}

    def _x(self, keys):
        out = []
        for k in keys:
            out.extend(self.alias.get(k, [k]))
        return out

    frozen = False

    def op(self, e, fn, reads=(), writes=()):
        if self.frozen:
            return None
        reads, writes = self._x(reads), self._x(writes)
        deps = self._deps(reads, writes)
        self._emit_waits(e, deps)
        self.cnt[e] += 1
        tok = ("eng", e, self.cnt[e])
        self.q[e].append(("op", fn, None))
        self._commit(tok, reads, writes)
        return tok

    def dma(self, fn, reads=(), writes=(), e="sp"):
        if self.frozen:
            return None
        reads, writes = self._x(reads), self._x(writes)
        deps = self._deps(reads, writes)
        if e == "pool":
            s = self.n_hw + self.sw_rr
            self.sw_rr = (self.sw_rr + 1) % (self.n_slots - self.n_hw)
        else:
            s = self.slot_rr
            self.slot_rr = (self.slot_rr + 1) % self.n_hw
        if self.slot_uses[s] > 0:
            deps.append(("dma", s, self.slot_uses[s]))
        self._emit_waits(e, deps)
        self.slot_uses[s] += 1
        tok = ("dma", s, self.slot_uses[s])
        self.q[e].append(("dma", fn, s))
        self._commit(tok, reads, writes)
        return tok

    def emit(self):
        nc = self.nc
        deps = [("dma", s, u) for s, u in enumerate(self.slot_uses) if u > 0]
        self._emit_waits("sp", deps)
        with contextlib.ExitStack() as st:
            esem = {e: st.enter_context(nc.semaphore("s_" + e)) for e in self.ENGS}
            dsem = [st.enter_context(nc.semaphore("d_%d" % i)) for i in range(self.n_slots)]
            block = st.enter_context(nc.Block())

            def run(e, eng):
                for (kind, a, b) in self.q[e]:
                    if kind == "wait":
                        if a[0] == "eng":
                            eng.wait_ge(esem[a[1]], b)
                        else:
                            eng.wait_ge(dsem[a[1]], 16 * b)
                    elif kind == "op":
                        a(eng).then_inc(esem[e], 1)
                    else:
                        a(eng).then_inc(dsem[b], 16)

            @block.tensor
            def _(eng):
                run("pe", eng)

            @block.scalar
            def _(eng):
                run("act", eng)

            @block.vector
            def _(eng):
                run("dve", eng)

            @block.gpsimd
            def _(eng):
                run("pool", eng)

            @block.sync
            def _(eng):
                run("sp", eng)


def build(stop="all", dbg=()):
    nc = bass.Bass("TRN2", target_bir_lowering=False)
    S = Sched(nc)
    st = contextlib.ExitStack()
    dr = {}

    def din(name, shape, dt=F32):
        dr[name] = nc.dram_tensor(name, list(shape), dt, kind="ExternalInput").ap()
        return dr[name]

    def dout(name, shape, dt=F32):
        dr[name] = nc.dram_tensor(name, list(shape), dt, kind="ExternalOutput").ap()
        return dr[name]

    def dscr(name, shape, dt=F32):
        kind = "ExternalOutput" if name.startswith("dbg_") else "Internal"
        dr[name] = nc.dram_tensor(name, list(shape), dt, kind=kind).ap()
        return dr[name]

    def sb(name, shape, dt=F32):
        return st.enter_context(nc.sbuf_tensor(name, list(shape), dt))

    xp = din("xp", [T, D]); xpT = din("xpT", [D, T])
    xs = din("xs", [NS, D]); xsT = din("xsT", [D, NS])
    w_in = din("w_in", [D, D_IN])
    scw = din("scw", [128, 24, 4]); scb = din("scb", [128, 24])
    dtb = din("dtb", [1, 32]); alog = din("alog", [1, 32]); dsk = din("dsk", [1, 32])
    nw = din("nw", [1, 2048]); wso = din("wso", [2048, D])
    ccw = din("ccw", [128, 8, 31]); ccb = din("ccb", [128, 8]); clg = din("clg", [128, 8]); clb = din("clb", [128, 8])
    wco = din("wco", [D, D]); wk = din("wk", [D, D]); wv = din("wv", [D, D]); wmo = din("wmo", [D, D]); wout = din("wout", [D, D])
    ln1g = din("ln1g", [1, D]); ln1b = din("ln1b", [1, D]); ln2g = din("ln2g", [1, D]); ln2b = din("ln2b", [1, D])
    wq = din("wq", [D, 2048]); skT = din("skT", [128, 16, 128])
    pu = din("pu", [16384, D]); pv = din("pv", [16384, D])
    mempT = din("mempT", [D, 256])
    st_ssd = din("st_ssd", [NS, 128, 2048]); st_sconvT = din("st_sconvT", [128, 24, NS, 3]); st_sconv = din("st_sconv", [NS, 3, 3072])
    st_cconvT = din("st_cconvT", [128, 8, NS, 30]); st_cconv = din("st_cconv", [NS, 30, D])
    ck = din("ck", [NS, 256, D]); cv = din("cv", [NS, 256, D])

    yp = dout("yp", [T, D]); ys = dout("ys", [NS, D])
    o_ssd_p = dout("o_ssd_p", [2048, 128]); o_sconv_p = dout("o_sconv_p", [3, 3072]); o_cconv_p = dout("o_cconv_p", [30, D])
    o_k_p = dout("o_k_p", [256, D]); o_v_p = dout("o_v_p", [256, D])
    o_ssd_s = dout("o_ssd_s", [NS, 128, 2048]); o_sconv_s = dout("o_sconv_s", [NS, 3, 3072]); o_cconv_s = dout("o_cconv_s", [NS, 30, D])

    ident = sb("ident", [128, 128]); identb = sb("identb", [128, 128], BF16)
    ones = sb("ones", [128, 128]); tri = sb("tri", [128, 128]); sel_last = sb("sel_last", [128, 128])
    S.op("pool", lambda e: e.memset(ident[:], 0.0), writes=["ident"])
    S.op("pool", lambda e: e.affine_select(out=ident[:], in_=ident[:], pattern=[[-1, 128]], compare_op=ALU.not_equal,
                                           fill=1.0, base=0, channel_multiplier=1), reads=["ident"], writes=["ident"])
    S.op("pool", lambda e: e.tensor_copy(out=identb[:], in_=ident[:]), reads=["ident"], writes=["identb"])
    S.op("pool", lambda e: e.memset(ones[:], 1.0), writes=["ones"])
    S.op("pool", lambda e: e.affine_select(out=tri[:], in_=ones[:], pattern=[[1, 128]], compare_op=ALU.is_ge,
                                           fill=0.0, base=0, channel_multiplier=-1), reads=["ones"], writes=["tri"])
    S.op("pool", lambda e: e.affine_select(out=sel_last[:], in_=ones[:], pattern=[[0, 128]], compare_op=ALU.is_ge,
                                           fill=0.0, base=-127, channel_multiplier=1), reads=["ones"], writes=["sel_last"])

    scw_t = sb("scw_t", [128, 24, 4]); scb_t = sb("scb_t", [128, 24])
    ccw_t = sb("ccw_t", [128, 8, 31]); ccb_t = sb("ccb_t", [128, 8]); clg_t = sb("clg_t", [128, 8]); clb_t = sb("clb_t", [128, 8])
    for t_, d_, k_ in ((scw_t, scw, "scw_t"), (scb_t, scb, "scb_t"), (ccw_t, ccw, "ccw_t"), (ccb_t, ccb, "ccb_t"),
                       (clg_t, clg, "clg_t"), (clb_t, clb, "clb_t")):
        S.dma(lambda e, t_=t_, d_=d_: e.dma_start(out=t_[:], in_=d_), writes=[k_])
    dtb_t = sb("dtb_t", [128, 32]); a_t = sb("a_t", [128, 32]); dsk_t = sb("dsk_t", [128, 32])
    S.dma(lambda e: e.dma_start(out=dtb_t[:], in_=dtb.partition_broadcast(128)), writes=["dtb_t"])
    S.dma(lambda e: e.dma_start(out=a_t[:], in_=alog.partition_broadcast(128)), writes=["a_t"])
    S.dma(lambda e: e.dma_start(out=dsk_t[:], in_=dsk.partition_broadcast(128)), writes=["dsk_t"])
    S.op("act", lambda e: e.activation(out=a_t[:], in_=a_t[:], func=AF.Exp), reads=["a_t"], writes=["a_t"])
    S.op("dve", lambda e: e.tensor_scalar(out=a_t[:], in0=a_t[:], scalar1=-1.0, scalar2=None, op0=ALU.mult), reads=["a_t"], writes=["a_t"])

    PB = [st.enter_context(nc.psum_tensor("pb%d" % i, [128, 512], F32)) for i in range(8)]

    R = sb("R", [128, 4096]); MT = sb("MT", [128, 4096], BF16)
    S.alias = {"sB": ["aT", "cvT"], "sC": ["G4", "G5", "G6", "G7"], "h0_0": ["G8", "G9", "G10", "G11"], "h0_1": ["G8", "G9", "G10", "G11"], "Ks": ["G4", "G5", "G6", "G7"],
               "Vs": ["G8", "G9", "G10", "G11"],
               "MT": ["MT0", "MT1", "MT2", "MT3"], "xT32": ["R"],
               "qb": ["y32"], "selw": ["y32"], "skT32": ["R"], "mT32": ["R"], "kv32": ["xres"], "mTb": ["sz"], "otok": ["R"], "o4": ["R"],
               "x1": ["xtok"], "h1": ["xtok"], "qpT": ["xdt"], "mrgb": ["xdte"], "x1T": ["xdte"], "ynT": ["MT0", "MT1"], "cactT": ["MT2"],
               "qT": ["MT3"], "gbr": ["aT"], "mrg": ["cvT"], "wld0": ["G4", "G5", "G6", "G7", "G8", "G9", "G10", "G11"], "Rall": ["R", "R2"]}

    import os
    STAGE = os.environ.get('STAGE', '')

    def cut(tag):
        if STAGE.startswith(tag):
            S.frozen = True

    def OP(e, fn, r=(), w=()):
        return S.op(e, fn, reads=r, writes=w)

    def bc(ap, shape):
        return ap.to_broadcast(list(shape))

    wld = [sb("wld0", [128, 4096])] * 2
    wbf = [sb("wbf0", [128, 4096], BF16), sb("wbf1", [128, 4096], BF16)]
    wctr = [0]

    def load_w32(dram, kc_n, c0, ncols, r0=0):
        i = wctr[0] % 2
        wctr[0] += 1
        src = dram[r0:r0 + kc_n * 128, c0:c0 + ncols].rearrange("(kc p) n -> p kc n", p=128)
        v32 = wld[i][:, 0:kc_n * ncols].rearrange("p (kc n) -> p kc n", kc=kc_n)
        vbf = wbf[i][:, 0:kc_n * ncols].rearrange("p (kc n) -> p kc n", kc=kc_n)
        S.dma(lambda e: e.dma_start(out=v32, in_=src), writes=["wld0"])
        if wctr[0] % 2:
            OP("act", lambda e: e.copy(out=vbf, in_=v32), ["wld0"], ["wbf%d" % i])
        else:
            OP("dve", lambda e: e.tensor_copy(out=vbf, in_=v32), ["wld0"], ["wbf%d" % i])
        return vbf, "wbf%d" % i


    PANELS = ([("w_in", w_in, 8, S_Z + pn * 512, 512) for pn in range(6)] + [("w_in", w_in, 8, S_XBC, 32)]
              + [("w_in", w_in, 8, pn * 512, 512) for pn in range(4)] + [("w_in", w_in, 8, S_DT + pn * 512, 512) for pn in range(4)]
              + [("w_in", w_in, 8, S_CONF + pn * 512, 512) for pn in range(2)]
              + [("w_in", w_in, 8, S_MEMQ + br * 1024 + pn * 512, 512) for br in range(3) for pn in range(2)]
              + [("wso", wso, 16, pn * 256, 256) for pn in range(4)] + [("wco", wco, 8, pn * 512, 512) for pn in range(2)]
              + [("wmo", wmo, 8, pn * 512, 512) for pn in range(2)] + [("wout", wout, 8, pn * 512, 512) for pn in range(2)]
              + [("wq", wq, 8, pn * 512, 512) for pn in range(4)])
    wscr = dscr("wscr", [len(PANELS), 128, 4096], BF16)
    panel_id = {}
    for pi, (nm_, dram_, kc_n, c0, ncols) in enumerate(PANELS):
        panel_id[(nm_, c0)] = pi
        S.dma(lambda e, pi=pi, dram_=dram_, kc_n=kc_n, c0=c0, ncols=ncols: e.dma_start(
            out=wscr[pi][:, 0:kc_n * ncols].rearrange("p (kc n) -> p kc n", kc=kc_n),
            in_=dram_[0:kc_n * 128, c0:c0 + ncols].rearrange("(kc p) n -> p kc n", p=128)), writes=["wscr%d" % pi], e="pool")
    pu16 = dscr("pu16", [16384, D], BF16); pv16 = dscr("pv16", [16384, D], BF16)
    TABKEYS = []
    for ti, (src_, dst_) in enumerate(((pu, pu16), (pv, pv16))):
        for c in range(32):
            S.dma(lambda e, src_=src_, dst_=dst_, c=c: e.dma_start(out=dst_[c * 512:(c + 1) * 512, :], in_=src_[c * 512:(c + 1) * 512, :]),
                  writes=["tab%d_%d" % (ti, c)], e="pool")
            TABKEYS.append("tab%d_%d" % (ti, c))

    def load_w(dram, kc_n, c0, ncols, r0=0):
        pi = panel_id[(dram.name, c0)]
        i = wctr[0] % 2
        wctr[0] += 1
        n_ = kc_n * ncols
        S.dma(lambda e: e.dma_start(out=wbf[i][:, 0:n_], in_=wscr[pi][:, 0:n_]), reads=["wscr%d" % pi], writes=["wbf%d" % i])
        return wbf[i][:, 0:n_].rearrange("p (kc n) -> p kc n", kc=kc_n), "wbf%d" % i

    hist_s = sb("hist_s", [128, 24, 3], BF16); hist_c = sb("hist_c", [128, 8, 30], BF16)
    OP("pool", lambda e: e.memset(hist_s[:], 0.0), w=["hist_s"])
    OP("pool", lambda e: e.memset(hist_c[:], 0.0), w=["hist_c"])
    hT = sb("hT", [128, 2048]); hTb = sb("hTb", [128, 2048], BF16)
    OP("pool", lambda e: e.memset(hT[:], 0.0), w=["hT"])
    OP("pool", lambda e: e.memset(hTb[:], 0.0), w=["hTb"])
    tail_s = sb("tail_s", [128, 24, 16]); tail_c = sb("tail_c", [128, 8, 32])
    nw_t = sb("nw_t", [128, 2048]); ln_t = sb("ln_t", [128, 4, 1024])
    S.dma(lambda e: e.dma_start(out=nw_t[:], in_=nw.partition_broadcast(128)), writes=["nw_t"])
    for i_, d_ in enumerate((ln1g, ln1b, ln2g, ln2b)):
        S.dma(lambda e, i_=i_, d_=d_: e.dma_start(out=ln_t[:, i_, :], in_=d_.partition_broadcast(128)), writes=["ln_t"])
    cut("c1")
    skT32 = R[:, 0:2048].rearrange("p (a b) -> p a b", b=128); skTb = sb("skTb", [128, 16, 128], BF16)
    S.dma(lambda e: e.dma_start(out=skT32[:], in_=skT), writes=["skT32"])
    OP("pool", lambda e: e.tensor_copy(out=skTb[:], in_=skT32[:]), ["skT32"], ["skTb"])
    cut("c2")
    iota16 = sb("iota16", [128, 16])
    OP("pool", lambda e: e.iota(iota16[:], pattern=[[1, 16]], base=0, channel_multiplier=0, allow_small_or_imprecise_dtypes=True), w=["iota16"])

    cut("c3")
    dtt = sb("dtt", [128, 32]); sz = sb("sz", [128, 2048], BF16)
    xres = sb("xres", [128, 1024]); kv32 = xres
    mTb = sz[:, :].rearrange("p (a b) -> p a b", b=256)
    KT = sb("KT", [128, 8, 256], BF16); Vb = sb("Vb", [128, 2, 1024], BF16)
    mT32 = R[:, 0:2048].rearrange("p (a b) -> p a b", b=256)
    S.dma(lambda e: e.dma_start(out=mT32[:], in_=mempT.rearrange("(kc p) m -> p kc m", p=128)), writes=["mT32"])
    OP("dve", lambda e: e.tensor_copy(out=mTb[:], in_=mT32[:]), ["mT32"], ["mTb"])
    for pn in range(2):
        vbf, wkey = load_w32(wk, 8, pn * 512, 512)
        for cc in range(4):
            for kc in range(8):
                OP("pe", lambda e, kc=kc, cc=cc, vbf=vbf: e.matmul(PB[0][:, 0:256], lhsT=vbf[:, kc, cc * 128:(cc + 1) * 128], rhs=mTb[:, kc, :],
                                                                  start=(kc == 0), stop=(kc == 7)), [wkey, "mTb"], ["pb0"])
            OP("act", lambda e, c=pn * 4 + cc: e.copy(out=KT[:, c, :], in_=PB[0][:, 0:256]), ["pb0"], ["KT"])
    cut("c4")
    for wi, (wd, od) in enumerate(((wk, o_k_p), (wv, o_v_p))):
        for mt in range(2):
            for pn in range(2):
                vbf, wkey = load_w32(wd, 8, pn * 512, 512)
                for kc in range(8):
                    OP("pe", lambda e, kc=kc, vbf=vbf, mt=mt: e.matmul(PB[1][:, :], lhsT=mTb[:, kc, mt * 128:(mt + 1) * 128], rhs=vbf[:, kc, :],
                                                                      start=(kc == 0), stop=(kc == 7)), [wkey, "mTb"], ["pb1"])
                OP("act", lambda e, pn=pn: e.copy(out=kv32[:, pn * 512:(pn + 1) * 512], in_=PB[1][:, :]), ["pb1"], ["kv32"])
                if wi == 1 and "novb" not in STAGE:
                    OP("dve", lambda e, pn=pn, mt=mt: e.tensor_copy(out=Vb[:, mt, pn * 512:(pn + 1) * 512], in_=kv32[:, pn * 512:(pn + 1) * 512]), ["kv32"], ["Vb"])
            if "noout" not in STAGE:
                S.dma(lambda e, od=od, mt=mt: e.dma_start(out=od[mt * 128:(mt + 1) * 128, :], in_=kv32[:]), reads=["kv32"], writes=["okv"])

    cut("c5")
    R_early = R
    xT32 = R_early[:, 0:1024].rearrange("p (a b) -> p a b", b=128); xTb = sb("xTb", [128, 8, 128], BF16)
    xtok = sb("xtok", [128, 2048])
    BT = sb("BT", [128, 4, 128], BF16); CT = sb("CT", [128, 4, 128], BF16); Btok = sb("Btok", [128, 4, 128], BF16)
    Ctok_s = sb("Ctok_s", [NS, 4, 128]); Btok_s = sb("Btok_s", [NS, 4, 128])
    xh = [sb("xh%d" % i, [128, 160], BF16) for i in range(2)]
    fm32 = [sb("fm32_%d" % i, [128, 512]) for i in range(2)]
    dg = [sb("dg0", [128, 4, 128], BF16)] * 2
    xdt = sb("xdt", [128, 2048], BF16); xdte = sb("xdte", [128, 2048], BF16); cbm = sb("cbm", [128, 512])
    sm = sb("sm", [128, 8, 32])
    y32 = sb("y32", [128, 2048]); ynT = MT[:, 0:2048].rearrange("p (a b) -> p a b", b=128)
    acv = sb("acv", [128, 16, 128]); aT = acv[:, 0:8, :]; cvT = acv[:, 8:16, :]; cactT = MT[:, 2048:3072].rearrange("p (a b) -> p a b", b=128)
    stat = sb("stat", [128, 4, 128])
    qT = MT[:, 3072:4096].rearrange("p (a b) -> p a b", b=128); oT = sb("oT", [128, 8, 128], BF16); PnT = sb("PnT", [128, 2, 128], BF16)
    att = sb("att", [128, 3, 256]); col = sb("col", [128, 16])
    gbr = acv[:, 0:8, :]; mrg = acv[:, 8:16, :]; mrgb = xdte[:, 0:1024].rearrange("p (a b) -> p a b", b=128)
    x1 = xtok[:, 0:1024]; x1T = xdte[:, 1024:2048].rearrange("p (a b) -> p a b", b=128); h1 = xtok[:, 1024:2048]
    qpT = xdt[:, :].rearrange("p (a b) -> p a b", b=128)
    top = sb("top", [128, 16, 16]); idxu = sb("idxu", [128, 16, 16], U32); idxf = sb("idxf", [128, 16, 16])
    best = sb("best", [128, 8, 16]); pos = sb("pos", [128, 8, 16], U32); pab = sb("pab", [128, 2, 128], U32); pabf = sb("pabf", [128, 2, 128])
    selw = y32; ids = sb("ids", [128, 2, 128]); idi = sb("idi", [128, 128], I32)
    gw = sb("gw", [128, 128]); dots = sb("dots", [128, 128]); coef = sb("coef", [128, 128])
    Gall = wld[0][:, :].rearrange("p (a b) -> p a b", b=1024); Gp = sb("Gp", [128, 4096], BF16); G = [Gp[:, i * 1024:(i + 1) * 1024] for i in range(4)] + [wld[0][:, :].bitcast(BF16)[:, i * 1024:(i + 1) * 1024] for i in range(8)]; NG = len(G); x1p = sb("x1p", [128, 1024]); x1pb = sb("x1pb", [128, 1024], BF16)
    dgv = [sb("dgv%d" % i, [128, 128], BF16) for i in range(2)]
    cut("c6")
    OP("pool", lambda e: e.memset(idi[:], 0), w=["idi"])
    sQ = sb("sQ", [128, NS, 16]); sB = acv; sC = Gall[:, 0:2, :].rearrange("p a (b n) -> p (a b) n", n=128)
    h0 = [Gall[:, 2:4, :].rearrange("p a d -> p (a d)")] * 2
    scr_x = dscr("scr_x", [NS, 2048]); scr_B = dscr("scr_B", [NS, 512]); scr_C = dscr("scr_C", [NS, 512]); scr_y = dscr("scr_y", [NS, 2048])
    scr_q = dscr("scr_q", [NS, 1024]); scr_o = dscr("scr_o", [NS, 4, 1024])
    cut("c7")
    sel32 = sb("sel32", [32, 128])
    OP("pool", lambda e: e.memset(sel32[:], 1.0), w=["sel32"])
    OP("pool", lambda e: e.affine_select(out=sel32[:], in_=sel32[:], pattern=[[1, 128]], compare_op=ALU.is_ge, fill=0.0, base=0,
                                         channel_multiplier=-4), ["sel32"], ["sel32"])
    OP("pool", lambda e: e.affine_select(out=sel32[:], in_=sel32[:], pattern=[[-1, 128]], compare_op=ALU.is_ge, fill=0.0, base=3,
                                         channel_multiplier=4), ["sel32"], ["sel32"])
    cut("c8")
    hp = sb("hp", [32, 4]); hp2 = sb("hp2", [32, 3, NS]); qsc = sb("qsc", [128, 4, NS])
    S.dma(lambda e: e.dma_start(out=hp[:, 0:1], in_=dtb.rearrange("o h -> h o")), writes=["hp"])
    S.dma(lambda e: e.dma_start(out=hp[:, 1:2], in_=alog.rearrange("o h -> h o")), writes=["hp"])
    S.dma(lambda e: e.dma_start(out=hp[:, 2:3], in_=dsk.rearrange("o h -> h o")), writes=["hp"])
    OP("act", lambda e: e.activation(out=hp[:, 1:2], in_=hp[:, 1:2], func=AF.Exp), ["hp"], ["hp"])
    OP("dve", lambda e: e.tensor_scalar(out=hp[:, 1:2], in0=hp[:, 1:2], scalar1=-1.0, scalar2=None, op0=ALU.mult), ["hp"], ["hp"])
    Ks = Gall[:, 0:2, :]; Vs = Gall[:, 2:4, :]; qb = selw[:, 0:1024]
    Sall = sb("Sall", [128, NS, 8]); Eall = sb("Eall", [128, NS, 8]); o4 = R[0:4, 1024:2048]; otok = R[0:NS, 0:1024]

    def ln_tok(src, dst, gi, N, scratch=None, skey="selw", key=None):
        scr = scratch if scratch is not None else selw[:, 0:1024]
        ks = [key] if key else ["h1", "x1"]
        OP("dve", lambda e: e.tensor_reduce(out=col[0:N, 0:1], in_=src[0:N, :], axis=AX.X, op=ALU.add), ks, ["col"])
        OP("dve", lambda e: e.tensor_scalar(out=col[0:N, 0:1], in0=col[0:N, 0:1], scalar1=1.0 / 1024, scalar2=None, op0=ALU.mult), ["col"], ["col"])
        OP("dve", lambda e: e.tensor_scalar(out=src[0:N, :], in0=src[0:N, :], scalar1=col[0:N, 0:1], scalar2=None, op0=ALU.subtract),
           ["col"] + ks, ks)
        OP("act", lambda e: e.activation(out=scr[0:N, :], in_=src[0:N, :], func=AF.Square, accum_out=col[0:N, 1:2]), ks, [skey, "col"])
        OP("dve", lambda e: e.tensor_scalar(out=col[0:N, 1:2], in0=col[0:N, 1:2], scalar1=1.0 / 1024, scalar2=EPS, op0=ALU.mult, op1=ALU.add),
           ["col"], ["col"])
        OP("act", lambda e: e.activation(out=col[0:N, 1:2], in_=col[0:N, 1:2], func=AF.Sqrt), ["col"], ["col"])
        OP("dve", lambda e: e.reciprocal(out=col[0:N, 1:2], in_=col[0:N, 1:2]), ["col"], ["col"])
        OP("dve", lambda e: e.scalar_tensor_tensor(out=dst[0:N, :], in0=src[0:N, :], scalar=col[0:N, 1:2], in1=ln_t[0:N, gi, :],
                                                   op0=ALU.mult, op1=ALU.mult), ks + ["col", "ln_t"], ks)
        OP("dve", lambda e: e.tensor_tensor(out=dst[0:N, :], in0=dst[0:N, :], in1=ln_t[0:N, gi + 1, :], op=ALU.add), ks + ["ln_t"], ks)

    import os
    STAGE = os.environ.get('STAGE', '')

    PUMP = int(os.environ.get('PUMP', '2'))
    PUMP2 = int(os.environ.get('PUMP2', '3'))
    PP = [PUMP2]

    def emit_block(blk):
        samp = blk == 16
        N = NS if samp else 128
        t0 = 0 if samp else blk * 128
        last = blk == 15
        srcT = xsT if samp else xpT[:, t0:t0 + 128]
        S.dma(lambda e: e.dma_start(out=xT32[:, :, 0:N], in_=srcT.rearrange("(kc p) t -> p kc t", p=128)), writes=["xT32"])
        OP("dve", lambda e: e.tensor_copy(out=xTb[:, :, 0:N], in_=xT32[:, :, 0:N]), ["xT32"], ["xTb"])
        S.dma(lambda e: e.dma_start(out=xres[0:N, :], in_=(xs if samp else xp[t0:t0 + 128, :])), writes=["xres"])

        def proj_fm(vbf, wkey, cc, ps, pskey):
            pump(PP[0])
            for kc in range(8):
                OP("pe", lambda e, kc=kc: e.matmul(ps[:, 0:N], lhsT=vbf[:, kc, cc * 128:(cc + 1) * 128], rhs=xTb[:, kc, 0:N],
                                                   start=(kc == 0), stop=(kc == 7)), [wkey, "xTb"], [pskey])

        def proj_tm(vbf, wkey, ncols, ps, pskey, c0=0):
            for kc in range(8):
                OP("pe", lambda e, kc=kc: e.matmul(ps[0:N, 0:ncols], lhsT=xTb[:, kc, 0:N], rhs=vbf[:, kc, c0:c0 + ncols],
                                                   start=(kc == 0), stop=(kc == 7)), [wkey, "xTb"], [pskey])

        PP[0] = 3
        for pn in range(6):
            pump(PUMP)
            vbf, wkey = load_w(w_in, 8, S_Z + pn * 512, 512)
            for cc in range(4):
                j = pn * 4 + cc
                ps, pskey = PB[j % 2], "pb%d" % (j % 2)
                proj_fm(vbf, wkey, cc, ps, pskey)
                xhj, xkey = xh[j % 2], "xh%d" % (j % 2)
                dgj, dkey = dg[0], "dg0"
                for k in range(4):
                    OP("act", lambda e, k=k, j=j, dgj=dgj: e.activation(out=dgj[:, k, :], in_=ident[:], func=AF.Copy, scale=scw_t[:, j, k:k + 1]),
                       ["ident", "scw_t"], [dkey])
                if not samp:
                    OP("dve", lambda e, j=j, xhj=xhj: e.tensor_copy(out=xhj[:, 0:3], in_=hist_s[:, j, :]), ["hist_s"], [xkey])
                    OP("act", lambda e, xhj=xhj, ps=ps: e.copy(out=xhj[:, 3:3 + N], in_=ps[:, 0:N]), [pskey], [xkey])
                    OP("dve", lambda e, j=j, xhj=xhj: e.tensor_copy(out=hist_s[:, j, :], in_=xhj[:, N:N + 3]), [xkey], ["hist_s"])
                    if last:
                        OP("dve", lambda e, j=j, ps=ps: e.tensor_copy(out=tail_s[:, j, 0:3], in_=ps[:, N - 3:N]), [pskey], ["tail_s"])
                    rhs_k = lambda k, xhj=xhj: xhj[:, k:k + N]
                else:
                    xv = xhj[:, 0:64].rearrange("p (b k) -> p b k", k=4)
                    stg = fm32[0][:, 0:48].rearrange("p (b k) -> p b k", k=3)
                    S.dma(lambda e, j=j, stg=stg: e.dma_start(out=stg, in_=st_sconvT[:, j, :, :]), writes=["fm32_0"])
                    OP("dve", lambda e, xv=xv, stg=stg: e.tensor_copy(out=xv[:, :, 0:3], in_=stg), ["fm32_0"], [xkey])
                    OP("act", lambda e, xv=xv, ps=ps: e.copy(out=xv[:, :, 3], in_=ps[:, 0:N]), [pskey], [xkey])
                    OP("dve", lambda e, j=j, ps=ps: e.tensor_copy(out=tail_s[:, j, 0:16], in_=ps[:, 0:N]), [pskey], ["tail_s"])
                    rhs_k = lambda k, xv=xv: xv[:, :, k]
                pc, pckey = PB[2 + j % 2], "pb%d" % (2 + j % 2)
                for k in range(4):
                    OP("pe", lambda e, k=k, dgj=dgj, rhs_k=rhs_k, pc=pc: e.matmul(pc[:, 0:N], lhsT=dgj[:, k, :], rhs=rhs_k(k),
                                                                                 start=(k == 0), stop=(k == 3)), [dkey, xkey], [pckey])
                fm, fkey = fm32[1], "fm32_1"
                OP("act", lambda e, j=j, pc=pc: e.activation(out=fm[:, 0:N], in_=pc[:, 0:N], func=AF.Silu, bias=scb_t[:, j:j + 1]),
                   [pckey, "scb_t"], [fkey])
                if j >= 20:
                    OP("act", lambda e, j=j: e.copy(out=CT[:, j - 20, 0:N], in_=fm[:, 0:N]), [fkey], ["CT"])
                    if not samp:
                        continue
                if 16 <= j < 20:
                    OP("act", lambda e, j=j: e.copy(out=BT[:, j - 16, 0:N], in_=fm[:, 0:N]), [fkey], ["BT"])
                pt, ptkey = PB[4 + j % 2], "pb%d" % (4 + j % 2)
                OP("pe", lambda e, pt=pt: e.transpose(out=pt[0:N, 0:128], in_=fm[:, 0:N], identity=ident[:]), [fkey, "ident"], [ptkey])
                if j < 16:
                    OP("act", lambda e, j=j, pt=pt: e.copy(out=xtok[0:N, j * 128:(j + 1) * 128], in_=pt[0:N, 0:128]), [ptkey], ["xtok"])
                elif j < 20:
                    OP("act", lambda e, j=j, pt=pt: e.copy(out=Btok[0:N, j - 16, :], in_=pt[0:N, 0:128]), [ptkey], ["Btok"])
                    if samp:
                        OP("act", lambda e, j=j, pt=pt: e.copy(out=Btok_s[0:N, j - 16, :], in_=pt[0:N, 0:128]), [ptkey], ["Btok_s"])
                else:
                    OP("act", lambda e, j=j, pt=pt: e.copy(out=Ctok_s[0:N, j - 20, :], in_=pt[0:N, 0:128]), [ptkey], ["Ctok_s"])

        if STAGE == "xbc":
            return
        pump(PUMP)
        vbf, wkey = load_w(w_in, 8, S_XBC, 32)
        proj_tm(vbf, wkey, 32, PB[4], "pb4")
        OP("dve", lambda e: e.tensor_tensor(out=dtt[0:N, :], in0=PB[4][0:N, 0:32], in1=dtb_t[0:N, :], op=ALU.add), ["pb4", "dtb_t"], ["dtt"])
        OP("act", lambda e: e.activation(out=dtt[0:N, :], in_=dtt[0:N, :], func=AF.Exp), ["dtt"], ["dtt"])
        OP("act", lambda e: e.activation(out=dtt[0:N, :], in_=dtt[0:N, :], func=AF.Ln, bias=1.0), ["dtt"], ["dtt"])
        if samp:
            OP("pe", lambda e: e.transpose(out=PB[4][0:32, 64:64 + NS], in_=dtt[0:NS, :], identity=ident[0:NS, 0:NS]), ["dtt", "ident"], ["pb4"])
            OP("act", lambda e: e.copy(out=hp2[:, 0, :], in_=PB[4][0:32, 64:64 + NS]), ["pb4"], ["hp2"])
            OP("act", lambda e: e.activation(out=hp2[:, 1, :], in_=hp2[:, 0, :], func=AF.Exp, scale=hp[:, 1:2]), ["hp2", "hp"], ["hp2"])
            OP("dve", lambda e: e.tensor_copy(out=hp2[:, 2, :], in_=bc(hp[:, 2:3], [32, NS])), ["hp"], ["hp2"])
        if STAGE == "dt":
            return
        for pn in range(4):
            pump(PUMP)
            vbf, wkey = load_w(w_in, 8, pn * 512, 512)
            ps, pskey = PB[pn % 2], "pb%d" % (pn % 2)
            proj_tm(vbf, wkey, 512, ps, pskey)
            OP("act", lambda e, pn=pn, ps=ps: e.activation(out=sz[0:N, pn * 512:(pn + 1) * 512], in_=ps[0:N, :], func=AF.Silu), [pskey], ["sz"])

        if STAGE == "z":
            return
        if not samp:
            da, acs, dte, cd, eacs = (sm[:, i_, :] for i_ in range(5))
            OP("dve", lambda e: e.tensor_tensor(out=da, in0=dtt[:, :], in1=a_t[:, :], op=ALU.mult), ["dtt", "a_t"], ["sm"])
            OP("pe", lambda e: e.matmul(PB[4][:, 0:32], lhsT=tri[:], rhs=da, start=True, stop=True), ["tri", "sm"], ["pb4"])
            OP("act", lambda e: e.copy(out=acs, in_=PB[4][:, 0:32]), ["pb4"], ["sm"])
            OP("pe", lambda e: e.matmul(PB[4][:, 32:64], lhsT=sel_last[:], rhs=acs, start=True, stop=True), ["sel_last", "sm"], ["pb4"])
            OP("dve", lambda e: e.tensor_tensor(out=dte, in0=PB[4][:, 32:64], in1=acs, op=ALU.subtract), ["pb4", "sm"], ["sm"])
            OP("act", lambda e: e.activation(out=dte, in_=dte, func=AF.Exp), ["sm"], ["sm"])
            OP("dve", lambda e: e.tensor_tensor(out=dte, in0=dte, in1=dtt[:, :], op=ALU.mult), ["sm", "dtt"], ["sm"])
            OP("act", lambda e: e.activation(out=cd, in_=PB[4][:, 32:64], func=AF.Exp), ["pb4"], ["sm"])
            OP("act", lambda e: e.activation(out=eacs, in_=acs, func=AF.Exp), ["sm"], ["sm"])
            R3 = R[:, :].rearrange("p (h l) -> p h l", l=128)
            OP("pool", lambda e: e.tensor_tensor(out=R3, in0=bc(tri[:, :].unsqueeze(1), [128, 32, 128]), in1=bc(da.unsqueeze(2), [128, 32, 128]),
                                                 op=ALU.mult), ["tri", "sm"], ["R", "R2"])
            for half in range(2):
                for q in range(4):
                    qq = half * 4 + q
                    OP("pe", lambda e, q=q, qq=qq: e.matmul(PB[q][:, :], lhsT=ones[:], rhs=R[:, qq * 512:(qq + 1) * 512], start=True, stop=True),
                       ["ones", "R" if half == 0 else "R2"], ["pb%d" % q])
                for hh in range(16):
                    h = half * 16 + hh
                    OP("dve", lambda e, h=h, hh=hh: e.tensor_scalar(out=R[:, h * 128:(h + 1) * 128], in0=PB[hh // 4][:, (hh % 4) * 128:(hh % 4 + 1) * 128],
                                                                    scalar1=acs[:, h:h + 1], scalar2=0.0, op0=ALU.subtract, op1=ALU.min),
                       ["pb%d" % (hh // 4), "sm"], ["R" if half == 0 else "R2"])
            pump(6)
            OP("act", lambda e: e.activation(out=R[:, :], in_=R[:, :], func=AF.Exp), ["R", "R2"], ["R", "R2"])
            pump(6)
            for g in range(4):
                OP("pe", lambda e, g=g: e.matmul(PB[4][:, g * 128:(g + 1) * 128], lhsT=BT[:, g, :], rhs=CT[:, g, :], start=True, stop=True),
                   ["BT", "CT"], ["pb4"])
            OP("dve", lambda e: e.tensor_tensor(out=cbm[:, :].rearrange("p (g l) -> p g l", l=128), in0=PB[4][:, :].rearrange("p (g l) -> p g l", l=128),
                                                in1=bc(tri[:, :].unsqueeze(1), [128, 4, 128]), op=ALU.mult), ["pb4", "tri"], ["cbm"])
            OP("dve", lambda e: e.tensor_tensor(out=MT[:, :].rearrange("p (g r l) -> p g r l", g=4, r=8),
                                                in0=R[:, :].rearrange("p (g r l) -> p g r l", g=4, r=8),
                                                in1=bc(cbm[:, :].rearrange("p (g l) -> p g l", l=128).unsqueeze(2), [128, 4, 8, 128]), op=ALU.mult),
               ["R", "R2", "cbm"], ["MT"])
            pump(6)
            x3 = xtok[:, :].rearrange("p (h d) -> p h d", d=64)
            OP("pool", lambda e: e.tensor_tensor(out=xdt[:, :].rearrange("p (h d) -> p h d", d=64), in0=x3,
                                                 in1=bc(dtt[:, :].unsqueeze(2), [128, 32, 64]), op=ALU.mult), ["xtok", "dtt"], ["xdt"])
            OP("pool", lambda e: e.tensor_tensor(out=xdte[:, :].rearrange("p (h d) -> p h d", d=64), in0=x3,
                                                 in1=bc(dte.unsqueeze(2), [128, 32, 64]), op=ALU.mult), ["xtok", "sm"], ["xdte"])
            for h in range(32):
                OP("pe", lambda e, h=h: e.matmul(PB[h // 8][:, (h % 8) * 64:(h % 8 + 1) * 64], lhsT=MT[:, h * 128:(h + 1) * 128],
                                                 rhs=xdt[:, h * 64:(h + 1) * 64], start=True, stop=True), ["MT", "xdt"], ["pb%d" % (h // 8)])
            for g in range(4):
                bk = 4 + g % 2
                OP("pe", lambda e, g=g, bk=bk: e.matmul(PB[bk][:, :], lhsT=CT[:, g, :], rhs=hTb[:, g * 512:(g + 1) * 512], start=True, stop=True),
                   ["CT", "hTb"], ["pb%d" % bk])
                OP("dve", lambda e, g=g, bk=bk: e.tensor_tensor(out=R[:, g * 512:(g + 1) * 512].rearrange("p (r d) -> p r d", d=64),
                                                                in0=PB[bk][:, :].rearrange("p (r d) -> p r d", d=64),
                                                                in1=bc(eacs[:, g * 8:(g + 1) * 8].unsqueeze(2), [128, 8, 64]), op=ALU.mult),
                   ["pb%d" % bk, "sm"], ["R"])
                OP("dve", lambda e, g=g: e.tensor_tensor(out=y32[:, g * 512:(g + 1) * 512], in0=PB[g][:, :], in1=R[:, g * 512:(g + 1) * 512],
                                                         op=ALU.add), ["pb%d" % g, "R"], ["y32"])
            OP("pool", lambda e: e.tensor_tensor(out=R[:, 2048:4096].rearrange("p (h d) -> p h d", d=64), in0=x3,
                                                 in1=bc(dsk_t[:, :].unsqueeze(2), [128, 32, 64]), op=ALU.mult), ["xtok", "dsk_t"], ["R2"])
            OP("pool", lambda e: e.tensor_tensor(out=y32[:, :], in0=y32[:, :], in1=R[:, 2048:4096], op=ALU.add), ["R2", "y32"], ["y32"])
            pump(6)
            for g in range(4):
                OP("pe", lambda e, g=g: e.matmul(PB[g][:, :], lhsT=Btok[:, g, :], rhs=xdte[:, g * 512:(g + 1) * 512], start=True, stop=True),
                   ["Btok", "xdte"], ["pb%d" % g])
            OP("dve", lambda e: e.tensor_tensor(out=hT[:, :].rearrange("p (h d) -> p h d", d=64), in0=hT[:, :].rearrange("p (h d) -> p h d", d=64),
                                                in1=bc(cd.unsqueeze(2), [128, 32, 64]), op=ALU.mult), ["hT", "sm"], ["hT"])
            for g in range(4):
                OP("dve", lambda e, g=g: e.tensor_tensor(out=hT[:, g * 512:(g + 1) * 512], in0=hT[:, g * 512:(g + 1) * 512], in1=PB[g][:, :],
                                                         op=ALU.add), ["hT", "pb%d" % g], ["hT"])
            OP("act", lambda e: e.copy(out=hTb[:, :], in_=hT[:, :]), ["hT"], ["hTb"])
            if last:
                for hp_ in range(16):
                    OP("pe", lambda e, hp_=hp_: e.transpose(out=PB[hp_ % 2][:, 0:128], in_=hT[:, hp_ * 128:(hp_ + 1) * 128], identity=ident[:]),
                       ["hT", "ident"], ["pb%d" % (hp_ % 2)])
                    OP("act", lambda e, hp_=hp_: e.copy(out=fm32[hp_ % 2][:, 0:128], in_=PB[hp_ % 2][:, 0:128]), ["pb%d" % (hp_ % 2)], ["fm32_%d" % (hp_ % 2)])
                    S.dma(lambda e, hp_=hp_: e.dma_start(out=o_ssd_p[hp_ * 128:(hp_ + 1) * 128, :], in_=fm32[hp_ % 2][:, 0:128]),
                          reads=["fm32_%d" % (hp_ % 2)], writes=["o_ssd_p"])
        else:
            S.dma(lambda e: e.dma_start(out=scr_x, in_=xtok[0:NS, :]), reads=["xtok"], writes=["scr_x"])
            S.dma(lambda e: e.dma_start(out=scr_B, in_=Btok_s[:, :, :].rearrange("p g n -> p (g n)")), reads=["Btok_s"], writes=["scr_B"])
            S.dma(lambda e: e.dma_start(out=scr_C, in_=Ctok_s[:, :, :].rearrange("p g n -> p (g n)")), reads=["Ctok_s"], writes=["scr_C"])
            S.dma(lambda e: e.dma_start(out=sQ[:], in_=scr_x.rearrange("b (q r) -> q b r", r=16)), reads=["scr_x"], writes=["sQ"])
            for g in range(4):
                S.dma(lambda e, g=g: e.dma_start(out=sB[32 * g:32 * (g + 1), :, :], in_=scr_B[:, g * 128:(g + 1) * 128].partition_broadcast(32)),
                      reads=["scr_B"], writes=["sB"])
                S.dma(lambda e, g=g: e.dma_start(out=sC[32 * g:32 * (g + 1), :, :], in_=scr_C[:, g * 128:(g + 1) * 128].partition_broadcast(32)),
                      reads=["scr_C"], writes=["sC"])
            OP("pe", lambda e: e.matmul(PB[4][:, 128:128 + 3 * NS], lhsT=sel32[:, :], rhs=hp2[:, :, :].rearrange("p a b -> p (a b)"),
                                        start=True, stop=True), ["sel32", "hp2"], ["pb4"])
            OP("act", lambda e: e.copy(out=qsc[:, 0:3, :].rearrange("p a b -> p (a b)"), in_=PB[4][:, 128:128 + 3 * NS]), ["pb4"], ["qsc"])
            dtx = sm[:, 0:8, :].rearrange("p a b -> p (a b)")[:, 0:256].rearrange("p (b r) -> p b r", r=16)
            OP("dve", lambda e: e.tensor_tensor(out=dtx, in0=sQ[:, :, :], in1=bc(qsc[:, 0, :].unsqueeze(2), [128, NS, 16]), op=ALU.mult),
               ["sQ", "qsc"], ["sm"])
            yo = y32[:, 0:256].rearrange("p (b r) -> p b r", r=16)
            for b in range(NS):
                hb, hkey = h0[b % 2], "h0_%d" % (b % 2)
                S.dma(lambda e, b=b, hb=hb: e.dma_start(out=hb[:, :], in_=st_ssd[b]), writes=[hkey])
                h3 = hb[:, :].rearrange("p (r n) -> p r n", n=128)
                R3 = R[:, 0:2048].rearrange("p (r n) -> p r n", n=128)
                OP("dve", lambda e, b=b, h3=h3, R3=R3: e.tensor_tensor(out=R3, in0=h3, in1=bc(sC[:, b, :].unsqueeze(1), [128, 16, 128]), op=ALU.mult),
                   [hkey, "sC"], ["R"])
                OP("dve", lambda e, b=b, R3=R3: e.tensor_reduce(out=yo[:, b, :], in_=R3, axis=AX.X, op=ALU.add), ["R"], ["y32"])
                R4 = R[:, 2048:4096].rearrange("p (r n) -> p r n", n=128)
                OP("pool", lambda e, b=b, R4=R4: e.tensor_tensor(out=R4, in0=bc(dtx[:, b, :].unsqueeze(2), [128, 16, 128]),
                                                                 in1=bc(sB[:, b, :].unsqueeze(1), [128, 16, 128]), op=ALU.mult), ["sm", "sB"], ["R2"])
                OP("dve", lambda e, b=b, hb=hb: e.scalar_tensor_tensor(out=hb[:, :], in0=hb[:, :], scalar=qsc[:, 1, b:b + 1], in1=R[:, 2048:4096],
                                                                       op0=ALU.mult, op1=ALU.add), [hkey, "qsc", "R2"], [hkey])
                S.dma(lambda e, b=b, hb=hb: e.dma_start(out=o_ssd_s[b], in_=hb[:, :]), reads=[hkey], writes=["o_ssd_s"])
            OP("dve", lambda e: e.tensor_tensor(out=R[:, 0:2048].rearrange("p (b n) -> p b n", n=128), in0=sC[:, :, :], in1=sB[:, :, :], op=ALU.mult),
               ["sC", "sB"], ["R"])
            OP("dve", lambda e: e.tensor_reduce(out=qsc[:, 3, :], in_=R[:, 0:2048].rearrange("p (b n) -> p b n", n=128), axis=AX.X, op=ALU.add),
               ["R"], ["qsc"])
            OP("dve", lambda e: e.tensor_tensor(out=yo, in0=yo, in1=bc(qsc[:, 1, :].unsqueeze(2), [128, NS, 16]), op=ALU.mult), ["y32", "qsc"], ["y32"])
            OP("dve", lambda e: e.tensor_tensor(out=dtx, in0=dtx, in1=bc(qsc[:, 3, :].unsqueeze(2), [128, NS, 16]), op=ALU.mult), ["sm", "qsc"], ["sm"])
            OP("dve", lambda e: e.tensor_tensor(out=yo, in0=yo, in1=dtx, op=ALU.add), ["y32", "sm"], ["y32"])
            OP("dve", lambda e: e.tensor_tensor(out=dtx, in0=sQ[:, :, :], in1=bc(qsc[:, 2, :].unsqueeze(2), [128, NS, 16]), op=ALU.mult), ["sQ", "qsc"], ["sm"])
            OP("dve", lambda e: e.tensor_tensor(out=yo, in0=yo, in1=dtx, op=ALU.add), ["y32", "sm"], ["y32"])
            S.dma(lambda e: e.dma_start(out=scr_y.rearrange("b (q r) -> q b r", r=16), in_=yo), reads=["y32"], writes=["scr_y"])
            S.dma(lambda e: e.dma_start(out=y32[0:NS, :], in_=scr_y), reads=["scr_y"], writes=["y32"])

        if STAGE == "ssd":
            return
        pump(PUMP)
        OP("dve", lambda e: e.tensor_tensor(out=y32[0:N, :], in0=y32[0:N, :], in1=sz[0:N, :], op=ALU.mult), ["y32", "sz"], ["y32"])
        OP("act", lambda e: e.activation(out=R[0:N, 0:2048], in_=y32[0:N, :], func=AF.Square, accum_out=col[0:N, 2:3]), ["y32"], ["R", "col"])
        OP("dve", lambda e: e.tensor_scalar(out=col[0:N, 2:3], in0=col[0:N, 2:3], scalar1=1.0 / 2048, scalar2=EPS, op0=ALU.mult, op1=ALU.add), ["col"], ["col"])
        OP("act", lambda e: e.activation(out=col[0:N, 2:3], in_=col[0:N, 2:3], func=AF.Sqrt), ["col"], ["col"])
        OP("dve", lambda e: e.reciprocal(out=col[0:N, 2:3], in_=col[0:N, 2:3]), ["col"], ["col"])
        OP("dve", lambda e: e.scalar_tensor_tensor(out=y32[0:N, :], in0=y32[0:N, :], scalar=col[0:N, 2:3], in1=nw_t[0:N, :], op0=ALU.mult, op1=ALU.mult),
           ["y32", "col", "nw_t"], ["y32"])
        for fc in range(16):
            pt, ptkey = PB[4 + fc % 2], "pb%d" % (4 + fc % 2)
            OP("pe", lambda e, fc=fc, pt=pt: e.transpose(out=pt[:, 0:N], in_=y32[0:N, fc * 128:(fc + 1) * 128], identity=ident[0:N, 0:N]),
               ["y32", "ident"], [ptkey])
            OP("act", lambda e, fc=fc, pt=pt: e.copy(out=ynT[:, fc, 0:N], in_=pt[:, 0:N]), [ptkey], ["ynT"])

        if STAGE == "rms":
            return
        pump(PUMP)
        PP[0] = 3
        for pn in range(4):
            pump(PUMP)
            vbf, wkey = load_w(w_in, 8, S_DT + pn * 512, 512)
            for cc in range(4):
                c8 = (pn % 2) * 4 + cc
                ps, pskey = PB[cc % 2], "pb%d" % (cc % 2)
                proj_fm(vbf, wkey, cc, ps, pskey)
                if pn < 2:
                    OP("act", lambda e, c8=c8, ps=ps: e.copy(out=aT[:, c8, 0:N], in_=ps[:, 0:N]), [pskey], ["aT"])
                    continue
                fm, fkey = fm32[1], "fm32_1"
                OP("act", lambda e, ps=ps: e.activation(out=fm[:, 0:N], in_=ps[:, 0:N], func=AF.Sigmoid), [pskey], [fkey])
                OP("dve", lambda e, c8=c8: e.tensor_tensor(out=fm[:, 0:N], in0=fm[:, 0:N], in1=aT[:, c8, 0:N], op=ALU.mult), [fkey, "aT"], [fkey])
                xhj, xkey = xh[cc % 2], "xh%d" % (cc % 2)
                if not samp:
                    OP("dve", lambda e, c8=c8, xhj=xhj: e.tensor_copy(out=xhj[:, 0:30], in_=hist_c[:, c8, :]), ["hist_c"], [xkey])
                    OP("dve", lambda e, xhj=xhj: e.tensor_copy(out=xhj[:, 30:30 + N], in_=fm[:, 0:N]), [fkey], [xkey])
                    OP("dve", lambda e, c8=c8, xhj=xhj: e.tensor_copy(out=hist_c[:, c8, :], in_=xhj[:, N:N + 30]), [xkey], ["hist_c"])
                    if last:
                        OP("dve", lambda e, c8=c8: e.tensor_copy(out=tail_c[:, c8, 0:30], in_=fm[:, N - 30:N]), [fkey], ["tail_c"])
                    rhs_k = lambda k, xhj=xhj: xhj[:, k:k + N]
                else:
                    gv = R[:, 0:496].rearrange("p (b k) -> p b k", k=31)
                    stg = R[:, 512:992].rearrange("p (b k) -> p b k", k=30)
                    S.dma(lambda e, c8=c8, stg=stg: e.dma_start(out=stg, in_=st_cconvT[:, c8, :, :]), writes=["R"])
                    gvb = xdt[:, 0:496].rearrange("p (b k) -> p b k", k=31)
                    OP("dve", lambda e, gvb=gvb, stg=stg: e.tensor_copy(out=gvb[:, :, 0:30], in_=stg), ["R"], ["xdt"])
                    OP("dve", lambda e, gvb=gvb: e.tensor_copy(out=gvb[:, :, 30], in_=fm[:, 0:N]), [fkey], ["xdt"])
                    OP("dve", lambda e, c8=c8: e.tensor_copy(out=tail_c[:, c8, 0:16], in_=fm[:, 0:N]), [fkey], ["tail_c"])
                    rhs_k = lambda k, gvb=gvb: gvb[:, :, k]
                    xkey = "xdt"
                OP("dve", lambda e, c8=c8, rhs_k=rhs_k: e.tensor_scalar(out=cvT[:, c8, 0:N], in0=rhs_k(0), scalar1=ccw_t[:, c8, 0:1],
                                                                       scalar2=ccb_t[:, c8:c8 + 1], op0=ALU.mult, op1=ALU.add),
                   [xkey, "ccw_t", "ccb_t"], ["cvT"])
                for k in range(1, 31):
                    OP("dve", lambda e, k=k, c8=c8, rhs_k=rhs_k: e.scalar_tensor_tensor(out=cvT[:, c8, 0:N], in0=rhs_k(k), scalar=ccw_t[:, c8, k:k + 1],
                                                                                     in1=cvT[:, c8, 0:N], op0=ALU.mult, op1=ALU.add),
                       [xkey, "ccw_t", "cvT"], ["cvT"])
        OP("act", lambda e: e.activation(out=aT[:, :, 0:N], in_=cvT[:, :, 0:N], func=AF.Square), ["cvT"], ["aT"])
        for c8 in range(8):
            OP("pe", lambda e, c8=c8: e.matmul(PB[4][:, 0:N], lhsT=ones[:], rhs=cvT[:, c8, 0:N], start=(c8 == 0), stop=(c8 == 7)), ["ones", "cvT"], ["pb4"])
        for c8 in range(8):
            OP("pe", lambda e, c8=c8: e.matmul(PB[5][:, 0:N], lhsT=ones[:], rhs=aT[:, c8, 0:N], start=(c8 == 0), stop=(c8 == 7)), ["ones", "aT"], ["pb5"])
        mean, var = stat[:, 0, 0:N], stat[:, 1, 0:N]
        OP("dve", lambda e: e.tensor_scalar(out=mean, in0=PB[4][:, 0:N], scalar1=1.0 / 1024, scalar2=None, op0=ALU.mult), ["pb4"], ["stat"])
        OP("dve", lambda e: e.tensor_scalar(out=var, in0=PB[5][:, 0:N], scalar1=1.0 / 1024, scalar2=None, op0=ALU.mult), ["pb5"], ["stat"])
        OP("dve", lambda e: e.tensor_tensor(out=stat[:, 2, 0:N], in0=mean, in1=mean, op=ALU.mult), ["stat"], ["stat"])
        OP("dve", lambda e: e.tensor_tensor(out=var, in0=var, in1=stat[:, 2, 0:N], op=ALU.subtract), ["stat"], ["stat"])
        OP("dve", lambda e: e.tensor_scalar(out=var, in0=var, scalar1=EPS, scalar2=None, op0=ALU.add), ["stat"], ["stat"])
        OP("act", lambda e: e.activation(out=var, in_=var, func=AF.Sqrt), ["stat"], ["stat"])
        OP("dve", lambda e: e.reciprocal(out=var, in_=var), ["stat"], ["stat"])
        OP("dve", lambda e: e.tensor_tensor(out=cvT[:, :, 0:N], in0=cvT[:, :, 0:N], in1=bc(mean.unsqueeze(1), [128, 8, N]), op=ALU.subtract),
           ["cvT", "stat"], ["cvT"])
        OP("dve", lambda e: e.tensor_tensor(out=cvT[:, :, 0:N], in0=cvT[:, :, 0:N], in1=bc(var.unsqueeze(1), [128, 8, N]), op=ALU.mult),
           ["cvT", "stat"], ["cvT"])
        for c8 in range(8):
            OP("act", lambda e, c8=c8: e.activation(out=cactT[:, c8, 0:N], in_=cvT[:, c8, 0:N], func=AF.Silu, bias=clb_t[:, c8:c8 + 1],
                                                    scale=clg_t[:, c8:c8 + 1]), ["cvT", "clb_t", "clg_t"], ["cactT"])

        if STAGE == "conf":
            return
        PP[0] = 2
        for pn in range(2):
            pump(PUMP)
            vbf, wkey = load_w(w_in, 8, S_CONF + pn * 512, 512)
            for cc in range(4):
                ps, pskey = PB[cc % 2], "pb%d" % (cc % 2)
                proj_fm(vbf, wkey, cc, ps, pskey)
                OP("act", lambda e, c=pn * 4 + cc, ps=ps: e.copy(out=qT[:, c, 0:N], in_=ps[:, 0:N]), [pskey], ["qT"])
                if samp:
                    OP("act", lambda e, c=pn * 4 + cc, ps=ps: e.copy(out=cvT[:, c, 0:N], in_=ps[:, 0:N]), [pskey], ["cvT"])
        if not samp:
            for h in range(4):
                pump(3)
                for c2 in range(2):
                    OP("pe", lambda e, h=h, c2=c2: e.matmul(PB[2][:, 0:256], lhsT=qT[:, 2 * h + c2, :], rhs=KT[:, 2 * h + c2, :],
                                                            start=(c2 == 0), stop=(c2 == 1)), ["qT", "KT"], ["pb2"])
                OP("dve", lambda e: e.tensor_reduce(out=col[:, 4:5], in_=PB[2][:, 0:256], axis=AX.X, op=ALU.max), ["pb2"], ["col"])
                OP("dve", lambda e: e.tensor_scalar(out=col[:, 4:5], in0=col[:, 4:5], scalar1=-1.0 / 16, scalar2=None, op0=ALU.mult), ["col"], ["col"])
                OP("act", lambda e: e.activation(out=att[:, 0, :], in_=PB[2][:, 0:256], func=AF.Exp, bias=col[:, 4:5], scale=1.0 / 16,
                                                 accum_out=col[:, 5:6]), ["pb2", "col"], ["att", "col"])
                OP("dve", lambda e: e.reciprocal(out=col[:, 5:6], in_=col[:, 5:6]), ["col"], ["col"])
                OP("dve", lambda e: e.tensor_scalar(out=att[:, 1, :], in0=att[:, 0, :], scalar1=col[:, 5:6], scalar2=None, op0=ALU.mult), ["att", "col"], ["att"])
                for mc in range(2):
                    OP("pe", lambda e, mc=mc: e.transpose(out=PB[3][:, mc * 128:(mc + 1) * 128], in_=att[:, 1, mc * 128:(mc + 1) * 128], identity=ident[:]),
                       ["att", "ident"], ["pb3"])
                OP("act", lambda e: e.copy(out=PnT[:, :, :].rearrange("p a b -> p (a b)"), in_=PB[3][:, 0:256]), ["pb3"], ["PnT"])
                for c2 in range(2):
                    for mc in range(2):
                        OP("pe", lambda e, h=h, c2=c2, mc=mc: e.matmul(PB[4][:, 0:128], lhsT=Vb[:, mc, (2 * h + c2) * 128:(2 * h + c2 + 1) * 128],
                                                                      rhs=PnT[:, mc, :], start=(mc == 0), stop=(mc == 1)), ["Vb", "PnT"], ["pb4"])
                    OP("act", lambda e, h=h, c2=c2: e.copy(out=oT[:, 2 * h + c2, :], in_=PB[4][:, 0:128]), ["pb4"], ["oT"])
        else:
            for c in range(8):
                OP("pe", lambda e, c=c: e.transpose(out=PB[2][0:NS, c * 128:(c + 1) * 128] if c < 4 else PB[3][0:NS, (c - 4) * 128:(c - 3) * 128],
                                                    in_=cvT[:, c, 0:NS], identity=ident[:]), ["cvT", "ident"], ["pb2", "pb3"])
            OP("act", lambda e: e.copy(out=otok[:, 0:512], in_=PB[2][0:NS, :]), ["pb2"], ["otok"])
            OP("act", lambda e: e.copy(out=otok[:, 512:1024], in_=PB[3][0:NS, :]), ["pb3"], ["otok"])
            S.dma(lambda e: e.dma_start(out=scr_q, in_=otok[:, :]), reads=["otok"], writes=["scr_q"])
            for b in range(NS):
                S.dma(lambda e, b=b: e.dma_start(out=qb[:, :], in_=scr_q[b:b + 1, :].partition_broadcast(128)), reads=["scr_q"], writes=["qb"])
                pump(4)
                Kb_, kk_ = (Ks, "Ks") if b % 2 == 0 else (Vs, "Vs")
                S.dma(lambda e, b=b, Kb_=Kb_: e.dma_start(out=Kb_[:, :, :], in_=ck[b].rearrange("(mc p) d -> p mc d", p=128)), writes=[kk_])
                OP("dve", lambda e, Kb_=Kb_: e.tensor_tensor(out=Kb_[:, :, :], in0=Kb_[:, :, :], in1=bc(qb[:, :].unsqueeze(1), [128, 2, 1024]), op=ALU.mult),
                   [kk_, "qb"], [kk_])
                OP("dve", lambda e, b=b, Kb_=Kb_: e.tensor_reduce(out=Sall[:, b, :], in_=Kb_[:, :, :].rearrange("p mc (h d) -> p (mc h) d", d=256), axis=AX.X,
                                                                  op=ALU.add), [kk_], ["Sall"])
            OP("pe", lambda e: e.transpose(out=PB[2][:, 0:128], in_=Sall[:, :, :].rearrange("p b c -> p (b c)"), identity=ident[:]), ["Sall", "ident"], ["pb2"])
            OP("dve", lambda e: e.tensor_reduce(out=col[:, 6:7], in_=PB[2][:, 0:128], axis=AX.X, op=ALU.max), ["pb2"], ["col"])
            OP("pe", lambda e: e.transpose(out=PB[3][0:1, 0:128], in_=col[:, 6:7], identity=ident[:]), ["col", "ident"], ["pb3"])
            OP("dve", lambda e: e.tensor_reduce(out=stat[0:1, 3, 0:NS], in_=PB[3][0:1, 0:128].rearrange("p (b c) -> p b c", c=8), axis=AX.X, op=ALU.max),
               ["pb3"], ["stat"])
            OP("pe", lambda e: e.matmul(PB[2][:, 256:256 + NS], lhsT=ones[0:1, :], rhs=stat[0:1, 3, 0:NS], start=True, stop=True), ["ones", "stat"], ["pb2"])
            OP("dve", lambda e: e.tensor_tensor(out=Eall[:, :, :], in0=Sall[:, :, :], in1=bc(PB[2][:, 256:256 + NS].unsqueeze(2), [128, NS, 8]),
                                                op=ALU.subtract), ["Sall", "pb2"], ["Eall"])
            OP("act", lambda e: e.activation(out=Eall[:, :, :], in_=Eall[:, :, :], func=AF.Exp, scale=1.0 / 16), ["Eall"], ["Eall"])
            OP("pe", lambda e: e.matmul(PB[3][:, 0:128], lhsT=ones[:], rhs=Eall[:, :, :].rearrange("p b c -> p (b c)"), start=True, stop=True),
               ["ones", "Eall"], ["pb3"])
            den = Sall[:, :, 0:4]
            pd = PB[3][:, 0:128].rearrange("p (b mc h) -> p b mc h", mc=2, h=4)
            OP("dve", lambda e: e.tensor_copy(out=den, in_=pd[:, :, 0, :]), ["pb3"], ["Sall"])
            OP("dve", lambda e: e.tensor_tensor(out=den, in0=den, in1=pd[:, :, 1, :], op=ALU.add), ["pb3", "Sall"], ["Sall"])
            OP("dve", lambda e: e.reciprocal(out=den, in_=den), ["Sall"], ["Sall"])
            OP("dve", lambda e: e.tensor_tensor(out=Eall[:, :, :].rearrange("p b (mc h) -> p b mc h", h=4),
                                                in0=Eall[:, :, :].rearrange("p b (mc h) -> p b mc h", h=4),
                                                in1=bc(den.unsqueeze(2), [128, NS, 2, 4]), op=ALU.mult), ["Eall", "Sall"], ["Eall"])
            for b in range(NS):
                pump(4)
                Vb_, vk_ = (Ks, "Ks") if b % 2 == 0 else (Vs, "Vs")
                S.dma(lambda e, b=b, Vb_=Vb_: e.dma_start(out=Vb_[:, :, :], in_=cv[b].rearrange("(mc p) d -> p mc d", p=128)), writes=[vk_])
                for hf in range(2):
                    for mc in range(2):
                        OP("pe", lambda e, b=b, hf=hf, mc=mc, Vb_=Vb_: e.matmul(PB[4 + hf][0:4, :], lhsT=Eall[:, b, mc * 4:(mc + 1) * 4],
                                                                      rhs=Vb_[:, mc, hf * 512:(hf + 1) * 512], start=(mc == 0), stop=(mc == 1)),
                           ["Eall", vk_], ["pb%d" % (4 + hf)])
                    OP("act", lambda e, hf=hf: e.copy(out=o4[:, hf * 512:(hf + 1) * 512], in_=PB[4 + hf][0:4, :]), ["pb%d" % (4 + hf)], ["o4"])
                S.dma(lambda e, b=b: e.dma_start(out=scr_o[b], in_=o4[:, :]), reads=["o4"], writes=["scr_o"])
            for h in range(4):
                S.dma(lambda e, h=h: e.dma_start(out=otok[:, h * 256:(h + 1) * 256], in_=scr_o[:, h, h * 256:(h + 1) * 256]), reads=["scr_o"], writes=["otok"])
            for c in range(8):
                OP("pe", lambda e, c=c: e.transpose(out=PB[4][:, 0:NS], in_=otok[:, c * 128:(c + 1) * 128], identity=ident[0:NS, 0:NS]),
                   ["otok", "ident"], ["pb4"])
                OP("act", lambda e, c=c: e.copy(out=oT[:, c, 0:NS], in_=PB[4][:, 0:NS]), ["pb4"], ["oT"])

        if STAGE == "attn":
            return
        pump(PUMP)
        for br in range(3):
            for pn in range(2):
                pump(PUMP)
                vbf, wkey = load_w(w_in, 8, S_MEMQ + br * 1024 + pn * 512, 512)
                for cc in range(4):
                    ps, pskey = PB[cc % 2], "pb%d" % (cc % 2)
                    proj_fm(vbf, wkey, cc, ps, pskey)
                    OP("act", lambda e, c=pn * 4 + cc, ps=ps: e.activation(out=gbr[:, c, 0:N], in_=ps[:, 0:N], func=AF.Sigmoid), [pskey], ["gbr"])
            wd, kcn, src, skey = ((wso, 16, ynT, "ynT"), (wco, 8, cactT, "cactT"), (wmo, 8, oT, "oT"))[br]
            pw = 4096 // kcn
            for pn in range(1024 // pw):
                pump(PUMP)
                vbf, wkey = load_w(wd, kcn, pn * pw, pw)
                for cc in range(pw // 128):
                    dch = pn * (pw // 128) + cc
                    ps, pskey = PB[2 + cc % 2], "pb%d" % (2 + cc % 2)
                    for kc in range(kcn):
                        OP("pe", lambda e, kc=kc, cc=cc, vbf=vbf, ps=ps, src=src, kcn=kcn: e.matmul(ps[:, 0:N], lhsT=vbf[:, kc, cc * 128:(cc + 1) * 128],
                                                                                          rhs=src[:, kc, 0:N], start=(kc == 0), stop=(kc == kcn - 1)),
                           [wkey, skey], [pskey])
                    if br == 0:
                        OP("dve", lambda e, dch=dch, ps=ps: e.tensor_tensor(out=mrg[:, dch, 0:N], in0=ps[:, 0:N], in1=gbr[:, dch, 0:N], op=ALU.mult),
                           [pskey, "gbr"], ["mrg"])
                    else:
                        OP("dve", lambda e, dch=dch, ps=ps: e.tensor_tensor(out=gbr[:, dch, 0:N], in0=ps[:, 0:N], in1=gbr[:, dch, 0:N], op=ALU.mult),
                           [pskey, "gbr"], ["gbr"])
                        OP("pool", lambda e, dch=dch: e.tensor_tensor(out=mrg[:, dch, 0:N], in0=mrg[:, dch, 0:N], in1=gbr[:, dch, 0:N], op=ALU.add),
                           ["mrg", "gbr"], ["mrg"])
        OP("act", lambda e: e.copy(out=mrgb[:, :, 0:N], in_=mrg[:, :, 0:N]), ["mrg"], ["mrgb"])
        for pn in range(2):
            pump(PUMP)
            vbf, wkey = load_w(wout, 8, pn * 512, 512)
            ps, pskey = PB[pn], "pb%d" % pn
            for kc in range(8):
                OP("pe", lambda e, kc=kc, vbf=vbf, ps=ps: e.matmul(ps[0:N, :], lhsT=mrgb[:, kc, 0:N], rhs=vbf[:, kc, :], start=(kc == 0), stop=(kc == 7)),
                   [wkey, "mrgb"], [pskey])
            OP("dve", lambda e, pn=pn, ps=ps: e.scalar_tensor_tensor(out=h1[0:N, pn * 512:(pn + 1) * 512], in0=xres[0:N, pn * 512:(pn + 1) * 512],
                                                                     scalar=ALPHA, in1=ps[0:N, :], op0=ALU.mult, op1=ALU.add), [pskey, "xres"], ["h1"])
        ln_tok(h1, x1, 0, N)
        for c in range(8):
            pt, ptkey = PB[4 + c % 2], "pb%d" % (4 + c % 2)
            OP("pe", lambda e, c=c, pt=pt: e.transpose(out=pt[:, 0:N], in_=x1[0:N, c * 128:(c + 1) * 128], identity=ident[0:N, 0:N]), ["x1", "ident"], [ptkey])
            OP("act", lambda e, c=c, pt=pt: e.copy(out=x1T[:, c, 0:N], in_=pt[:, 0:N]), [ptkey], ["x1T"])

        if STAGE == "ln1":
            return
        for pn in range(4):
            vbf, wkey = load_w(wq, 8, pn * 512, 512)
            for cc in range(4):
                pump(2)
                ps, pskey = PB[cc % 2], "pb%d" % (cc % 2)
                for kc in range(8):
                    OP("pe", lambda e, kc=kc, cc=cc, vbf=vbf, ps=ps: e.matmul(ps[:, 0:N], lhsT=vbf[:, kc, cc * 128:(cc + 1) * 128], rhs=x1T[:, kc, 0:N],
                                                                             start=(kc == 0), stop=(kc == 7)), [wkey, "x1T"], [pskey])
                OP("act", lambda e, c=pn * 4 + cc, ps=ps: e.copy(out=qpT[:, c, 0:N], in_=ps[:, 0:N]), [pskey], ["qpT"])
        for c in range(16):
            OP("pe", lambda e, c=c: e.matmul(PB[2 + c // 4][0:N, (c % 4) * 128:(c % 4 + 1) * 128], lhsT=qpT[:, c, 0:N], rhs=skTb[:, c, :],
                                             start=True, stop=True), ["qpT", "skTb"], ["pb%d" % (2 + c // 4)])
        for q in range(4):
            OP("act", lambda e, q=q: e.copy(out=R[0:N, q * 512:(q + 1) * 512], in_=PB[2 + q][0:N, :]), ["pb%d" % (2 + q)], ["R"])
        sc_, scw_ = R[0:N, 0:2048], R[0:N, 2048:4096]
        for c in range(16):
            cs = slice(c * 128, (c + 1) * 128)
            OP("dve", lambda e, c=c, cs=cs: e.max(out=top[0:N, c, 0:8], in_=sc_[:, cs]), ["R"], ["top"])
            OP("dve", lambda e, c=c, cs=cs: e.match_replace(out=scw_[:, cs], in_to_replace=top[0:N, c, 0:8], in_values=sc_[:, cs], imm_value=-1e30),
               ["R", "top"], ["R2"])
            OP("dve", lambda e, c=c, cs=cs: e.max(out=top[0:N, c, 8:16], in_=scw_[:, cs]), ["R2"], ["top"])
            OP("dve", lambda e, c=c, cs=cs: e.max_index(out=idxu[0:N, c, 0:8], in_max=top[0:N, c, 0:8], in_values=sc_[:, cs]), ["R", "top"], ["idxu"])
            OP("dve", lambda e, c=c, cs=cs: e.max_index(out=idxu[0:N, c, 8:16], in_max=top[0:N, c, 8:16], in_values=sc_[:, cs]), ["R", "top"], ["idxu"])
        OP("dve", lambda e: e.tensor_copy(out=idxf[0:N, :, :], in_=idxu[0:N, :, :]), ["idxu"], ["idxf"])
        topv = top[0:N, :, :].rearrange("p (h two) k -> p h two k", two=2)
        idxv = idxf[0:N, :, :].rearrange("p (h two) k -> p h two k", two=2)
        cand = R[0:N, 0:2048].rearrange("p (h a b) -> p h a b", h=8, a=16)
        candw = R[0:N, 2048:4096]
        OP("dve", lambda e: e.tensor_tensor(out=cand, in0=bc(topv[:, :, 0, :].unsqueeze(3), [N, 8, 16, 16]), in1=bc(topv[:, :, 1, :].unsqueeze(2), [N, 8, 16, 16]),
                                            op=ALU.add), ["top"], ["R"])
        for h in range(8):
            cs = slice(h * 256, (h + 1) * 256)
            OP("dve", lambda e, h=h, cs=cs: e.max(out=best[0:N, h, 0:8], in_=sc_[:, cs]), ["R"], ["best"])
            OP("dve", lambda e, h=h, cs=cs: e.match_replace(out=candw[:, cs], in_to_replace=best[0:N, h, 0:8], in_values=sc_[:, cs], imm_value=-1e30),
               ["R", "best"], ["R2"])
            OP("dve", lambda e, h=h, cs=cs: e.max(out=best[0:N, h, 8:16], in_=candw[:, cs]), ["R2"], ["best"])
            OP("dve", lambda e, h=h, cs=cs: e.max_index(out=pos[0:N, h, 0:8], in_max=best[0:N, h, 0:8], in_values=sc_[:, cs]), ["R", "best"], ["pos"])
            OP("dve", lambda e, h=h, cs=cs: e.max_index(out=pos[0:N, h, 8:16], in_max=best[0:N, h, 8:16], in_values=sc_[:, cs]), ["R", "best"], ["pos"])
        posf = pos[0:N, :, :].rearrange("p h k -> p (h k)")
        OP("dve", lambda e: e.tensor_single_scalar(out=pab[0:N, 0, :], in_=posf, scalar=4, op=ALU.logical_shift_right), ["pos"], ["pab"])
        OP("dve", lambda e: e.tensor_single_scalar(out=pab[0:N, 1, :], in_=posf, scalar=15, op=ALU.bitwise_and), ["pos"], ["pab"])
        OP("dve", lambda e: e.tensor_copy(out=pabf[0:N, :, :], in_=pab[0:N, :, :]), ["pab"], ["pabf"])
        for two in range(2):
            m4 = selw[0:N, :].rearrange("p (h k a) -> p h k a", h=8, k=16)
            OP("dve", lambda e, two=two, m4=m4: e.tensor_tensor(out=m4, in0=bc(pabf[0:N, two, :].rearrange("p (h k) -> p h k", k=16).unsqueeze(3), [N, 8, 16, 16]),
                                                               in1=bc(iota16[0:N, :].unsqueeze(1).unsqueeze(1), [N, 8, 16, 16]), op=ALU.is_equal),
               ["pabf", "iota16"], ["selw"])
            OP("dve", lambda e, two=two, m4=m4: e.tensor_tensor(out=m4, in0=m4, in1=bc(idxv[:, :, two, :].unsqueeze(2), [N, 8, 16, 16]), op=ALU.mult),
               ["selw", "idxf"], ["selw"])
            OP("dve", lambda e, two=two: e.tensor_reduce(out=ids[0:N, two, :], in_=selw[0:N, :].rearrange("p (hk a) -> p hk a", a=16), axis=AX.X, op=ALU.add),
               ["selw"], ["ids"])
        OP("dve", lambda e: e.scalar_tensor_tensor(out=ids[0:N, 0, :], in0=ids[0:N, 0, :], scalar=128.0, in1=ids[0:N, 1, :], op0=ALU.mult, op1=ALU.add),
           ["ids"], ["ids"])
        gwt = pabf[0:N, 0, :]
        g3 = gwt.rearrange("p (h k) -> p h k", k=16)
        OP("dve", lambda e: e.tensor_tensor(out=g3, in0=best[0:N, :, :], in1=bc(best[0:N, :, 0:1], [N, 8, 16]), op=ALU.subtract), ["best", "pabf"], ["pabf"])
        OP("act", lambda e: e.activation(out=gwt, in_=gwt, func=AF.Exp), ["pabf"], ["pabf"])
        OP("dve", lambda e: e.tensor_reduce(out=col[0:N, 8:16], in_=g3, axis=AX.X, op=ALU.add), ["pabf"], ["col"])
        OP("dve", lambda e: e.reciprocal(out=col[0:N, 8:16], in_=col[0:N, 8:16]), ["col"], ["col"])
        OP("dve", lambda e: e.tensor_tensor(out=g3, in0=g3, in1=bc(col[0:N, 8:16].unsqueeze(2), [N, 8, 16]), op=ALU.mult), ["pabf", "col"], ["pabf"])
        pump(10 ** 6)
        OP("act", lambda e: e.copy(out=x1p[0:N, :], in_=x1[0:N, :]), ["x1"], ["x1p"])
        OP("act", lambda e: e.copy(out=x1pb[0:N, :], in_=x1[0:N, :]), ["x1"], ["x1pb"])
        OP("dve", lambda e: e.tensor_copy(out=idi[0:N, :], in_=ids[0:N, 0, :]), ["ids"], ["idi"])
        OP("act", lambda e: e.copy(out=gw[0:N, :], in_=gwt), ["pabf"], ["gw"])
        if STAGE == "topk":
            return
        return

    def gen_peer(blk):
        samp = blk == 16
        N = NS if samp else 128
        t0 = 0 if samp else blk * 128
        GRP = 16
        ring = [0]
        for g0 in range(0, 128, GRP):
            gi_ = g0 // GRP
            dk, ck = "dots%d" % (gi_ % 2), "coef%d" % (gi_ % 2)
            for s_ in range(g0, g0 + GRP):
                ri = ring[0] % NG
                ring[0] += 1
                Gs, gkey = G[ri], "G%d" % ri
                S.dma(lambda e, s_=s_, Gs=Gs: e.indirect_dma_start(out=Gs[0:N, :], out_offset=None, in_=pu16[:, :],
                                                                   in_offset=bass.IndirectOffsetOnAxis(ap=idi[0:N, s_:s_ + 1], axis=0)),
                      reads=["idi"] + TABKEYS, writes=[gkey], e="pool")
                OP("dve", lambda e, s_=s_, Gs=Gs: e.scalar_tensor_tensor(out=Gs[0:N, :], in0=Gs[0:N, :], scalar=1.0, in1=x1pb[0:N, :], op0=ALU.mult,
                                                                         op1=ALU.mult, accum_out=dots[0:N, s_:s_ + 1]), [gkey, "x1pb"], [gkey, dk])
                yield
            OP("act", lambda e, g0=g0: e.activation(out=coef[0:N, g0:g0 + GRP], in_=dots[0:N, g0:g0 + GRP], func=AF.Gelu), [dk], [ck])
            OP("dve", lambda e, g0=g0: e.tensor_tensor(out=coef[0:N, g0:g0 + GRP], in0=coef[0:N, g0:g0 + GRP], in1=gw[0:N, g0:g0 + GRP], op=ALU.mult),
               [ck, "gw"], [ck])
            for s_ in range(g0, g0 + GRP):
                ri = ring[0] % NG
                ring[0] += 1
                Gs, gkey = G[ri], "G%d" % ri
                dv, dvkey = dgv[s_ % 2], "dgv%d" % (s_ % 2)
                S.dma(lambda e, s_=s_, Gs=Gs: e.indirect_dma_start(out=Gs[0:N, :], out_offset=None, in_=pv16[:, :],
                                                                   in_offset=bass.IndirectOffsetOnAxis(ap=idi[0:N, s_:s_ + 1], axis=0)),
                      reads=["idi"] + TABKEYS, writes=[gkey], e="pool")
                OP("act", lambda e, s_=s_, dv=dv: e.activation(out=dv[0:N, 0:N], in_=ident[0:N, 0:N], func=AF.Copy, scale=coef[0:N, s_:s_ + 1]),
                   ["ident", ck], [dvkey])
                for hf in range(2):
                    OP("pe", lambda e, s_=s_, hf=hf, dv=dv, Gs=Gs: e.matmul(PB[6 + hf][0:N, :], lhsT=dv[0:N, 0:N], rhs=Gs[0:N, hf * 512:(hf + 1) * 512],
                                                                           start=(s_ == 0), stop=(s_ == 127)), [dvkey, gkey], ["pb%d" % (6 + hf)])
                yield
        for hf in range(2):
            OP("dve", lambda e, hf=hf: e.scalar_tensor_tensor(out=x1p[0:N, hf * 512:(hf + 1) * 512], in0=x1p[0:N, hf * 512:(hf + 1) * 512], scalar=ALPHA,
                                                              in1=PB[6 + hf][0:N, :], op0=ALU.mult, op1=ALU.add), ["pb%d" % (6 + hf), "x1p"], ["x1p"])
        ln_tok(x1p, x1p, 2, N, scratch=G[0], skey="G0", key="x1p")
        S.dma(lambda e: e.dma_start(out=(ys if samp else yp[t0:t0 + 128, :]), in_=x1p[0:N, :]), reads=["x1p"], writes=["yout"])
        yield

    pend = [None]

    def pump(k):
        for _ in range(k):
            if pend[0] is None:
                return
            try:
                next(pend[0])
            except StopIteration:
                pend[0] = None
                return

    def emit_tails():
        for (tl, tkey, nch, ncol, o_p, o_s, st_in, W) in ((tail_s, "tail_s", 24, 3, o_sconv_p, o_sconv_s, st_sconv, 3), (tail_c, "tail_c", 8, 30, o_cconv_p, o_cconv_s, st_cconv, 30)):
            pass

    blist = list(range(17)) if stop == "all" else [int(x) for x in str(stop).split("+") if x != "0x"]
    if str(stop).isdigit():
        blist = list(range(int(stop)))
    for blk in blist:
        if blk == 15:
            pass
        emit_block(blk)
        if STAGE == '':
            pend[0] = gen_peer(blk)
        if blk == 15:
            for (tl, tkey, nch, ncol, o_p) in ((tail_s, "tail_s", 24, 3, o_sconv_p), (tail_c, "tail_c", 8, 30, o_cconv_p)):
                for j in range(nch):
                    OP("pe", lambda e, tl=tl, j=j, ncol=ncol: e.transpose(out=PB[j % 2][0:ncol, 0:128], in_=tl[:, j, 0:ncol], identity=ident[:]),
                       [tkey, "ident"], ["pb%d" % (j % 2)])
                    OP("act", lambda e, j=j, ncol=ncol: e.copy(out=fm32[j % 2][0:ncol, 0:128], in_=PB[j % 2][0:ncol, 0:128]), ["pb%d" % (j % 2)], ["fm32_%d" % (j % 2)])
                    S.dma(lambda e, j=j, ncol=ncol, o_p=o_p: e.dma_start(out=o_p[:, j * 128:(j + 1) * 128], in_=fm32[j % 2][0:ncol, 0:128]),
                          reads=["fm32_%d" % (j % 2)], writes=["otail"])
        if blk == 16:
            for (tl, tkey, nch, W, o_s, st_in) in ((tail_s, "tail_s", 24, 3, o_sconv_s, st_sconv), (tail_c, "tail_c", 8, 30, o_cconv_s, st_cconv)):
                S.dma(lambda e, o_s=o_s, st_in=st_in, W=W: e.dma_start(out=o_s[:, 0:W - 1, :], in_=st_in[:, 1:W, :]), writes=["otail_s"])
                for j in range(nch):
                    OP("pe", lambda e, tl=tl, j=j: e.transpose(out=PB[j % 2][0:NS, 0:128], in_=tl[:, j, 0:NS], identity=ident[:]),
                       [tkey, "ident"], ["pb%d" % (j % 2)])
                    OP("act", lambda e, j=j: e.copy(out=fm32[j % 2][0:NS, 0:128], in_=PB[j % 2][0:NS, 0:128]), ["pb%d" % (j % 2)], ["fm32_%d" % (j % 2)])
                    S.dma(lambda e, j=j, o_s=o_s, W=W: e.dma_start(out=o_s[:, W - 1, j * 128:(j + 1) * 128], in_=fm32[j % 2][0:NS, 0:128]),
                          reads=["fm32_%d" % (j % 2)], writes=["otail_s2"])
    pump(10 ** 6)
    for nm in dbg:
        t_, key = {"y32": (y32, "y32"), "x1": (x1, "x1"), "xtok": (xtok, "xtok"), "h1": (h1, "h1"), "mrg": (mrg, "mrg"), "hT": (hT, "hT"),
                   "cvT": (cvT, "cvT"), "oT": (oT, "oT"), "ynT": (ynT, "ynT"), "dots": (dots, "dots"), "ids": (ids, "ids"), "gw": (gw, "gw"),
                   "coef": (coef, "coef"), "dtt": (dtt, "dtt"), "cactT": (cactT, "cactT"), "qsc": (qsc, "qsc"), "hp2": (hp2, "hp2"), "hp": (hp, "hp"), "sQ": (sQ, "sQ"), "sB": (sB, "sB"), "sC": (sC, "sC")}[nm]
        shp = list(t_[:].shape)
        dd = dscr("dbg_" + nm, shp, F32)
        if t_[:].dtype != F32:
            n_ = int(np.prod(shp[1:]))
            t32 = R[:, 0:n_].rearrange("p (a b) -> p a b", b=shp[-1]) if len(shp) == 3 else R[:, 0:n_]
            OP("dve", lambda e, t_=t_, t32=t32: e.tensor_copy(out=t32, in_=t_[:]), [key], ["R"])
            S.dma(lambda e, dd=dd, t32=t32: e.dma_start(out=dd, in_=t32), reads=["R"])
        else:
            S.dma(lambda e, dd=dd, t_=t_: e.dma_start(out=dd, in_=t_[:]), reads=[key])
    S.emit()
    st.close()
    return nc


def _fm(v, nch):
    v = np.asarray(v)
    return np.ascontiguousarray(np.moveaxis(v.reshape((nch, 128) + v.shape[1:]), 0, 1))


def prep_inputs(inp, c):
    f = lambda a: np.ascontiguousarray(np.asarray(a, dtype=np.float32))
    L = 0
    sl = slice(NS * c, NS * (c + 1))
    m = {}
    m["xp"] = f(inp["x_prompt"][c]); m["xpT"] = f(inp["x_prompt"][c].T)
    m["xs"] = f(inp["x_sample"][sl, 0]); m["xsT"] = f(inp["x_sample"][sl, 0].T)
    m["w_in"] = f(inp["w_in"][L])
    m["scw"] = _fm(f(inp["ssd_conv_w"][L].T), 24); m["scb"] = _fm(f(inp["ssd_conv_b"][L]), 24)
    m["dtb"] = f(inp["ssd_dt_bias"][L][None]); m["alog"] = f(inp["ssd_a_log"][L][None]); m["dsk"] = f(inp["ssd_d"][L][None])
    m["nw"] = f(inp["ssd_norm_w"][L][None]); m["wso"] = f(inp["ssd_w_out"][L])
    m["ccw"] = _fm(f(inp["conf_conv_w"][L].T), 8); m["ccb"] = _fm(f(inp["conf_conv_b"][L]), 8)
    m["clg"] = _fm(f(inp["conf_ln_g"][L]), 8); m["clb"] = _fm(f(inp["conf_ln_b"][L]), 8)
    m["wco"] = f(inp["conf_w_out"][L]); m["wk"] = f(inp["mem_w_k"][L]); m["wv"] = f(inp["mem_w_v"][L])
    m["wmo"] = f(inp["mem_w_o"][L]); m["wout"] = f(inp["w_out"][L])
    for k in ("ln1_g", "ln1_b", "ln2_g", "ln2_b"):
        m[k.replace("_", "")] = f(inp[k][L][None])
    m["wq"] = f(inp["peer_w_q"][L])
    sk = np.asarray(inp["peer_sub_keys"][L]).reshape(16, 128, 128)
    m["skT"] = f(np.transpose(sk, (2, 0, 1)))
    m["pu"] = f(inp["peer_u"][L]); m["pv"] = f(inp["peer_v"][L])
    m["mempT"] = f(inp["mem_prompt"][c].T)
    m["st_ssd"] = f(np.asarray(inp["state_ssd"][L][sl]).reshape(NS, 128, 2048))
    sc = np.asarray(inp["state_ssd_conv"][L][sl])
    m["st_sconv"] = f(sc)
    m["st_sconvT"] = f(np.transpose(sc.reshape(NS, 3, 24, 128), (3, 2, 0, 1)))
    cc = np.asarray(inp["state_conf_conv"][L][sl])
    m["st_cconv"] = f(cc)
    m["st_cconvT"] = f(np.transpose(cc.reshape(NS, 30, 8, 128), (3, 2, 0, 1)))
    m["ck"] = f(np.asarray(inp["cache_mem_k"][L][sl]).reshape(NS, 256, D))
    m["cv"] = f(np.asarray(inp["cache_mem_v"][L][sl]).reshape(NS, 256, D))
    return m


_NC_CACHE = {}


def kernel(**inputs):
    if "nc" not in _NC_CACHE:
        _NC_CACHE["nc"] = build(stop="all")
    nc = _NC_CACHE["nc"]
    in_maps = [prep_inputs(inputs, c) for c in range(8)]
    res = run_bass_kernel_spmd(nc, in_maps, core_ids=list(range(8)))
    r = res.results
    st = lambda k: np.stack([np.asarray(r[c][k], dtype=np.float32) for c in range(8)])
    y_p = st("yp")
    y_s = st("ys").reshape(128, 1, D)
    ssd_p = st("o_ssd_p").reshape(1, 8, 32, 64, 128)
    sconv_p = st("o_sconv_p").reshape(1, 8, 3, 3072)
    cconv_p = st("o_cconv_p").reshape(1, 8, 30, D)
    k_p = st("o_k_p").reshape(1, 8, 256, 4, 256)
    v_p = st("o_v_p").reshape(1, 8, 256, 4, 256)
    ssd_s = st("o_ssd_s").reshape(1, 128, 32, 64, 128)
    sconv_s = st("o_sconv_s").reshape(1, 128, 3, 3072)
    cconv_s = st("o_cconv_s").reshape(1, 128, 30, D)
    return (y_p, y_s, ssd_p, sconv_p, cconv_p, k_p, v_p, ssd_s, sconv_s, cconv_s)
```

```python
import contextlib
import numpy as np
import concourse.bass as bass
import concourse.mybir as mybir
from concourse.bass_utils import run_bass_kernel_spmd

F32 = mybir.dt.float32
BF16 = mybir.dt.bfloat16
I32 = mybir.dt.int32
U32 = mybir.dt.uint32
AF = mybir.ActivationFunctionType
ALU = mybir.AluOpType
AX = mybir.AxisListType

D = 1024
T = 2048
NS = 16
ALPHA = 2.0 ** 0.25
EPS = 1e-5
S_Z, S_XBC, S_DT, S_CONF, S_MEMQ, D_IN = 2048, 5120, 5152, 7200, 8224, 11296


class Sched:
    ENGS = ("pe", "act", "dve", "pool", "sp")

    def __init__(self, nc, n_dma_slots=48, same_engine_sync=True):
        import os
        same_engine_sync = os.environ.get('SAMESYNC', '1') == '1'
        self.nc = nc
        self.q = {e: [] for e in self.ENGS}
        self.cnt = {e: 0 for e in self.ENGS}
        self.waited = {}
        self.same = same_engine_sync
        self.last_write = {}
        self.readers = {}
        self.n_slots = n_dma_slots
        self.slot_uses = [0] * n_dma_slots
        self.slot_rr = 0
        self.sw_rr = 0
        self.n_hw = n_dma_slots - 16

    def _deps(self, reads, writes):
        deps = []
        for b in reads:
            t = self.last_write.get(b)
            if t is not None:
                deps.append(t)
        for b in writes:
            t = self.last_write.get(b)
            if t is not None:
                deps.append(t)
            deps.extend(self.readers.get(b, ()))
        return deps

    def _commit(self, tok, reads, writes):
        for b in reads:
            self.readers.setdefault(b, []).append(tok)
        for b in writes:
            self.last_write[b] = tok
            self.readers[b] = []

    def _emit_waits(self, e, deps):
        need = {}
        for (kind, key, n) in deps:
            if kind == "eng" and key == e and (not self.same or e in ("pe", "sp")):
                continue
            k = (kind, key)
            if n > need.get(k, 0):
                need[k] = n
        for k, n in need.items():
            if self.waited.get((e, k), 0) >= n:
                continue
            self.waited[(e, k)] = n
            self.q[e].append(("wait", k, n))

    alias = {}

    sub = {}

    def _x(self, keys):
        out = []
        for k in keys:
            if "#" in k:
                base, i = k.split("#")
                al = self.alias.get(base, [base])
                assert len(al) == 1, k
                out.append(al[0] + "#" + i)
            else:
                for a_ in self.alias.get(k, [k]):
                    n = self.sub.get(a_)
                    if n:
                        out.extend("%s#%d" % (a_, i) for i in range(n))
                    else:
                        out.append(a_)
        return out

    frozen = False

    def op(self, e, fn, reads=(), writes=()):
        if self.frozen:
            return None
        reads, writes = self._x(reads), self._x(writes)
        deps = self._deps(reads, writes)
        self._emit_waits(e, deps)
        self.cnt[e] += 1
        tok = ("eng", e, self.cnt[e])
        self.q[e].append(("op", fn, None))
        self._commit(tok, reads, writes)
        return tok

    def dma(self, fn, reads=(), writes=(), e="sp"):
        if self.frozen:
            return None
        reads, writes = self._x(reads), self._x(writes)
        deps = self._deps(reads, writes)
        if e == "pool":
            s = self.n_hw + self.sw_rr
            self.sw_rr = (self.sw_rr + 1) % (self.n_slots - self.n_hw)
        else:
            s = self.slot_rr
            self.slot_rr = (self.slot_rr + 1) % self.n_hw
        if self.slot_uses[s] > 0:
            deps.append(("dma", s, self.slot_uses[s]))
        self._emit_waits(e, deps)
        self.slot_uses[s] += 1
        tok = ("dma", s, self.slot_uses[s])
        self.q[e].append(("dma", fn, s))
        self._commit(tok, reads, writes)
        return tok

    def emit(self):
        nc = self.nc
        deps = [("dma", s, u) for s, u in enumerate(self.slot_uses) if u > 0]
        self._emit_waits("sp", deps)
        with contextlib.ExitStack() as st:
            esem = {e: st.enter_context(nc.semaphore("s_" + e)) for e in self.ENGS}
            dsem = [st.enter_context(nc.semaphore("d_%d" % i)) for i in range(self.n_slots)]
            block = st.enter_context(nc.Block())

            def run(e, eng):
                for (kind, a, b) in self.q[e]:
                    if kind == "wait":
                        if a[0] == "eng":
                            eng.wait_ge(esem[a[1]], b)
                        else:
                            eng.wait_ge(dsem[a[1]], 16 * b)
                    elif kind == "op":
                        a(eng).then_inc(esem[e], 1)
                    else:
                        a(eng).then_inc(dsem[b], 16)

            @block.tensor
            def _(eng):
                run("pe", eng)

            @block.scalar
            def _(eng):
                run("act", eng)

            @block.vector
            def _(eng):
                run("dve", eng)

            @block.gpsimd
            def _(eng):
                run("pool", eng)

            @block.sync
            def _(eng):
                run("sp", eng)


def build(stop="all", dbg=()):
    nc = bass.Bass("TRN2", target_bir_lowering=False)
    S = Sched(nc)
    st = contextlib.ExitStack()
    dr = {}

    def din(name, shape, dt=F32):
        dr[name] = nc.dram_tensor(name, list(shape), dt, kind="ExternalInput").ap()
        return dr[name]

    def dout(name, shape, dt=F32):
        dr[name] = nc.dram_tensor(name, list(shape), dt, kind="ExternalOutput").ap()
        return dr[name]

    def dscr(name, shape, dt=F32):
        kind = "ExternalOutput" if name.startswith("dbg_") else "Internal"
        dr[name] = nc.dram_tensor(name, list(shape), dt, kind=kind).ap()
        return dr[name]

    def sb(name, shape, dt=F32):
        return st.enter_context(nc.sbuf_tensor(name, list(shape), dt))

    xp = din("xp", [T, D]); xpT = din("xpT", [D, T])
    xs = din("xs", [NS, D]); xsT = din("xsT", [D, NS])
    w_in = din("w_in", [D, D_IN])
    scw = din("scw", [128, 24, 4]); scb = din("scb", [128, 24])
    dtb = din("dtb", [1, 32]); alog = din("alog", [1, 32]); dsk = din("dsk", [1, 32])
    nw = din("nw", [1, 2048]); wso = din("wso", [2048, D])
    ccw = din("ccw", [128, 8, 31]); ccb = din("ccb", [128, 8]); clg = din("clg", [128, 8]); clb = din("clb", [128, 8])
    wco = din("wco", [D, D]); wk = din("wk", [D, D]); wv = din("wv", [D, D]); wmo = din("wmo", [D, D]); wout = din("wout", [D, D])
    ln1g = din("ln1g", [1, D]); ln1b = din("ln1b", [1, D]); ln2g = din("ln2g", [1, D]); ln2b = din("ln2b", [1, D])
    wq = din("wq", [D, 2048]); skT = din("skT", [128, 16, 128])
    pu = din("pu", [16384, D]); pv = din("pv", [16384, D])
    mempT = din("mempT", [D, 256])
    st_ssd = din("st_ssd", [NS, 128, 2048]); st_sconvT = din("st_sconvT", [128, 24, NS, 3]); st_sconv = din("st_sconv", [NS, 3, 3072])
    st_cconvT = din("st_cconvT", [128, 8, NS, 30]); st_cconv = din("st_cconv", [NS, 30, D])
    ck = din("ck", [NS, 256, D]); cv = din("cv", [NS, 256, D])

    yp = dout("yp", [T, D]); ys = dout("ys", [NS, D])
    o_ssd_p = dout("o_ssd_p", [2048, 128]); o_sconv_p = dout("o_sconv_p", [3, 3072]); o_cconv_p = dout("o_cconv_p", [30, D])
    o_k_p = dout("o_k_p", [256, D]); o_v_p = dout("o_v_p", [256, D])
    o_ssd_s = dout("o_ssd_s", [NS, 128, 2048]); o_sconv_s = dout("o_sconv_s", [NS, 3, 3072]); o_cconv_s = dout("o_cconv_s", [NS, 30, D])

    ident = sb("ident", [128, 128]); identb = sb("identb", [128, 128], BF16)
    ones = sb("ones", [128, 128]); tri = sb("tri", [128, 128]); sel_last = sb("sel_last", [128, 128])
    S.op("pool", lambda e: e.memset(ident[:], 0.0), writes=["ident"])
    S.op("pool", lambda e: e.affine_select(out=ident[:], in_=ident[:], pattern=[[-1, 128]], compare_op=ALU.not_equal,
                                           fill=1.0, base=0, channel_multiplier=1), reads=["ident"], writes=["ident"])
    S.op("pool", lambda e: e.tensor_copy(out=identb[:], in_=ident[:]), reads=["ident"], writes=["identb"])
    S.op("pool", lambda e: e.memset(ones[:], 1.0), writes=["ones"])
    S.op("pool", lambda e: e.affine_select(out=tri[:], in_=ones[:], pattern=[[1, 128]], compare_op=ALU.is_ge,
                                           fill=0.0, base=0, channel_multiplier=-1), reads=["ones"], writes=["tri"])
    S.op("pool", lambda e: e.affine_select(out=sel_last[:], in_=ones[:], pattern=[[0, 128]], compare_op=ALU.is_ge,
                                           fill=0.0, base=-127, channel_multiplier=1), reads=["ones"], writes=["sel_last"])

    scw_t = sb("scw_t", [128, 24, 4]); scb_t = sb("scb_t", [128, 24])
    ccw_t = sb("ccw_t", [128, 8, 31]); ccb_t = sb("ccb_t", [128, 8]); clg_t = sb("clg_t", [128, 8]); clb_t = sb("clb_t", [128, 8])
    for t_, d_, k_ in ((scw_t, scw, "scw_t"), (scb_t, scb, "scb_t"), (ccw_t, ccw, "ccw_t"), (ccb_t, ccb, "ccb_t"),
                       (clg_t, clg, "clg_t"), (clb_t, clb, "clb_t")):
        S.dma(lambda e, t_=t_, d_=d_: e.dma_start(out=t_[:], in_=d_), writes=[k_])
    dtb_t = sb("dtb_t", [128, 32]); a_t = sb("a_t", [128, 32]); dsk_t = sb("dsk_t", [128, 32])
    S.dma(lambda e: e.dma_start(out=dtb_t[:], in_=dtb.partition_broadcast(128)), writes=["dtb_t"])
    S.dma(lambda e: e.dma_start(out=a_t[:], in_=alog.partition_broadcast(128)), writes=["a_t"])
    S.dma(lambda e: e.dma_start(out=dsk_t[:], in_=dsk.partition_broadcast(128)), writes=["dsk_t"])
    S.op("act", lambda e: e.activation(out=a_t[:], in_=a_t[:], func=AF.Exp), reads=["a_t"], writes=["a_t"])
    S.op("dve", lambda e: e.tensor_scalar(out=a_t[:], in0=a_t[:], scalar1=-1.0, scalar2=None, op0=ALU.mult), reads=["a_t"], writes=["a_t"])

    PB = [st.enter_context(nc.psum_tensor("pb%d" % i, [128, 512], F32)) for i in range(8)]

    R = sb("R", [128, 4096]); MT = sb("MT", [128, 4096], BF16)
    S.alias = {"sB": ["aT", "cvT"], "sC": ["G4", "G5", "G6", "G7"], "h0_0": ["G8", "G9", "G10", "G11"], "h0_1": ["G8", "G9", "G10", "G11"], "Ks": ["G4", "G5", "G6", "G7"],
               "Vs": ["G8", "G9", "G10", "G11"],
               "xT32": ["R"],
               "qb": ["y32"], "selw": ["y32"], "skT32": ["R"], "mT32": ["R"], "kv32": ["xres"], "mTb": ["sz"], "otok": ["R"], "o4": ["R"],
               "x1": ["xtok"], "h1": ["xtok"], "qpT": ["xdt"], "mrgb": ["xdte"], "x1T": ["xdte"], "ynT": ["MT"], "cactT": ["MT"],
               "qT": ["MT"], "gbr": ["aT"], "mrg": ["cvT"], "wld0": ["G4", "G5", "G6", "G7", "G8", "G9", "G10", "G11"], "Rall": ["R", "R2"]}

    import os
    STAGE = os.environ.get('STAGE', '')

    def cut(tag):
        if STAGE.startswith(tag):
            S.frozen = True

    S.sub = {"xhc": 8, "aT": 8, "cvT": 8, "R": 16, "R2": 16, "hT": 4, "xdt": 16, "xdte": 16, "MT": 32, "dots": 32, "dg0": 4, "top": 16, "idxu": 16,
             "best": 8, "pos": 8}

    def OP(e, fn, r=(), w=()):
        return S.op(e, fn, reads=r, writes=w)

    def bc(ap, shape):
        return ap.to_broadcast(list(shape))

    wld = [sb("wld0", [128, 4096])] * 2
    wbf = [sb("wbf0", [128, 4096], BF16), sb("wbf1", [128, 4096], BF16)]
    wctr = [0]

    def load_w32(dram, kc_n, c0, ncols, r0=0):
        i = wctr[0] % 2
        wctr[0] += 1
        src = dram[r0:r0 + kc_n * 128, c0:c0 + ncols].rearrange("(kc p) n -> p kc n", p=128)
        v32 = wld[i][:, 0:kc_n * ncols].rearrange("p (kc n) -> p kc n", kc=kc_n)
        vbf = wbf[i][:, 0:kc_n * ncols].rearrange("p (kc n) -> p kc n", kc=kc_n)
        S.dma(lambda e: e.dma_start(out=v32, in_=src), writes=["wld0"])
        if wctr[0] % 2:
            OP("act", lambda e: e.copy(out=vbf, in_=v32), ["wld0"], ["wbf%d" % i])
        else:
            OP("dve", lambda e: e.tensor_copy(out=vbf, in_=v32), ["wld0"], ["wbf%d" % i])
        return vbf, "wbf%d" % i


    PANELS = ([("w_in", w_in, 8, S_Z + pn * 512, 512) for pn in range(6)] + [("w_in", w_in, 8, S_XBC, 32)]
              + [("w_in", w_in, 8, pn * 512, 512) for pn in range(4)] + [("w_in", w_in, 8, S_DT + pn * 512, 512) for pn in range(4)]
              + [("w_in", w_in, 8, S_CONF + pn * 512, 512) for pn in range(2)]
              + [("w_in", w_in, 8, S_MEMQ + br * 1024 + pn * 512, 512) for br in range(3) for pn in range(2)]
              + [("wso", wso, 16, pn * 256, 256) for pn in range(4)] + [("wco", wco, 8, pn * 512, 512) for pn in range(2)]
              + [("wmo", wmo, 8, pn * 512, 512) for pn in range(2)] + [("wout", wout, 8, pn * 512, 512) for pn in range(2)]
              + [("wq", wq, 8, pn * 512, 512) for pn in range(4)])
    wscr = dscr("wscr", [len(PANELS), 128, 4096], BF16)
    panel_id = {}
    for pi, (nm_, dram_, kc_n, c0, ncols) in enumerate(PANELS):
        panel_id[(nm_, c0)] = pi
        S.dma(lambda e, pi=pi, dram_=dram_, kc_n=kc_n, c0=c0, ncols=ncols: e.dma_start(
            out=wscr[pi][:, 0:kc_n * ncols].rearrange("p (kc n) -> p kc n", kc=kc_n),
            in_=dram_[0:kc_n * 128, c0:c0 + ncols].rearrange("(kc p) n -> p kc n", p=128)), writes=["wscr%d" % pi], e="pool")
    pu16 = dscr("pu16", [16384, D], BF16); pv16 = dscr("pv16", [16384, D], BF16)
    TABKEYS = []
    for ti, (src_, dst_) in enumerate(((pu, pu16), (pv, pv16))):
        for c in range(32):
            S.dma(lambda e, src_=src_, dst_=dst_, c=c: e.dma_start(out=dst_[c * 512:(c + 1) * 512, :], in_=src_[c * 512:(c + 1) * 512, :]),
                  writes=["tab%d_%d" % (ti, c)], e="pool")
            TABKEYS.append("tab%d_%d" % (ti, c))

    def load_w(dram, kc_n, c0, ncols, r0=0):
        pi = panel_id[(dram.name, c0)]
        i = wctr[0] % 2
        wctr[0] += 1
        n_ = kc_n * ncols
        S.dma(lambda e: e.dma_start(out=wbf[i][:, 0:n_], in_=wscr[pi][:, 0:n_]), reads=["wscr%d" % pi], writes=["wbf%d" % i])
        return wbf[i][:, 0:n_].rearrange("p (kc n) -> p kc n", kc=kc_n), "wbf%d" % i

    hist_s = sb("hist_s", [128, 24, 3], BF16); hist_c = sb("hist_c", [128, 8, 30], BF16)
    OP("pool", lambda e: e.memset(hist_s[:], 0.0), w=["hist_s"])
    OP("pool", lambda e: e.memset(hist_c[:], 0.0), w=["hist_c"])
    hT = sb("hT", [128, 2048]); hTb = sb("hTb", [128, 2048], BF16)
    OP("pool", lambda e: e.memset(hT[:], 0.0), w=["hT"])
    OP("pool", lambda e: e.memset(hTb[:], 0.0), w=["hTb"])
    tail_s = sb("tail_s", [128, 24, 16]); tail_c = sb("tail_c", [128, 8, 32])
    nw_t = sb("nw_t", [128, 2048]); ln_t = sb("ln_t", [128, 4, 1024])
    S.dma(lambda e: e.dma_start(out=nw_t[:], in_=nw.partition_broadcast(128)), writes=["nw_t"])
    for i_, d_ in enumerate((ln1g, ln1b, ln2g, ln2b)):
        S.dma(lambda e, i_=i_, d_=d_: e.dma_start(out=ln_t[:, i_, :], in_=d_.partition_broadcast(128)), writes=["ln_t"])
    cut("c1")
    skT32 = R[:, 0:2048].rearrange("p (a b) -> p a b", b=128); skTb = sb("skTb", [128, 16, 128], BF16)
    S.dma(lambda e: e.dma_start(out=skT32[:], in_=skT), writes=["skT32"])
    OP("pool", lambda e: e.tensor_copy(out=skTb[:], in_=skT32[:]), ["skT32"], ["skTb"])
    cut("c2")
    iota16 = sb("iota16", [128, 16])
    OP("pool", lambda e: e.iota(iota16[:], pattern=[[1, 16]], base=0, channel_multiplier=0, allow_small_or_imprecise_dtypes=True), w=["iota16"])

    cut("c3")
    dtt = sb("dtt", [128, 32]); sz = sb("sz", [128, 2048], BF16)
    xres = sb("xres", [128, 1024]); kv32 = xres
    mTb = sz[:, :].rearrange("p (a b) -> p a b", b=256)
    KT = sb("KT", [128, 8, 256], BF16); Vb = sb("Vb", [128, 2, 1024], BF16)
    mT32 = R[:, 0:2048].rearrange("p (a b) -> p a b", b=256)
    S.dma(lambda e: e.dma_start(out=mT32[:], in_=mempT.rearrange("(kc p) m -> p kc m", p=128)), writes=["mT32"])
    OP("dve", lambda e: e.tensor_copy(out=mTb[:], in_=mT32[:]), ["mT32"], ["mTb"])
    for pn in range(2):
        vbf, wkey = load_w32(wk, 8, pn * 512, 512)
        for cc in range(4):
            for kc in range(8):
                OP("pe", lambda e, kc=kc, cc=cc, vbf=vbf: e.matmul(PB[0][:, 0:256], lhsT=vbf[:, kc, cc * 128:(cc + 1) * 128], rhs=mTb[:, kc, :],
                                                                  start=(kc == 0), stop=(kc == 7)), [wkey, "mTb"], ["pb0"])
            OP("act", lambda e, c=pn * 4 + cc: e.copy(out=KT[:, c, :], in_=PB[0][:, 0:256]), ["pb0"], ["KT"])
    cut("c4")
    for wi, (wd, od) in enumerate(((wk, o_k_p), (wv, o_v_p))):
        for mt in range(2):
            for pn in range(2):
                vbf, wkey = load_w32(wd, 8, pn * 512, 512)
                for kc in range(8):
                    OP("pe", lambda e, kc=kc, vbf=vbf, mt=mt: e.matmul(PB[1][:, :], lhsT=mTb[:, kc, mt * 128:(mt + 1) * 128], rhs=vbf[:, kc, :],
                                                                      start=(kc == 0), stop=(kc == 7)), [wkey, "mTb"], ["pb1"])
                OP("act", lambda e, pn=pn: e.copy(out=kv32[:, pn * 512:(pn + 1) * 512], in_=PB[1][:, :]), ["pb1"], ["kv32"])
                if wi == 1 and "novb" not in STAGE:
                    OP("dve", lambda e, pn=pn, mt=mt: e.tensor_copy(out=Vb[:, mt, pn * 512:(pn + 1) * 512], in_=kv32[:, pn * 512:(pn + 1) * 512]), ["kv32"], ["Vb"])
            if "noout" not in STAGE:
                S.dma(lambda e, od=od, mt=mt: e.dma_start(out=od[mt * 128:(mt + 1) * 128, :], in_=kv32[:]), reads=["kv32"], writes=["okv"])

    cut("c5")
    R_early = R
    xT32 = R_early[:, 0:1024].rearrange("p (a b) -> p a b", b=128); xTb = sb("xTb", [128, 8, 128], BF16)
    xtok = sb("xtok", [128, 2048])
    BT = sb("BT", [128, 4, 128], BF16); CT = sb("CT", [128, 4, 128], BF16); Btok = sb("Btok", [128, 4, 128], BF16)
    Ctok_s = sb("Ctok_s", [NS, 4, 128]); Btok_s = sb("Btok_s", [NS, 4, 128])
    xh = [sb("xh%d" % i, [128, 160], BF16) for i in range(2)]; xhc = sb("xhc", [128, 8, 160], BF16)
    fm32 = [sb("fm32_%d" % i, [128, 512]) for i in range(2)]
    dg = [sb("dg0", [128, 4, 128], BF16)] * 2
    xdt = sb("xdt", [128, 2048], BF16); xdte = sb("xdte", [128, 2048], BF16); cbm = sb("cbm", [128, 512])
    sm = sb("sm", [128, 8, 32])
    y32 = sb("y32", [128, 2048]); ynT = MT[:, 0:2048].rearrange("p (a b) -> p a b", b=128)
    acv = sb("acv", [128, 16, 128]); aT = acv[:, 0:8, :]; cvT = acv[:, 8:16, :]; cactT = MT[:, 2048:3072].rearrange("p (a b) -> p a b", b=128)
    stat = sb("stat", [128, 4, 128])
    qT = MT[:, 3072:4096].rearrange("p (a b) -> p a b", b=128); oT = sb("oT", [128, 8, 128], BF16); PnT = sb("PnT", [128, 2, 128], BF16)
    att = sb("att", [128, 3, 256]); col = sb("col", [128, 16])
    gbr = acv[:, 0:8, :]; mrg = acv[:, 8:16, :]; mrgb = xdte[:, 0:1024].rearrange("p (a b) -> p a b", b=128)
    x1 = xtok[:, 0:1024]; x1T = xdte[:, 1024:2048].rearrange("p (a b) -> p a b", b=128); h1 = xtok[:, 1024:2048]
    qpT = xdt[:, :].rearrange("p (a b) -> p a b", b=128)
    top = sb("top", [128, 16, 16]); idxu = sb("idxu", [128, 16, 16], U32); idxf = sb("idxf", [128, 16, 16])
    best = sb("best", [128, 8, 16]); pos = sb("pos", [128, 8, 16], U32); pab = sb("pab", [128, 2, 128], U32); pabf = sb("pabf", [128, 2, 128])
    selw = y32; ids = sb("ids", [128, 2, 128]); idi = sb("idi", [128, 128], I32)
    gw = sb("gw", [128, 128]); dots = sb("dots", [128, 128]); coef = sb("coef", [128, 128])
    Gall = wld[0][:, :].rearrange("p (a b) -> p a b", b=1024); Gp = sb("Gp", [128, 4096], BF16); G = [Gp[:, i * 1024:(i + 1) * 1024] for i in range(4)] + [wld[0][:, :].bitcast(BF16)[:, i * 1024:(i + 1) * 1024] for i in range(8)]; NG = len(G); x1p = sb("x1p", [128, 1024])
    dgv = [sb("dgv%d" % i, [128, 128], BF16) for i in range(2)]
    cut("c6")
    OP("pool", lambda e: e.memset(idi[:], 0), w=["idi"])
    sQ = sb("sQ", [128, NS, 16]); sB = acv; sC = Gall[:, 0:2, :].rearrange("p a (b n) -> p (a b) n", n=128)
    h0 = [Gall[:, 2:4, :].rearrange("p a d -> p (a d)")] * 2
    scr_x = dscr("scr_x", [NS, 2048]); scr_B = dscr("scr_B", [NS, 512]); scr_C = dscr("scr_C", [NS, 512]); scr_y = dscr("scr_y", [NS, 2048])
    scr_q = dscr("scr_q", [NS, 1024]); scr_o = dscr("scr_o", [NS, 4, 1024])
    cut("c7")
    sel32 = sb("sel32", [32, 128])
    OP("pool", lambda e: e.memset(sel32[:], 1.0), w=["sel32"])
    OP("pool", lambda e: e.affine_select(out=sel32[:], in_=sel32[:], pattern=[[1, 128]], compare_op=ALU.is_ge, fill=0.0, base=0,
                                         channel_multiplier=-4), ["sel32"], ["sel32"])
    OP("pool", lambda e: e.affine_select(out=sel32[:], in_=sel32[:], pattern=[[-1, 128]], compare_op=ALU.is_ge, fill=0.0, base=3,
                                         channel_multiplier=4), ["sel32"], ["sel32"])
    cut("c8")
    hp = sb("hp", [32, 4]); hp2 = sb("hp2", [32, 3, NS]); qsc = sb("qsc", [128, 4, NS])
    S.dma(lambda e: e.dma_start(out=hp[:, 0:1], in_=dtb.rearrange("o h -> h o")), writes=["hp"])
    S.dma(lambda e: e.dma_start(out=hp[:, 1:2], in_=alog.rearrange("o h -> h o")), writes=["hp"])
    S.dma(lambda e: e.dma_start(out=hp[:, 2:3], in_=dsk.rearrange("o h -> h o")), writes=["hp"])
    OP("act", lambda e: e.activation(out=hp[:, 1:2], in_=hp[:, 1:2], func=AF.Exp), ["hp"], ["hp"])
    OP("dve", lambda e: e.tensor_scalar(out=hp[:, 1:2], in0=hp[:, 1:2], scalar1=-1.0, scalar2=None, op0=ALU.mult), ["hp"], ["hp"])
    Ks = Gall[:, 0:2, :]; Vs = Gall[:, 2:4, :]; qb = selw[:, 0:1024]
    Sall = sb("Sall", [128, NS, 8]); Eall = sb("Eall", [128, NS, 8]); o4 = R[0:4, 1024:2048]; otok = R[0:NS, 0:1024]

    def ln_tok(src, dst, gi, N, scratch=None, skey="selw", key=None):
        scr = scratch if scratch is not None else selw[:, 0:1024]
        ks = [key] if key else ["h1", "x1"]
        OP("dve", lambda e: e.tensor_reduce(out=col[0:N, 0:1], in_=src[0:N, :], axis=AX.X, op=ALU.add), ks, ["col"])
        OP("dve", lambda e: e.tensor_scalar(out=col[0:N, 0:1], in0=col[0:N, 0:1], scalar1=1.0 / 1024, scalar2=None, op0=ALU.mult), ["col"], ["col"])
        OP("dve", lambda e: e.tensor_scalar(out=src[0:N, :], in0=src[0:N, :], scalar1=col[0:N, 0:1], scalar2=None, op0=ALU.subtract),
           ["col"] + ks, ks)
        OP("act", lambda e: e.activation(out=scr[0:N, :], in_=src[0:N, :], func=AF.Square, accum_out=col[0:N, 1:2]), ks, [skey, "col"])
        OP("dve", lambda e: e.tensor_scalar(out=col[0:N, 1:2], in0=col[0:N, 1:2], scalar1=1.0 / 1024, scalar2=EPS, op0=ALU.mult, op1=ALU.add),
           ["col"], ["col"])
        OP("act", lambda e: e.activation(out=col[0:N, 1:2], in_=col[0:N, 1:2], func=AF.Sqrt), ["col"], ["col"])
        OP("dve", lambda e: e.reciprocal(out=col[0:N, 1:2], in_=col[0:N, 1:2]), ["col"], ["col"])
        OP("dve", lambda e: e.scalar_tensor_tensor(out=dst[0:N, :], in0=src[0:N, :], scalar=col[0:N, 1:2], in1=ln_t[0:N, gi, :],
                                                   op0=ALU.mult, op1=ALU.mult), ks + ["col", "ln_t"], ks)
        OP("dve", lambda e: e.tensor_tensor(out=dst[0:N, :], in0=dst[0:N, :], in1=ln_t[0:N, gi + 1, :], op=ALU.add), ks + ["ln_t"], ks)

    import os
    STAGE = os.environ.get('STAGE', '')

    PUMP = int(os.environ.get('PUMP', '2'))
    PUMP2 = int(os.environ.get('PUMP2', '3'))
    PP = [PUMP2]

    def emit_block(blk):
        samp = blk == 16
        N = NS if samp else 128
        t0 = 0 if samp else blk * 128
        last = blk == 15
        srcT = xsT if samp else xpT[:, t0:t0 + 128]
        S.dma(lambda e: e.dma_start(out=xT32[:, :, 0:N], in_=srcT.rearrange("(kc p) t -> p kc t", p=128)), writes=["xT32"])
        OP("dve", lambda e: e.tensor_copy(out=xTb[:, :, 0:N], in_=xT32[:, :, 0:N]), ["xT32"], ["xTb"])
        S.dma(lambda e: e.dma_start(out=xres[0:N, :], in_=(xs if samp else xp[t0:t0 + 128, :])), writes=["xres"])

        def proj_fm(vbf, wkey, cc, ps, pskey):
            pump(PP[0])
            for kc in range(8):
                OP("pe", lambda e, kc=kc: e.matmul(ps[:, 0:N], lhsT=vbf[:, kc, cc * 128:(cc + 1) * 128], rhs=xTb[:, kc, 0:N],
                                                   start=(kc == 0), stop=(kc == 7)), [wkey, "xTb"], [pskey])

        def proj_tm(vbf, wkey, ncols, ps, pskey, c0=0):
            for kc in range(8):
                OP("pe", lambda e, kc=kc: e.matmul(ps[0:N, 0:ncols], lhsT=xTb[:, kc, 0:N], rhs=vbf[:, kc, c0:c0 + ncols],
                                                   start=(kc == 0), stop=(kc == 7)), [wkey, "xTb"], [pskey])

        def conf_front(defer):
            gens = []
            for pn in range(4):
                pump(PUMP)
                vbf, wkey = load_w(w_in, 8, S_DT + pn * 512, 512)
                for cc in range(4):
                    c8 = (pn % 2) * 4 + cc
                    ps, pskey = PB[cc % 2], "pb%d" % (cc % 2)
                    proj_fm(vbf, wkey, cc, ps, pskey)
                    if pn < 2:
                        OP("act", lambda e, c8=c8, ps=ps: e.copy(out=aT[:, c8, 0:N], in_=ps[:, 0:N]), [pskey], ["aT#%d" % c8])
                        continue
                    fm, fkey = fm32[1], "fm32_1"
                    OP("act", lambda e, ps=ps: e.activation(out=fm[:, 0:N], in_=ps[:, 0:N], func=AF.Sigmoid), [pskey], [fkey])
                    OP("dve", lambda e, c8=c8: e.tensor_tensor(out=fm[:, 0:N], in0=fm[:, 0:N], in1=aT[:, c8, 0:N], op=ALU.mult), [fkey, "aT#%d" % c8], [fkey])
                    xhj, xkey = (xhc[:, c8, :], "xhc#%d" % c8) if defer else (xh[cc % 2], "xh%d" % (cc % 2))
                    if not samp:
                        OP("dve", lambda e, c8=c8, xhj=xhj: e.tensor_copy(out=xhj[:, 0:30], in_=hist_c[:, c8, :]), ["hist_c"], [xkey])
                        OP("dve", lambda e, xhj=xhj: e.tensor_copy(out=xhj[:, 30:30 + N], in_=fm[:, 0:N]), [fkey], [xkey])
                        OP("dve", lambda e, c8=c8, xhj=xhj: e.tensor_copy(out=hist_c[:, c8, :], in_=xhj[:, N:N + 30]), [xkey], ["hist_c"])
                        if last:
                            OP("dve", lambda e, c8=c8: e.tensor_copy(out=tail_c[:, c8, 0:30], in_=fm[:, N - 30:N]), [fkey], ["tail_c"])
                        rhs_k = lambda k, xhj=xhj: xhj[:, k:k + N]
                    else:
                        gv = R[:, 0:496].rearrange("p (b k) -> p b k", k=31)
                        stg = R[:, 512:992].rearrange("p (b k) -> p b k", k=30)
                        S.dma(lambda e, c8=c8, stg=stg: e.dma_start(out=stg, in_=st_cconvT[:, c8, :, :]), writes=["R"])
                        gvb = xdt[:, 0:496].rearrange("p (b k) -> p b k", k=31)
                        OP("dve", lambda e, gvb=gvb, stg=stg: e.tensor_copy(out=gvb[:, :, 0:30], in_=stg), ["R"], ["xdt"])
                        OP("dve", lambda e, gvb=gvb: e.tensor_copy(out=gvb[:, :, 30], in_=fm[:, 0:N]), [fkey], ["xdt"])
                        OP("dve", lambda e, c8=c8: e.tensor_copy(out=tail_c[:, c8, 0:16], in_=fm[:, 0:N]), [fkey], ["tail_c"])
                        rhs_k = lambda k, gvb=gvb: gvb[:, :, k]
                        xkey = "xdt"
                    def conv_taps(c8=c8, rhs_k=rhs_k, xkey=xkey):
                        OP("dve", lambda e: e.tensor_scalar(out=cvT[:, c8, 0:N], in0=rhs_k(0), scalar1=ccw_t[:, c8, 0:1],
                                                            scalar2=ccb_t[:, c8:c8 + 1], op0=ALU.mult, op1=ALU.add),
                           [xkey, "ccw_t", "ccb_t"], ["cvT#%d" % c8])
                        OP("dve", lambda e: e.tensor_scalar(out=aT[:, c8, 0:N], in0=rhs_k(1), scalar1=ccw_t[:, c8, 1:2],
                                                            scalar2=None, op0=ALU.mult), [xkey, "ccw_t"], ["aT#%d" % c8])
                        yield
                        for k in range(2, 31):
                            acc_, akey_ = (cvT, "cvT#%d" % c8) if k % 2 == 0 else (aT, "aT#%d" % c8)
                            OP("dve", lambda e, k=k, acc_=acc_: e.scalar_tensor_tensor(out=acc_[:, c8, 0:N], in0=rhs_k(k), scalar=ccw_t[:, c8, k:k + 1],
                                                                                      in1=acc_[:, c8, 0:N], op0=ALU.mult, op1=ALU.add),
                               [xkey, "ccw_t", akey_], [akey_])
                            if k % 2:
                                yield
                        OP("dve", lambda e: e.tensor_tensor(out=cvT[:, c8, 0:N], in0=cvT[:, c8, 0:N], in1=aT[:, c8, 0:N], op=ALU.add),
                           ["cvT#%d" % c8, "aT#%d" % c8], ["cvT#%d" % c8])
                        yield
                    if defer:
                        gens.append(conv_taps())
                    else:
                        for _ in conv_taps():
                            pass
            return gens

        cgen = [None]
        if not samp:
            PP[0] = 3
            import itertools
            cgen[0] = itertools.chain(*conf_front(True))

        def pumpc(k):
            for _ in range(k):
                if cgen[0] is None:
                    return
                try:
                    next(cgen[0])
                except StopIteration:
                    cgen[0] = None
                    return

        PP[0] = 3
        for pn in range(6):
            pump(PUMP)
            vbf, wkey = load_w(w_in, 8, S_Z + pn * 512, 512)
            for cc in range(4):
                j = pn * 4 + cc
                pumpc(6)
                ps, pskey = PB[j % 2], "pb%d" % (j % 2)
                proj_fm(vbf, wkey, cc, ps, pskey)
                xhj, xkey = xh[j % 2], "xh%d" % (j % 2)
                dgj, dkey = dg[0], "dg0"
                for k in range(4):
                    OP("act", lambda e, k=k, j=j, dgj=dgj: e.activation(out=dgj[:, k, :], in_=ident[:], func=AF.Copy, scale=scw_t[:, j, k:k + 1]),
                       ["ident", "scw_t"], ["dg0#%d" % k])
                if not samp:
                    OP("dve", lambda e, j=j, xhj=xhj: e.tensor_copy(out=xhj[:, 0:3], in_=hist_s[:, j, :]), ["hist_s"], [xkey])
                    OP("act", lambda e, xhj=xhj, ps=ps: e.copy(out=xhj[:, 3:3 + N], in_=ps[:, 0:N]), [pskey], [xkey])
                    OP("dve", lambda e, j=j, xhj=xhj: e.tensor_copy(out=hist_s[:, j, :], in_=xhj[:, N:N + 3]), [xkey], ["hist_s"])
                    if last:
                        OP("dve", lambda e, j=j, ps=ps: e.tensor_copy(out=tail_s[:, j, 0:3], in_=ps[:, N - 3:N]), [pskey], ["tail_s"])
                    rhs_k = lambda k, xhj=xhj: xhj[:, k:k + N]
                else:
                    xv = xhj[:, 0:64].rearrange("p (b k) -> p b k", k=4)
                    stg = fm32[0][:, 0:48].rearrange("p (b k) -> p b k", k=3)
                    S.dma(lambda e, j=j, stg=stg: e.dma_start(out=stg, in_=st_sconvT[:, j, :, :]), writes=["fm32_0"])
                    OP("dve", lambda e, xv=xv, stg=stg: e.tensor_copy(out=xv[:, :, 0:3], in_=stg), ["fm32_0"], [xkey])
                    OP("act", lambda e, xv=xv, ps=ps: e.copy(out=xv[:, :, 3], in_=ps[:, 0:N]), [pskey], [xkey])
                    OP("dve", lambda e, j=j, ps=ps: e.tensor_copy(out=tail_s[:, j, 0:16], in_=ps[:, 0:N]), [pskey], ["tail_s"])
                    rhs_k = lambda k, xv=xv: xv[:, :, k]
                pc, pckey = PB[2 + j % 2], "pb%d" % (2 + j % 2)
                for k in range(4):
                    OP("pe", lambda e, k=k, dgj=dgj, rhs_k=rhs_k, pc=pc: e.matmul(pc[:, 0:N], lhsT=dgj[:, k, :], rhs=rhs_k(k),
                                                                                 start=(k == 0), stop=(k == 3)), [dkey, xkey], [pckey])
                fm, fkey = fm32[1], "fm32_1"
                OP("act", lambda e, j=j, pc=pc: e.activation(out=fm[:, 0:N], in_=pc[:, 0:N], func=AF.Silu, bias=scb_t[:, j:j + 1]),
                   [pckey, "scb_t"], [fkey])
                if j >= 20:
                    OP("act", lambda e, j=j: e.copy(out=CT[:, j - 20, 0:N], in_=fm[:, 0:N]), [fkey], ["CT"])
                    if not samp:
                        continue
                if 16 <= j < 20:
                    OP("act", lambda e, j=j: e.copy(out=BT[:, j - 16, 0:N], in_=fm[:, 0:N]), [fkey], ["BT"])
                pt, ptkey = PB[4 + j % 2], "pb%d" % (4 + j % 2)
                OP("pe", lambda e, pt=pt: e.transpose(out=pt[0:N, 0:128], in_=fm[:, 0:N], identity=ident[:]), [fkey, "ident"], [ptkey])
                if j < 16:
                    OP("act", lambda e, j=j, pt=pt: e.copy(out=xtok[0:N, j * 128:(j + 1) * 128], in_=pt[0:N, 0:128]), [ptkey], ["xtok"])
                elif j < 20:
                    OP("act", lambda e, j=j, pt=pt: e.copy(out=Btok[0:N, j - 16, :], in_=pt[0:N, 0:128]), [ptkey], ["Btok"])
                    if samp:
                        OP("act", lambda e, j=j, pt=pt: e.copy(out=Btok_s[0:N, j - 16, :], in_=pt[0:N, 0:128]), [ptkey], ["Btok_s"])
                else:
                    OP("act", lambda e, j=j, pt=pt: e.copy(out=Ctok_s[0:N, j - 20, :], in_=pt[0:N, 0:128]), [ptkey], ["Ctok_s"])

        if STAGE == "xbc":
            return
        pumpc(10 ** 6)
        pump(PUMP)
        vbf, wkey = load_w(w_in, 8, S_XBC, 32)
        proj_tm(vbf, wkey, 32, PB[4], "pb4")
        OP("dve", lambda e: e.tensor_tensor(out=dtt[0:N, :], in0=PB[4][0:N, 0:32], in1=dtb_t[0:N, :], op=ALU.add), ["pb4", "dtb_t"], ["dtt"])
        OP("act", lambda e: e.activation(out=dtt[0:N, :], in_=dtt[0:N, :], func=AF.Exp), ["dtt"], ["dtt"])
        OP("act", lambda e: e.activation(out=dtt[0:N, :], in_=dtt[0:N, :], func=AF.Ln, bias=1.0), ["dtt"], ["dtt"])
        if samp:
            OP("pe", lambda e: e.transpose(out=PB[4][0:32, 64:64 + NS], in_=dtt[0:NS, :], identity=ident[0:NS, 0:NS]), ["dtt", "ident"], ["pb4"])
            OP("act", lambda e: e.copy(out=hp2[:, 0, :], in_=PB[4][0:32, 64:64 + NS]), ["pb4"], ["hp2"])
            OP("act", lambda e: e.activation(out=hp2[:, 1, :], in_=hp2[:, 0, :], func=AF.Exp, scale=hp[:, 1:2]), ["hp2", "hp"], ["hp2"])
            OP("dve", lambda e: e.tensor_copy(out=hp2[:, 2, :], in_=bc(hp[:, 2:3], [32, NS])), ["hp"], ["hp2"])
        if STAGE == "dt":
            return
        for pn in range(4):
            pump(PUMP)
            vbf, wkey = load_w(w_in, 8, pn * 512, 512)
            ps, pskey = PB[pn % 2], "pb%d" % (pn % 2)
            proj_tm(vbf, wkey, 512, ps, pskey)
            OP("act", lambda e, pn=pn, ps=ps: e.activation(out=sz[0:N, pn * 512:(pn + 1) * 512], in_=ps[0:N, :], func=AF.Silu), [pskey], ["sz"])

        if STAGE == "z":
            return
        if not samp:
            da, acs, dte, cd, eacs = (sm[:, i_, :] for i_ in range(5))
            OP("dve", lambda e: e.tensor_tensor(out=da, in0=dtt[:, :], in1=a_t[:, :], op=ALU.mult), ["dtt", "a_t"], ["sm"])
            OP("pe", lambda e: e.matmul(PB[4][:, 0:32], lhsT=tri[:], rhs=da, start=True, stop=True), ["tri", "sm"], ["pb4"])
            OP("act", lambda e: e.copy(out=acs, in_=PB[4][:, 0:32]), ["pb4"], ["sm"])
            OP("pe", lambda e: e.matmul(PB[4][:, 32:64], lhsT=sel_last[:], rhs=acs, start=True, stop=True), ["sel_last", "sm"], ["pb4"])
            OP("dve", lambda e: e.tensor_tensor(out=dte, in0=PB[4][:, 32:64], in1=acs, op=ALU.subtract), ["pb4", "sm"], ["sm"])
            OP("act", lambda e: e.activation(out=dte, in_=dte, func=AF.Exp), ["sm"], ["sm"])
            OP("dve", lambda e: e.tensor_tensor(out=dte, in0=dte, in1=dtt[:, :], op=ALU.mult), ["sm", "dtt"], ["sm"])
            OP("act", lambda e: e.activation(out=cd, in_=PB[4][:, 32:64], func=AF.Exp), ["pb4"], ["sm"])
            OP("act", lambda e: e.activation(out=eacs, in_=acs, func=AF.Exp), ["sm"], ["sm"])
            R3 = R[:, :].rearrange("p (h l) -> p h l", l=128)
            OP("pool", lambda e: e.tensor_tensor(out=R3, in0=bc(tri[:, :].unsqueeze(1), [128, 32, 128]), in1=bc(da.unsqueeze(2), [128, 32, 128]),
                                                 op=ALU.mult), ["tri", "sm"], ["R", "R2"])
            for half in range(2):
                for q in range(4):
                    qq = half * 4 + q
                    OP("pe", lambda e, q=q, qq=qq: e.matmul(PB[q][:, :], lhsT=ones[:], rhs=R[:, qq * 512:(qq + 1) * 512], start=True, stop=True),
                       ["ones", "R" if half == 0 else "R2"], ["pb%d" % q])
                for hh in range(16):
                    h = half * 16 + hh
                    OP("dve", lambda e, h=h, hh=hh: e.tensor_scalar(out=R[:, h * 128:(h + 1) * 128], in0=PB[hh // 4][:, (hh % 4) * 128:(hh % 4 + 1) * 128],
                                                                    scalar1=acs[:, h:h + 1], scalar2=0.0, op0=ALU.subtract, op1=ALU.min),
                       ["pb%d" % (hh // 4), "sm"], ["R#%d" % hh if half == 0 else "R2#%d" % hh])
            pump(6)
            OP("act", lambda e: e.activation(out=R[:, :], in_=R[:, :], func=AF.Exp), ["R", "R2"], ["R", "R2"])
            pump(6)
            for g in range(4):
                OP("pe", lambda e, g=g: e.matmul(PB[4][:, g * 128:(g + 1) * 128], lhsT=BT[:, g, :], rhs=CT[:, g, :], start=True, stop=True),
                   ["BT", "CT"], ["pb4"])
            OP("dve", lambda e: e.tensor_tensor(out=cbm[:, :].rearrange("p (g l) -> p g l", l=128), in0=PB[4][:, :].rearrange("p (g l) -> p g l", l=128),
                                                in1=bc(tri[:, :].unsqueeze(1), [128, 4, 128]), op=ALU.mult), ["pb4", "tri"], ["cbm"])
            OP("dve", lambda e: e.tensor_tensor(out=MT[:, :].rearrange("p (g r l) -> p g r l", g=4, r=8),
                                                in0=R[:, :].rearrange("p (g r l) -> p g r l", g=4, r=8),
                                                in1=bc(cbm[:, :].rearrange("p (g l) -> p g l", l=128).unsqueeze(2), [128, 4, 8, 128]), op=ALU.mult),
               ["R", "R2", "cbm"], ["MT"])
            pump(6)
            x3 = xtok[:, :].rearrange("p (h d) -> p h d", d=64)
            OP("pool", lambda e: e.tensor_tensor(out=xdt[:, :].rearrange("p (h d) -> p h d", d=64), in0=x3,
                                                 in1=bc(dtt[:, :].unsqueeze(2), [128, 32, 64]), op=ALU.mult), ["xtok", "dtt"], ["xdt"])
            OP("pool", lambda e: e.tensor_tensor(out=xdte[:, :].rearrange("p (h d) -> p h d", d=64), in0=x3,
                                                 in1=bc(dte.unsqueeze(2), [128, 32, 64]), op=ALU.mult), ["xtok", "sm"], ["xdte"])
            for h in range(32):
                OP("pe", lambda e, h=h: e.matmul(PB[h // 8][:, (h % 8) * 64:(h % 8 + 1) * 64], lhsT=MT[:, h * 128:(h + 1) * 128],
                                                 rhs=xdt[:, h * 64:(h + 1) * 64], start=True, stop=True), ["MT", "xdt"], ["pb%d" % (h // 8)])
            for g in range(4):
                bk = 4 + g % 2
                OP("pe", lambda e, g=g, bk=bk: e.matmul(PB[bk][:, :], lhsT=CT[:, g, :], rhs=hTb[:, g * 512:(g + 1) * 512], start=True, stop=True),
                   ["CT", "hTb"], ["pb%d" % bk])
                OP("dve", lambda e, g=g, bk=bk: e.tensor_tensor(out=R[:, g * 512:(g + 1) * 512].rearrange("p (r d) -> p r d", d=64),
                                                                in0=PB[bk][:, :].rearrange("p (r d) -> p r d", d=64),
                                                                in1=bc(eacs[:, g * 8:(g + 1) * 8].unsqueeze(2), [128, 8, 64]), op=ALU.mult),
                   ["pb%d" % bk, "sm"], ["R#%d" % (4 * g + i_) for i_ in range(4)])
                OP("dve", lambda e, g=g: e.tensor_tensor(out=y32[:, g * 512:(g + 1) * 512], in0=PB[g][:, :], in1=R[:, g * 512:(g + 1) * 512],
                                                         op=ALU.add), ["pb%d" % g] + ["R#%d" % (4 * g + i_) for i_ in range(4)], ["y32"])
            OP("pool", lambda e: e.tensor_tensor(out=R[:, 2048:4096].rearrange("p (h d) -> p h d", d=64), in0=x3,
                                                 in1=bc(dsk_t[:, :].unsqueeze(2), [128, 32, 64]), op=ALU.mult), ["xtok", "dsk_t"], ["R2"])
            OP("pool", lambda e: e.tensor_tensor(out=y32[:, :], in0=y32[:, :], in1=R[:, 2048:4096], op=ALU.add), ["R2", "y32"], ["y32"])
            pump(6)
            for g in range(4):
                OP("pe", lambda e, g=g: e.matmul(PB[g][:, :], lhsT=Btok[:, g, :], rhs=xdte[:, g * 512:(g + 1) * 512], start=True, stop=True),
                   ["Btok", "xdte"], ["pb%d" % g])
            OP("dve", lambda e: e.tensor_tensor(out=hT[:, :].rearrange("p (h d) -> p h d", d=64), in0=hT[:, :].rearrange("p (h d) -> p h d", d=64),
                                                in1=bc(cd.unsqueeze(2), [128, 32, 64]), op=ALU.mult), ["hT", "sm"], ["hT"])
            for g in range(4):
                OP("dve", lambda e, g=g: e.tensor_tensor(out=hT[:, g * 512:(g + 1) * 512], in0=hT[:, g * 512:(g + 1) * 512], in1=PB[g][:, :],
                                                         op=ALU.add), ["hT#%d" % g, "pb%d" % g], ["hT#%d" % g])
            OP("act", lambda e: e.copy(out=hTb[:, :], in_=hT[:, :]), ["hT"], ["hTb"])
            if last:
                for hp_ in range(16):
                    OP("pe", lambda e, hp_=hp_: e.transpose(out=PB[hp_ % 2][:, 0:128], in_=hT[:, hp_ * 128:(hp_ + 1) * 128], identity=ident[:]),
                       ["hT", "ident"], ["pb%d" % (hp_ % 2)])
                    OP("act", lambda e, hp_=hp_: e.copy(out=fm32[hp_ % 2][:, 0:128], in_=PB[hp_ % 2][:, 0:128]), ["pb%d" % (hp_ % 2)], ["fm32_%d" % (hp_ % 2)])
                    S.dma(lambda e, hp_=hp_: e.dma_start(out=o_ssd_p[hp_ * 128:(hp_ + 1) * 128, :], in_=fm32[hp_ % 2][:, 0:128]),
                          reads=["fm32_%d" % (hp_ % 2)], writes=["o_ssd_p"])
        else:
            S.dma(lambda e: e.dma_start(out=scr_x, in_=xtok[0:NS, :]), reads=["xtok"], writes=["scr_x"])
            S.dma(lambda e: e.dma_start(out=scr_B, in_=Btok_s[:, :, :].rearrange("p g n -> p (g n)")), reads=["Btok_s"], writes=["scr_B"])
            S.dma(lambda e: e.dma_start(out=scr_C, in_=Ctok_s[:, :, :].rearrange("p g n -> p (g n)")), reads=["Ctok_s"], writes=["scr_C"])
            S.dma(lambda e: e.dma_start(out=sQ[:], in_=scr_x.rearrange("b (q r) -> q b r", r=16)), reads=["scr_x"], writes=["sQ"])
            for g in range(4):
                S.dma(lambda e, g=g: e.dma_start(out=sB[32 * g:32 * (g + 1), :, :], in_=scr_B[:, g * 128:(g + 1) * 128].partition_broadcast(32)),
                      reads=["scr_B"], writes=["sB"])
                S.dma(lambda e, g=g: e.dma_start(out=sC[32 * g:32 * (g + 1), :, :], in_=scr_C[:, g * 128:(g + 1) * 128].partition_broadcast(32)),
                      reads=["scr_C"], writes=["sC"])
            OP("pe", lambda e: e.matmul(PB[4][:, 128:128 + 3 * NS], lhsT=sel32[:, :], rhs=hp2[:, :, :].rearrange("p a b -> p (a b)"),
                                        start=True, stop=True), ["sel32", "hp2"], ["pb4"])
            OP("act", lambda e: e.copy(out=qsc[:, 0:3, :].rearrange("p a b -> p (a b)"), in_=PB[4][:, 128:128 + 3 * NS]), ["pb4"], ["qsc"])
            dtx = sm[:, 0:8, :].rearrange("p a b -> p (a b)")[:, 0:256].rearrange("p (b r) -> p b r", r=16)
            OP("dve", lambda e: e.tensor_tensor(out=dtx, in0=sQ[:, :, :], in1=bc(qsc[:, 0, :].unsqueeze(2), [128, NS, 16]), op=ALU.mult),
               ["sQ", "qsc"], ["sm"])
            yo = y32[:, 0:256].rearrange("p (b r) -> p b r", r=16)
            for b in range(NS):
                hb, hkey = h0[b % 2], "h0_%d" % (b % 2)
                S.dma(lambda e, b=b, hb=hb: e.dma_start(out=hb[:, :], in_=st_ssd[b]), writes=[hkey])
                h3 = hb[:, :].rearrange("p (r n) -> p r n", n=128)
                R3 = R[:, 0:2048].rearrange("p (r n) -> p r n", n=128)
                OP("dve", lambda e, b=b, h3=h3, R3=R3: e.tensor_tensor(out=R3, in0=h3, in1=bc(sC[:, b, :].unsqueeze(1), [128, 16, 128]), op=ALU.mult),
                   [hkey, "sC"], ["R"])
                OP("dve", lambda e, b=b, R3=R3: e.tensor_reduce(out=yo[:, b, :], in_=R3, axis=AX.X, op=ALU.add), ["R"], ["y32"])
                R4 = R[:, 2048:4096].rearrange("p (r n) -> p r n", n=128)
                OP("pool", lambda e, b=b, R4=R4: e.tensor_tensor(out=R4, in0=bc(dtx[:, b, :].unsqueeze(2), [128, 16, 128]),
                                                                 in1=bc(sB[:, b, :].unsqueeze(1), [128, 16, 128]), op=ALU.mult), ["sm", "sB"], ["R2"])
                OP("dve", lambda e, b=b, hb=hb: e.scalar_tensor_tensor(out=hb[:, :], in0=hb[:, :], scalar=qsc[:, 1, b:b + 1], in1=R[:, 2048:4096],
                                                                       op0=ALU.mult, op1=ALU.add), [hkey, "qsc", "R2"], [hkey])
                S.dma(lambda e, b=b, hb=hb: e.dma_start(out=o_ssd_s[b], in_=hb[:, :]), reads=[hkey], writes=["o_ssd_s"])
            OP("dve", lambda e: e.tensor_tensor(out=R[:, 0:2048].rearrange("p (b n) -> p b n", n=128), in0=sC[:, :, :], in1=sB[:, :, :], op=ALU.mult),
               ["sC", "sB"], ["R"])
            OP("dve", lambda e: e.tensor_reduce(out=qsc[:, 3, :], in_=R[:, 0:2048].rearrange("p (b n) -> p b n", n=128), axis=AX.X, op=ALU.add),
               ["R"], ["qsc"])
            OP("dve", lambda e: e.tensor_tensor(out=yo, in0=yo, in1=bc(qsc[:, 1, :].unsqueeze(2), [128, NS, 16]), op=ALU.mult), ["y32", "qsc"], ["y32"])
            OP("dve", lambda e: e.tensor_tensor(out=dtx, in0=dtx, in1=bc(qsc[:, 3, :].unsqueeze(2), [128, NS, 16]), op=ALU.mult), ["sm", "qsc"], ["sm"])
            OP("dve", lambda e: e.tensor_tensor(out=yo, in0=yo, in1=dtx, op=ALU.add), ["y32", "sm"], ["y32"])
            OP("dve", lambda e: e.tensor_tensor(out=dtx, in0=sQ[:, :, :], in1=bc(qsc[:, 2, :].unsqueeze(2), [128, NS, 16]), op=ALU.mult), ["sQ", "qsc"], ["sm"])
            OP("dve", lambda e: e.tensor_tensor(out=yo, in0=yo, in1=dtx, op=ALU.add), ["y32", "sm"], ["y32"])
            S.dma(lambda e: e.dma_start(out=scr_y.rearrange("b (q r) -> q b r", r=16), in_=yo), reads=["y32"], writes=["scr_y"])
            S.dma(lambda e: e.dma_start(out=y32[0:NS, :], in_=scr_y), reads=["scr_y"], writes=["y32"])

        if STAGE == "ssd":
            return
        pump(PUMP)
        OP("dve", lambda e: e.tensor_tensor(out=y32[0:N, :], in0=y32[0:N, :], in1=sz[0:N, :], op=ALU.mult), ["y32", "sz"], ["y32"])
        OP("act", lambda e: e.activation(out=R[0:N, 0:2048], in_=y32[0:N, :], func=AF.Square, accum_out=col[0:N, 2:3]), ["y32"], ["R", "col"])
        OP("dve", lambda e: e.tensor_scalar(out=col[0:N, 2:3], in0=col[0:N, 2:3], scalar1=1.0 / 2048, scalar2=EPS, op0=ALU.mult, op1=ALU.add), ["col"], ["col"])
        OP("act", lambda e: e.activation(out=col[0:N, 2:3], in_=col[0:N, 2:3], func=AF.Sqrt), ["col"], ["col"])
        OP("dve", lambda e: e.reciprocal(out=col[0:N, 2:3], in_=col[0:N, 2:3]), ["col"], ["col"])
        OP("dve", lambda e: e.scalar_tensor_tensor(out=y32[0:N, :], in0=y32[0:N, :], scalar=col[0:N, 2:3], in1=nw_t[0:N, :], op0=ALU.mult, op1=ALU.mult),
           ["y32", "col", "nw_t"], ["y32"])
        for fc in range(16):
            pt, ptkey = PB[4 + fc % 2], "pb%d" % (4 + fc % 2)
            OP("pe", lambda e, fc=fc, pt=pt: e.transpose(out=pt[:, 0:N], in_=y32[0:N, fc * 128:(fc + 1) * 128], identity=ident[0:N, 0:N]),
               ["y32", "ident"], [ptkey])
            OP("act", lambda e, fc=fc, pt=pt: e.copy(out=ynT[:, fc, 0:N], in_=pt[:, 0:N]), [ptkey], ["MT#%d" % fc])

        if STAGE == "rms":
            return
        pump(PUMP)
        PP[0] = 3
        if samp:
            PP[0] = 3
            conf_front(False)
        OP("act", lambda e: e.activation(out=aT[:, :, 0:N], in_=cvT[:, :, 0:N], func=AF.Square), ["cvT"], ["aT"])
        for c8 in range(8):
            OP("pe", lambda e, c8=c8: e.matmul(PB[4][:, 0:N], lhsT=ones[:], rhs=cvT[:, c8, 0:N], start=(c8 == 0), stop=(c8 == 7)), ["ones", "cvT"], ["pb4"])
        for c8 in range(8):
            OP("pe", lambda e, c8=c8: e.matmul(PB[5][:, 0:N], lhsT=ones[:], rhs=aT[:, c8, 0:N], start=(c8 == 0), stop=(c8 == 7)), ["ones", "aT"], ["pb5"])
        mean, var = stat[:, 0, 0:N], stat[:, 1, 0:N]
        OP("dve", lambda e: e.tensor_scalar(out=mean, in0=PB[4][:, 0:N], scalar1=1.0 / 1024, scalar2=None, op0=ALU.mult), ["pb4"], ["stat"])
        OP("dve", lambda e: e.tensor_scalar(out=var, in0=PB[5][:, 0:N], scalar1=1.0 / 1024, scalar2=None, op0=ALU.mult), ["pb5"], ["stat"])
        OP("dve", lambda e: e.tensor_tensor(out=stat[:, 2, 0:N], in0=mean, in1=mean, op=ALU.mult), ["stat"], ["stat"])
        OP("dve", lambda e: e.tensor_tensor(out=var, in0=var, in1=stat[:, 2, 0:N], op=ALU.subtract), ["stat"], ["stat"])
        OP("dve", lambda e: e.tensor_scalar(out=var, in0=var, scalar1=EPS, scalar2=None, op0=ALU.add), ["stat"], ["stat"])
        OP("act", lambda e: e.activation(out=var, in_=var, func=AF.Sqrt), ["stat"], ["stat"])
        OP("dve", lambda e: e.reciprocal(out=var, in_=var), ["stat"], ["stat"])
        OP("dve", lambda e: e.tensor_tensor(out=cvT[:, :, 0:N], in0=cvT[:, :, 0:N], in1=bc(mean.unsqueeze(1), [128, 8, N]), op=ALU.subtract),
           ["cvT", "stat"], ["cvT"])
        OP("dve", lambda e: e.tensor_tensor(out=cvT[:, :, 0:N], in0=cvT[:, :, 0:N], in1=bc(var.unsqueeze(1), [128, 8, N]), op=ALU.mult),
           ["cvT", "stat"], ["cvT"])
        for c8 in range(8):
            OP("act", lambda e, c8=c8: e.activation(out=cactT[:, c8, 0:N], in_=cvT[:, c8, 0:N], func=AF.Silu, bias=clb_t[:, c8:c8 + 1],
                                                    scale=clg_t[:, c8:c8 + 1]), ["cvT", "clb_t", "clg_t"], ["MT#%d" % (16 + c8)])

        if STAGE == "conf":
            return
        PP[0] = 2
        for pn in range(2):
            pump(PUMP)
            vbf, wkey = load_w(w_in, 8, S_CONF + pn * 512, 512)
            for cc in range(4):
                ps, pskey = PB[cc % 2], "pb%d" % (cc % 2)
                proj_fm(vbf, wkey, cc, ps, pskey)
                OP("act", lambda e, c=pn * 4 + cc, ps=ps: e.copy(out=qT[:, c, 0:N], in_=ps[:, 0:N]), [pskey], ["MT#%d" % (24 + pn * 4 + cc)])
                if samp:
                    OP("act", lambda e, c=pn * 4 + cc, ps=ps: e.copy(out=cvT[:, c, 0:N], in_=ps[:, 0:N]), [pskey], ["cvT"])
        if not samp:
            for h in range(4):
                pump(3)
                for c2 in range(2):
                    OP("pe", lambda e, h=h, c2=c2: e.matmul(PB[2][:, 0:256], lhsT=qT[:, 2 * h + c2, :], rhs=KT[:, 2 * h + c2, :],
                                                            start=(c2 == 0), stop=(c2 == 1)), ["qT", "KT"], ["pb2"])
                OP("dve", lambda e: e.tensor_reduce(out=col[:, 4:5], in_=PB[2][:, 0:256], axis=AX.X, op=ALU.max), ["pb2"], ["col"])
                OP("dve", lambda e: e.tensor_scalar(out=col[:, 4:5], in0=col[:, 4:5], scalar1=-1.0 / 16, scalar2=None, op0=ALU.mult), ["col"], ["col"])
                OP("act", lambda e: e.activation(out=att[:, 0, :], in_=PB[2][:, 0:256], func=AF.Exp, bias=col[:, 4:5], scale=1.0 / 16,
                                                 accum_out=col[:, 5:6]), ["pb2", "col"], ["att", "col"])
                OP("dve", lambda e: e.reciprocal(out=col[:, 5:6], in_=col[:, 5:6]), ["col"], ["col"])
                OP("dve", lambda e: e.tensor_scalar(out=att[:, 1, :], in0=att[:, 0, :], scalar1=col[:, 5:6], scalar2=None, op0=ALU.mult), ["att", "col"], ["att"])
                for mc in range(2):
                    OP("pe", lambda e, mc=mc: e.transpose(out=PB[3][:, mc * 128:(mc + 1) * 128], in_=att[:, 1, mc * 128:(mc + 1) * 128], identity=ident[:]),
                       ["att", "ident"], ["pb3"])
                OP("act", lambda e: e.copy(out=PnT[:, :, :].rearrange("p a b -> p (a b)"), in_=PB[3][:, 0:256]), ["pb3"], ["PnT"])
                for c2 in range(2):
                    for mc in range(2):
                        OP("pe", lambda e, h=h, c2=c2, mc=mc: e.matmul(PB[4][:, 0:128], lhsT=Vb[:, mc, (2 * h + c2) * 128:(2 * h + c2 + 1) * 128],
                                                                      rhs=PnT[:, mc, :], start=(mc == 0), stop=(mc == 1)), ["Vb", "PnT"], ["pb4"])
                    OP("act", lambda e, h=h, c2=c2: e.copy(out=oT[:, 2 * h + c2, :], in_=PB[4][:, 0:128]), ["pb4"], ["oT"])
        else:
            for c in range(8):
                OP("pe", lambda e, c=c: e.transpose(out=PB[2][0:NS, c * 128:(c + 1) * 128] if c < 4 else PB[3][0:NS, (c - 4) * 128:(c - 3) * 128],
                                                    in_=cvT[:, c, 0:NS], identity=ident[:]), ["cvT", "ident"], ["pb2", "pb3"])
            OP("act", lambda e: e.copy(out=otok[:, 0:512], in_=PB[2][0:NS, :]), ["pb2"], ["otok"])
            OP("act", lambda e: e.copy(out=otok[:, 512:1024], in_=PB[3][0:NS, :]), ["pb3"], ["otok"])
            S.dma(lambda e: e.dma_start(out=scr_q, in_=otok[:, :]), reads=["otok"], writes=["scr_q"])
            for b in range(NS):
                S.dma(lambda e, b=b: e.dma_start(out=qb[:, :], in_=scr_q[b:b + 1, :].partition_broadcast(128)), reads=["scr_q"], writes=["qb"])
                Kb_, kk_ = (Ks, "Ks") if b % 2 == 0 else (Vs, "Vs")
                S.dma(lambda e, b=b, Kb_=Kb_: e.dma_start(out=Kb_[:, :, :], in_=ck[b].rearrange("(mc p) d -> p mc d", p=128)), writes=[kk_])
                OP("dve", lambda e, Kb_=Kb_: e.tensor_tensor(out=Kb_[:, :, :], in0=Kb_[:, :, :], in1=bc(qb[:, :].unsqueeze(1), [128, 2, 1024]), op=ALU.mult),
                   [kk_, "qb"], [kk_])
                OP("dve", lambda e, b=b, Kb_=Kb_: e.tensor_reduce(out=Sall[:, b, :], in_=Kb_[:, :, :].rearrange("p mc (h d) -> p (mc h) d", d=256), axis=AX.X,
                                                                  op=ALU.add), [kk_], ["Sall"])
            OP("pe", lambda e: e.transpose(out=PB[2][:, 0:128], in_=Sall[:, :, :].rearrange("p b c -> p (b c)"), identity=ident[:]), ["Sall", "ident"], ["pb2"])
            OP("dve", lambda e: e.tensor_reduce(out=col[:, 6:7], in_=PB[2][:, 0:128], axis=AX.X, op=ALU.max), ["pb2"], ["col"])
            OP("pe", lambda e: e.transpose(out=PB[3][0:1, 0:128], in_=col[:, 6:7], identity=ident[:]), ["col", "ident"], ["pb3"])
            OP("dve", lambda e: e.tensor_reduce(out=stat[0:1, 3, 0:NS], in_=PB[3][0:1, 0:128].rearrange("p (b c) -> p b c", c=8), axis=AX.X, op=ALU.max),
               ["pb3"], ["stat"])
            OP("pe", lambda e: e.matmul(PB[2][:, 256:256 + NS], lhsT=ones[0:1, :], rhs=stat[0:1, 3, 0:NS], start=True, stop=True), ["ones", "stat"], ["pb2"])
            OP("dve", lambda e: e.tensor_tensor(out=Eall[:, :, :], in0=Sall[:, :, :], in1=bc(PB[2][:, 256:256 + NS].unsqueeze(2), [128, NS, 8]),
                                                op=ALU.subtract), ["Sall", "pb2"], ["Eall"])
            OP("act", lambda e: e.activation(out=Eall[:, :, :], in_=Eall[:, :, :], func=AF.Exp, scale=1.0 / 16), ["Eall"], ["Eall"])
            OP("pe", lambda e: e.matmul(PB[3][:, 0:128], lhsT=ones[:], rhs=Eall[:, :, :].rearrange("p b c -> p (b c)"), start=True, stop=True),
               ["ones", "Eall"], ["pb3"])
            den = Sall[:, :, 0:4]
            pd = PB[3][:, 0:128].rearrange("p (b mc h) -> p b mc h", mc=2, h=4)
            OP("dve", lambda e: e.tensor_copy(out=den, in_=pd[:, :, 0, :]), ["pb3"], ["Sall"])
            OP("dve", lambda e: e.tensor_tensor(out=den, in0=den, in1=pd[:, :, 1, :], op=ALU.add), ["pb3", "Sall"], ["Sall"])
            OP("dve", lambda e: e.reciprocal(out=den, in_=den), ["Sall"], ["Sall"])
            OP("dve", lambda e: e.tensor_tensor(out=Eall[:, :, :].rearrange("p b (mc h) -> p b mc h", h=4),
                                                in0=Eall[:, :, :].rearrange("p b (mc h) -> p b mc h", h=4),
                                                in1=bc(den.unsqueeze(2), [128, NS, 2, 4]), op=ALU.mult), ["Eall", "Sall"], ["Eall"])
            for b in range(NS):
                Vb_, vk_ = (Ks, "Ks") if b % 2 == 0 else (Vs, "Vs")
                S.dma(lambda e, b=b, Vb_=Vb_: e.dma_start(out=Vb_[:, :, :], in_=cv[b].rearrange("(mc p) d -> p mc d", p=128)), writes=[vk_])
                for hf in range(2):
                    for mc in range(2):
                        OP("pe", lambda e, b=b, hf=hf, mc=mc, Vb_=Vb_: e.matmul(PB[4 + hf][0:4, :], lhsT=Eall[:, b, mc * 4:(mc + 1) * 4],
                                                                      rhs=Vb_[:, mc, hf * 512:(hf + 1) * 512], start=(mc == 0), stop=(mc == 1)),
                           ["Eall", vk_], ["pb%d" % (4 + hf)])
                    OP("act", lambda e, hf=hf: e.copy(out=o4[:, hf * 512:(hf + 1) * 512], in_=PB[4 + hf][0:4, :]), ["pb%d" % (4 + hf)], ["o4"])
                S.dma(lambda e, b=b: e.dma_start(out=scr_o[b], in_=o4[:, :]), reads=["o4"], writes=["scr_o"])
            for h in range(4):
                S.dma(lambda e, h=h: e.dma_start(out=otok[:, h * 256:(h + 1) * 256], in_=scr_o[:, h, h * 256:(h + 1) * 256]), reads=["scr_o"], writes=["otok"])
            for c in range(8):
                OP("pe", lambda e, c=c: e.transpose(out=PB[4][:, 0:NS], in_=otok[:, c * 128:(c + 1) * 128], identity=ident[0:NS, 0:NS]),
                   ["otok", "ident"], ["pb4"])
                OP("act", lambda e, c=c: e.copy(out=oT[:, c, 0:NS], in_=PB[4][:, 0:NS]), ["pb4"], ["oT"])

        if STAGE == "attn":
            return
        pump(PUMP)
        for br in range(3):
            for pn in range(2):
                pump(PUMP)
                vbf, wkey = load_w(w_in, 8, S_MEMQ + br * 1024 + pn * 512, 512)
                for cc in range(4):
                    ps, pskey = PB[cc % 2], "pb%d" % (cc % 2)
                    proj_fm(vbf, wkey, cc, ps, pskey)
                    OP("act", lambda e, c=pn * 4 + cc, ps=ps: e.activation(out=gbr[:, c, 0:N], in_=ps[:, 0:N], func=AF.Sigmoid), [pskey], ["gbr#%d" % (pn * 4 + cc)])
            wd, kcn, src, skey = ((wso, 16, ynT, "ynT"), (wco, 8, cactT, "cactT"), (wmo, 8, oT, "oT"))[br]
            pw = 4096 // kcn
            for pn in range(1024 // pw):
                pump(PUMP)
                vbf, wkey = load_w(wd, kcn, pn * pw, pw)
                for cc in range(pw // 128):
                    dch = pn * (pw // 128) + cc
                    ps, pskey = PB[2 + cc % 2], "pb%d" % (2 + cc % 2)
                    for kc in range(kcn):
                        OP("pe", lambda e, kc=kc, cc=cc, vbf=vbf, ps=ps, src=src, kcn=kcn: e.matmul(ps[:, 0:N], lhsT=vbf[:, kc, cc * 128:(cc + 1) * 128],
                                                                                          rhs=src[:, kc, 0:N], start=(kc == 0), stop=(kc == kcn - 1)),
                           [wkey, skey], [pskey])
                    if br == 0:
                        OP("dve", lambda e, dch=dch, ps=ps: e.tensor_tensor(out=mrg[:, dch, 0:N], in0=ps[:, 0:N], in1=gbr[:, dch, 0:N], op=ALU.mult),
                           [pskey, "gbr#%d" % dch], ["mrg#%d" % dch])
                    else:
                        OP("dve", lambda e, dch=dch, ps=ps: e.tensor_tensor(out=gbr[:, dch, 0:N], in0=ps[:, 0:N], in1=gbr[:, dch, 0:N], op=ALU.mult),
                           [pskey, "gbr#%d" % dch], ["gbr#%d" % dch])
                        OP("pool", lambda e, dch=dch: e.tensor_tensor(out=mrg[:, dch, 0:N], in0=mrg[:, dch, 0:N], in1=gbr[:, dch, 0:N], op=ALU.add),
                           ["mrg#%d" % dch, "gbr#%d" % dch], ["mrg#%d" % dch])
        OP("act", lambda e: e.copy(out=mrgb[:, :, 0:N], in_=mrg[:, :, 0:N]), ["mrg"], ["mrgb"])
        for pn in range(2):
            pump(PUMP)
            vbf, wkey = load_w(wout, 8, pn * 512, 512)
            ps, pskey = PB[pn], "pb%d" % pn
            for kc in range(8):
                OP("pe", lambda e, kc=kc, vbf=vbf, ps=ps: e.matmul(ps[0:N, :], lhsT=mrgb[:, kc, 0:N], rhs=vbf[:, kc, :], start=(kc == 0), stop=(kc == 7)),
                   [wkey, "mrgb"], [pskey])
            OP("dve", lambda e, pn=pn, ps=ps: e.scalar_tensor_tensor(out=h1[0:N, pn * 512:(pn + 1) * 512], in0=xres[0:N, pn * 512:(pn + 1) * 512],
                                                                     scalar=ALPHA, in1=ps[0:N, :], op0=ALU.mult, op1=ALU.add), [pskey, "xres"], ["h1"])
        ln_tok(h1, x1, 0, N)
        for c in range(8):
            pt, ptkey = PB[4 + c % 2], "pb%d" % (4 + c % 2)
            OP("pe", lambda e, c=c, pt=pt: e.transpose(out=pt[:, 0:N], in_=x1[0:N, c * 128:(c + 1) * 128], identity=ident[0:N, 0:N]), ["x1", "ident"], [ptkey])
            OP("act", lambda e, c=c, pt=pt: e.copy(out=x1T[:, c, 0:N], in_=pt[:, 0:N]), [ptkey], ["xdte#%d" % (8 + c)])

        if STAGE == "ln1":
            return
        for pn in range(4):
            vbf, wkey = load_w(wq, 8, pn * 512, 512)
            for cc in range(4):
                pump(2)
                ps, pskey = PB[cc % 2], "pb%d" % (cc % 2)
                for kc in range(8):
                    OP("pe", lambda e, kc=kc, cc=cc, vbf=vbf, ps=ps: e.matmul(ps[:, 0:N], lhsT=vbf[:, kc, cc * 128:(cc + 1) * 128], rhs=x1T[:, kc, 0:N],
                                                                             start=(kc == 0), stop=(kc == 7)), [wkey, "x1T"], [pskey])
                OP("act", lambda e, c=pn * 4 + cc, ps=ps: e.copy(out=qpT[:, c, 0:N], in_=ps[:, 0:N]), [pskey], ["qpT#%d" % (pn * 4 + cc)])
        for c in range(16):
            OP("pe", lambda e, c=c: e.matmul(PB[2 + c // 4][0:N, (c % 4) * 128:(c % 4 + 1) * 128], lhsT=qpT[:, c, 0:N], rhs=skTb[:, c, :],
                                             start=True, stop=True), ["qpT#%d" % c, "skTb"], ["pb%d" % (2 + c // 4)])
        for q in range(4):
            OP("act", lambda e, q=q: e.copy(out=R[0:N, q * 512:(q + 1) * 512], in_=PB[2 + q][0:N, :]), ["pb%d" % (2 + q)], ["R"])
        sc_, scw_ = R[0:N, 0:2048], R[0:N, 2048:4096]
        CS = [slice(c * 128, (c + 1) * 128) for c in range(16)]
        for c in range(16):
            OP("dve", lambda e, c=c: e.max(out=top[0:N, c, 0:8], in_=sc_[:, CS[c]]), ["R"], ["top#%d" % c])
        for c in range(16):
            OP("dve", lambda e, c=c: e.match_replace(out=scw_[:, CS[c]], in_to_replace=top[0:N, c, 0:8], in_values=sc_[:, CS[c]], imm_value=-1e30),
               ["R", "top#%d" % c], ["R2#%d" % c])
        for c in range(16):
            OP("dve", lambda e, c=c: e.max(out=top[0:N, c, 8:16], in_=scw_[:, CS[c]]), ["R2#%d" % c], ["top#%d" % c])
        for c in range(16):
            OP("dve", lambda e, c=c: e.max_index(out=idxu[0:N, c, 0:8], in_max=top[0:N, c, 0:8], in_values=sc_[:, CS[c]]), ["R", "top#%d" % c], ["idxu#%d" % c])
        for c in range(16):
            OP("dve", lambda e, c=c: e.max_index(out=idxu[0:N, c, 8:16], in_max=top[0:N, c, 8:16], in_values=sc_[:, CS[c]]), ["R", "top#%d" % c], ["idxu#%d" % c])
        OP("dve", lambda e: e.tensor_copy(out=idxf[0:N, :, :], in_=idxu[0:N, :, :]), ["idxu"], ["idxf"])
        topv = top[0:N, :, :].rearrange("p (h two) k -> p h two k", two=2)
        idxv = idxf[0:N, :, :].rearrange("p (h two) k -> p h two k", two=2)
        cand = R[0:N, 0:2048].rearrange("p (h a b) -> p h a b", h=8, a=16)
        candw = R[0:N, 2048:4096]
        OP("dve", lambda e: e.tensor_tensor(out=cand, in0=bc(topv[:, :, 0, :].unsqueeze(3), [N, 8, 16, 16]), in1=bc(topv[:, :, 1, :].unsqueeze(2), [N, 8, 16, 16]),
                                            op=ALU.add), ["top"], ["R"])
        HS = [slice(h * 256, (h + 1) * 256) for h in range(8)]
        for h in range(8):
            OP("dve", lambda e, h=h: e.max(out=best[0:N, h, 0:8], in_=sc_[:, HS[h]]), ["R"], ["best#%d" % h])
        for h in range(8):
            OP("dve", lambda e, h=h: e.match_replace(out=candw[:, HS[h]], in_to_replace=best[0:N, h, 0:8], in_values=sc_[:, HS[h]], imm_value=-1e30),
               ["R", "best#%d" % h], ["R2#%d" % (2 * h), "R2#%d" % (2 * h + 1)])
        for h in range(8):
            OP("dve", lambda e, h=h: e.max(out=best[0:N, h, 8:16], in_=candw[:, HS[h]]), ["R2#%d" % (2 * h), "R2#%d" % (2 * h + 1)], ["best#%d" % h])
        for h in range(8):
            OP("dve", lambda e, h=h: e.max_index(out=pos[0:N, h, 0:8], in_max=best[0:N, h, 0:8], in_values=sc_[:, HS[h]]), ["R", "best#%d" % h], ["pos#%d" % h])
        for h in range(8):
            OP("dve", lambda e, h=h: e.max_index(out=pos[0:N, h, 8:16], in_max=best[0:N, h, 8:16], in_values=sc_[:, HS[h]]), ["R", "best#%d" % h], ["pos#%d" % h])
        posf = pos[0:N, :, :].rearrange("p h k -> p (h k)")
        OP("dve", lambda e: e.tensor_single_scalar(out=pab[0:N, 0, :], in_=posf, scalar=4, op=ALU.logical_shift_right), ["pos"], ["pab"])
        OP("dve", lambda e: e.tensor_single_scalar(out=pab[0:N, 1, :], in_=posf, scalar=15, op=ALU.bitwise_and), ["pos"], ["pab"])
        OP("dve", lambda e: e.tensor_copy(out=pabf[0:N, :, :], in_=pab[0:N, :, :]), ["pab"], ["pabf"])
        for two in range(2):
            m4 = selw[0:N, :].rearrange("p (h k a) -> p h k a", h=8, k=16)
            OP("dve", lambda e, two=two, m4=m4: e.tensor_tensor(out=m4, in0=bc(pabf[0:N, two, :].rearrange("p (h k) -> p h k", k=16).unsqueeze(3), [N, 8, 16, 16]),
                                                               in1=bc(iota16[0:N, :].unsqueeze(1).unsqueeze(1), [N, 8, 16, 16]), op=ALU.is_equal),
               ["pabf", "iota16"], ["selw"])
            OP("dve", lambda e, two=two, m4=m4: e.tensor_tensor(out=m4, in0=m4, in1=bc(idxv[:, :, two, :].unsqueeze(2), [N, 8, 16, 16]), op=ALU.mult),
               ["selw", "idxf"], ["selw"])
            OP("dve", lambda e, two=two: e.tensor_reduce(out=ids[0:N, two, :], in_=selw[0:N, :].rearrange("p (hk a) -> p hk a", a=16), axis=AX.X, op=ALU.add),
               ["selw"], ["ids"])
        OP("dve", lambda e: e.scalar_tensor_tensor(out=ids[0:N, 0, :], in0=ids[0:N, 0, :], scalar=128.0, in1=ids[0:N, 1, :], op0=ALU.mult, op1=ALU.add),
           ["ids"], ["ids"])
        gwt = pabf[0:N, 0, :]
        g3 = gwt.rearrange("p (h k) -> p h k", k=16)
        OP("dve", lambda e: e.tensor_tensor(out=g3, in0=best[0:N, :, :], in1=bc(best[0:N, :, 0:1], [N, 8, 16]), op=ALU.subtract), ["best", "pabf"], ["pabf"])
        OP("act", lambda e: e.activation(out=gwt, in_=gwt, func=AF.Exp), ["pabf"], ["pabf"])
        OP("dve", lambda e: e.tensor_reduce(out=col[0:N, 8:16], in_=g3, axis=AX.X, op=ALU.add), ["pabf"], ["col"])
        OP("dve", lambda e: e.reciprocal(out=col[0:N, 8:16], in_=col[0:N, 8:16]), ["col"], ["col"])
        OP("dve", lambda e: e.tensor_tensor(out=g3, in0=g3, in1=bc(col[0:N, 8:16].unsqueeze(2), [N, 8, 16]), op=ALU.mult), ["pabf", "col"], ["pabf"])
        pump(10 ** 6)
        OP("act", lambda e: e.copy(out=x1p[0:N, :], in_=x1[0:N, :]), ["x1"], ["x1p"])
        OP("dve", lambda e: e.tensor_copy(out=idi[0:N, :], in_=ids[0:N, 0, :]), ["ids"], ["idi"])
        OP("act", lambda e: e.copy(out=gw[0:N, :], in_=gwt), ["pabf"], ["gw"])
        if STAGE == "topk":
            return
        return

    def gen_peer(blk):
        samp = blk == 16
        N = NS if samp else 128
        t0 = 0 if samp else blk * 128
        GRP = 16
        ring = [0]
        for g0 in range(0, 128, GRP):
            gi_ = g0 // GRP
            dk, ck = ["dots#%d" % ((gi_ % 2) * 16 + i_) for i_ in range(16)], "coef%d" % (gi_ % 2)
            for s_ in range(g0, g0 + GRP):
                ri = ring[0] % NG
                ring[0] += 1
                Gs, gkey = G[ri], "G%d" % ri
                S.dma(lambda e, s_=s_, Gs=Gs: e.indirect_dma_start(out=Gs[0:N, :], out_offset=None, in_=pu16[:, :],
                                                                   in_offset=bass.IndirectOffsetOnAxis(ap=idi[0:N, s_:s_ + 1], axis=0)),
                      reads=["idi"] + TABKEYS, writes=[gkey], e="pool")
                OP("dve", lambda e, s_=s_, Gs=Gs: e.scalar_tensor_tensor(out=Gs[0:N, :], in0=Gs[0:N, :], scalar=1.0, in1=x1p[0:N, :], op0=ALU.mult,
                                                                         op1=ALU.mult, accum_out=dots[0:N, s_:s_ + 1]), [gkey, "x1p"], [gkey, dk[s_ % 16]])
                yield
            OP("act", lambda e, g0=g0: e.activation(out=coef[0:N, g0:g0 + GRP], in_=dots[0:N, g0:g0 + GRP], func=AF.Gelu), dk, [ck])
            OP("dve", lambda e, g0=g0: e.tensor_tensor(out=coef[0:N, g0:g0 + GRP], in0=coef[0:N, g0:g0 + GRP], in1=gw[0:N, g0:g0 + GRP], op=ALU.mult),
               [ck, "gw"], [ck])
            for s_ in range(g0, g0 + GRP):
                ri = ring[0] % NG
                ring[0] += 1
                Gs, gkey = G[ri], "G%d" % ri
                dv, dvkey = dgv[s_ % 2], "dgv%d" % (s_ % 2)
                S.dma(lambda e, s_=s_, Gs=Gs: e.indirect_dma_start(out=Gs[0:N, :], out_offset=None, in_=pv16[:, :],
                                                                   in_offset=bass.IndirectOffsetOnAxis(ap=idi[0:N, s_:s_ + 1], axis=0)),
                      reads=["idi"] + TABKEYS, writes=[gkey], e="pool")
                OP("act", lambda e, s_=s_, dv=dv: e.activation(out=dv[0:N, 0:N], in_=ident[0:N, 0:N], func=AF.Copy, scale=coef[0:N, s_:s_ + 1]),
                   ["ident", ck], [dvkey])
                for hf in range(2):
                    OP("pe", lambda e, s_=s_, hf=hf, dv=dv, Gs=Gs: e.matmul(PB[6 + hf][0:N, :], lhsT=dv[0:N, 0:N], rhs=Gs[0:N, hf * 512:(hf + 1) * 512],
                                                                           start=(s_ == 0), stop=(s_ == 127)), [dvkey, gkey], ["pb%d" % (6 + hf)])
                yield
        for hf in range(2):
            OP("dve", lambda e, hf=hf: e.scalar_tensor_tensor(out=x1p[0:N, hf * 512:(hf + 1) * 512], in0=x1p[0:N, hf * 512:(hf + 1) * 512], scalar=ALPHA,
                                                              in1=PB[6 + hf][0:N, :], op0=ALU.mult, op1=ALU.add), ["pb%d" % (6 + hf), "x1p"], ["x1p"])
        ln_tok(x1p, x1p, 2, N, scratch=G[0], skey="G0", key="x1p")
        S.dma(lambda e: e.dma_start(out=(ys if samp else yp[t0:t0 + 128, :]), in_=x1p[0:N, :]), reads=["x1p"], writes=["yout"])
        yield

    pend = [None]

    def pump(k):
        for _ in range(k):
            if pend[0] is None:
                return
            try:
                next(pend[0])
            except StopIteration:
                pend[0] = None
                return

    def emit_tails():
        for (tl, tkey, nch, ncol, o_p, o_s, st_in, W) in ((tail_s, "tail_s", 24, 3, o_sconv_p, o_sconv_s, st_sconv, 3), (tail_c, "tail_c", 8, 30, o_cconv_p, o_cconv_s, st_cconv, 30)):
            pass

    blist = list(range(17)) if stop == "all" else [int(x) for x in str(stop).split("+") if x != "0x"]
    if str(stop).isdigit():
        blist = list(range(int(stop)))
    for blk in blist:
        if blk == 15:
            pass
        emit_block(blk)
        if STAGE == '':
            pend[0] = gen_peer(blk)
        if blk == 15:
            for (tl, tkey, nch, ncol, o_p) in ((tail_s, "tail_s", 24, 3, o_sconv_p), (tail_c, "tail_c", 8, 30, o_cconv_p)):
                for j in range(nch):
                    OP("pe", lambda e, tl=tl, j=j, ncol=ncol: e.transpose(out=PB[j % 2][0:ncol, 0:128], in_=tl[:, j, 0:ncol], identity=ident[:]),
                       [tkey, "ident"], ["pb%d" % (j % 2)])
                    OP("act", lambda e, j=j, ncol=ncol: e.copy(out=fm32[j % 2][0:ncol, 0:128], in_=PB[j % 2][0:ncol, 0:128]), ["pb%d" % (j % 2)], ["fm32_%d" % (j % 2)])
                    S.dma(lambda e, j=j, ncol=ncol, o_p=o_p: e.dma_start(out=o_p[:, j * 128:(j + 1) * 128], in_=fm32[j % 2][0:ncol, 0:128]),
                          reads=["fm32_%d" % (j % 2)], writes=["otail"])
        if blk == 16:
            for (tl, tkey, nch, W, o_s, st_in) in ((tail_s, "tail_s", 24, 3, o_sconv_s, st_sconv), (tail_c, "tail_c", 8, 30, o_cconv_s, st_cconv)):
                S.dma(lambda e, o_s=o_s, st_in=st_in, W=W: e.dma_start(out=o_s[:, 0:W - 1, :], in_=st_in[:, 1:W, :]), writes=["otail_s"])
                for j in range(nch):
                    OP("pe", lambda e, tl=tl, j=j: e.transpose(out=PB[j % 2][0:NS, 0:128], in_=tl[:, j, 0:NS], identity=ident[:]),
                       [tkey, "ident"], ["pb%d" % (j % 2)])
                    OP("act", lambda e, j=j: e.copy(out=fm32[j % 2][0:NS, 0:128], in_=PB[j % 2][0:NS, 0:128]), ["pb%d" % (j % 2)], ["fm32_%d" % (j % 2)])
                    S.dma(lambda e, j=j, o_s=o_s, W=W: e.dma_start(out=o_s[:, W - 1, j * 128:(j + 1) * 128], in_=fm32[j % 2][0:NS, 0:128]),
                          reads=["fm32_%d" % (j % 2)], writes=["otail_s2"])
    pump(10 ** 6)
    for nm in dbg:
        t_, key = {"y32": (y32, "y32"), "x1": (x1, "x1"), "xtok": (xtok, "xtok"), "h1": (h1, "h1"), "mrg": (mrg, "mrg"), "hT": (hT, "hT"),
                   "cvT": (cvT, "cvT"), "oT": (oT, "oT"), "ynT": (ynT, "ynT"), "dots": (dots, "dots"), "ids": (ids, "ids"), "gw": (gw, "gw"),
                   "coef": (coef, "coef"), "dtt": (dtt, "dtt"), "cactT": (cactT, "cactT"), "qsc": (qsc, "qsc"), "hp2": (hp2, "hp2"), "hp": (hp, "hp"), "sQ": (sQ, "sQ"), "sB": (sB, "sB"), "sC": (sC, "sC")}[nm]
        shp = list(t_[:].shape)
        dd = dscr("dbg_" + nm, shp, F32)
        if t_[:].dtype != F32:
            n_ = int(np.prod(shp[1:]))
            t32 = R[:, 0:n_].rearrange("p (a b) -> p a b", b=shp[-1]) if len(shp) == 3 else R[:, 0:n_]
            OP("dve", lambda e, t_=t_, t32=t32: e.tensor_copy(out=t32, in_=t_[:]), [key], ["R"])
            S.dma(lambda e, dd=dd, t32=t32: e.dma_start(out=dd, in_=t32), reads=["R"])
        else:
            S.dma(lambda e, dd=dd, t_=t_: e.dma_start(out=dd, in_=t_[:]), reads=[key])
    S.emit()
    st.close()
    return nc


def _fm(v, nch):
    v = np.asarray(v)
    return np.ascontiguousarray(np.moveaxis(v.reshape((nch, 128) + v.shape[1:]), 0, 1))


def prep_inputs(inp, c):
    f = lambda a: np.ascontiguousarray(np.asarray(a, dtype=np.float32))
    L = 0
    sl = slice(NS * c, NS * (c + 1))
    m = {}
    m["xp"] = f(inp["x_prompt"][c]); m["xpT"] = f(inp["x_prompt"][c].T)
    m["xs"] = f(inp["x_sample"][sl, 0]); m["xsT"] = f(inp["x_sample"][sl, 0].T)
    m["w_in"] = f(inp["w_in"][L])
    m["scw"] = _fm(f(inp["ssd_conv_w"][L].T), 24); m["scb"] = _fm(f(inp["ssd_conv_b"][L]), 24)
    m["dtb"] = f(inp["ssd_dt_bias"][L][None]); m["alog"] = f(inp["ssd_a_log"][L][None]); m["dsk"] = f(inp["ssd_d"][L][None])
    m["nw"] = f(inp["ssd_norm_w"][L][None]); m["wso"] = f(inp["ssd_w_out"][L])
    m["ccw"] = _fm(f(inp["conf_conv_w"][L].T), 8); m["ccb"] = _fm(f(inp["conf_conv_b"][L]), 8)
    m["clg"] = _fm(f(inp["conf_ln_g"][L]), 8); m["clb"] = _fm(f(inp["conf_ln_b"][L]), 8)
    m["wco"] = f(inp["conf_w_out"][L]); m["wk"] = f(inp["mem_w_k"][L]); m["wv"] = f(inp["mem_w_v"][L])
    m["wmo"] = f(inp["mem_w_o"][L]); m["wout"] = f(inp["w_out"][L])
    for k in ("ln1_g", "ln1_b", "ln2_g", "ln2_b"):
        m[k.replace("_", "")] = f(inp[k][L][None])
    m["wq"] = f(inp["peer_w_q"][L])
    sk = np.asarray(inp["peer_sub_keys"][L]).reshape(16, 128, 128)
    m["skT"] = f(np.transpose(sk, (2, 0, 1)))
    m["pu"] = f(inp["peer_u"][L]); m["pv"] = f(inp["peer_v"][L])
    m["mempT"] = f(inp["mem_prompt"][c].T)
    m["st_ssd"] = f(np.asarray(inp["state_ssd"][L][sl]).reshape(NS, 128, 2048))
    sc = np.asarray(inp["state_ssd_conv"][L][sl])
    m["st_sconv"] = f(sc)
    m["st_sconvT"] = f(np.transpose(sc.reshape(NS, 3, 24, 128), (3, 2, 0, 1)))
    cc = np.asarray(inp["state_conf_conv"][L][sl])
    m["st_cconv"] = f(cc)
    m["st_cconvT"] = f(np.transpose(cc.reshape(NS, 30, 8, 128), (3, 2, 0, 1)))
    m["ck"] = f(np.asarray(inp["cache_mem_k"][L][sl]).reshape(NS, 256, D))
    m["cv"] = f(np.asarray(inp["cache_mem_v"][L][sl]).reshape(NS, 256, D))
    return m


_NC_CACHE = {}


def kernel(**inputs):
    if "nc" not in _NC_CACHE:
        _NC_CACHE["nc"] = build(stop="all")
    nc = _NC_CACHE["nc"]
    in_maps = [prep_inputs(inputs, c) for c in range(8)]
    res = run_bass_kernel_spmd(nc, in_maps, core_ids=list(range(8)))
    r = res.results
    st = lambda k: np.stack([np.asarray(r[c][k], dtype=np.float32) for c in range(8)])
    y_p = st("yp")
    y_s = st("ys").reshape(128, 1, D)
    ssd_p = st("o_ssd_p").reshape(1, 8, 32, 64, 128)
    sconv_p = st("o_sconv_p").reshape(1, 8, 3, 3072)
    cconv_p = st("o_cconv_p").reshape(1, 8, 30, D)
    k_p = st("o_k_p").reshape(1, 8, 256, 4, 256)
    v_p = st("o_v_p").reshape(1, 8, 256, 4, 256)
    ssd_s = st("o_ssd_s").reshape(1, 128, 32, 64, 128)
    sconv_s = st("o_sconv_s").reshape(1, 128, 3, 3072)
    cconv_s = st("o_cconv_s").reshape(1, 128, 30, D)
    return (y_p, y_s, ssd_p, sconv_p, cconv_p, k_p, v_p, ssd_s, sconv_s, cconv_s)
```
